# Optimizing a Trainium2 kernel written in Bass

```python
import math
import jax, jax.numpy as jnp
from jax import lax
import numpy as np

D_MODEL = 1024
BATCH = 8
SEQ = 2048
DEPTH = 4
DEC_BATCH = 128
DEC_SEQ = 4
PAST_LEN = 16384
PAGE_SIZE = 128

N_MIXERS = 3
N_S5_LAYERS = (DEPTH + 2) // 3
N_RET_LAYERS = (DEPTH + 1) // 3
N_GLA_LAYERS = DEPTH // 3
S5_GROUP = 16
S5_GROUPS = D_MODEL // S5_GROUP
S5_STATE = 64
S5_DT_MIN = 1e-3
S5_DT_MAX = 1e-1
RET_HEADS = 4
RET_DK = D_MODEL // RET_HEADS
RET_DV = 2 * D_MODEL // RET_HEADS
RET_CHUNK = 128
ROPE_BASE = 10000.0
GLA_HEADS = 4
GLA_DK = D_MODEL // 2 // GLA_HEADS
GLA_DV = D_MODEL // GLA_HEADS
GLA_GATE_RANK = 16
GLA_GATE_NORM = 16.0
GLA_CHUNK = 64
D_FF = ((8 * D_MODEL // 3 + 255) // 256) * 256
CONV_W = 3
NORM_EPS = 1e-6

kernel_name = 'hybrid_s5_retention_gla_convffn_step'


def rmsnorm(x, g):
    xf = x.astype(jnp.float32)
    y = xf * lax.rsqrt(jnp.mean(xf * xf, axis=-1, keepdims=True) + NORM_EPS)
    return (y * g.astype(jnp.float32)).astype(x.dtype)


def _linear_combine(e1, e2):
    a1, b1 = e1
    a2, b2 = e2
    return a1 * a2, a2 * b1 + b2


def s5_mixer(u, h0_re, h0_im, lam_re, lam_im, b_re, b_im, c_re, c_im, log_dt, d_skip, w_glu):
    bsz, l, d = u.shape
    f32 = jnp.float32
    uf = u.astype(f32)
    lam = lax.complex(lam_re.astype(f32), lam_im.astype(f32))
    dt = jnp.exp(log_dt.astype(f32))[:, None]
    a_bar = jnp.exp(lam * dt)
    b_bar = ((a_bar - 1.0) / lam)[..., None] * lax.complex(b_re.astype(f32), b_im.astype(f32))
    c_mat = lax.complex(c_re.astype(f32), c_im.astype(f32))
    ug = uf.reshape(bsz, l, S5_GROUPS, S5_GROUP).astype(jnp.complex64)
    bu = jnp.einsum('blgp,gnp->blgn', ug, b_bar)
    h0 = lax.complex(h0_re.astype(f32), h0_im.astype(f32))
    bu = bu.at[:, 0].add(a_bar * h0)
    a = jnp.broadcast_to(a_bar, (1, l) + a_bar.shape)
    _, hs = lax.associative_scan(_linear_combine, (a, bu), axis=1)
    y = jnp.einsum('gpn,blgn->blgp', c_mat, hs).real.reshape(bsz, l, d) + d_skip.astype(f32) * uf
    z = jax.nn.gelu(y) @ w_glu.astype(f32)
    out = z[..., :d] * jax.nn.sigmoid(z[..., d:])
    h_last = hs[:, -1]
    return out.astype(u.dtype), jnp.real(h_last), jnp.imag(h_last)


def rotate_half(t, pos):
    half = t.shape[-1] // 2
    freq = 1.0 / (ROPE_BASE ** jnp.linspace(0.0, 1.0, half, dtype=jnp.float32))
    ang = pos[:, None] * freq[None, :]
    cos = jnp.cos(ang)[None, :, None, :]
    sin = jnp.sin(ang)[None, :, None, :]
    t1, t2 = t[..., :half], t[..., half:]
    return jnp.concatenate([t1 * cos - t2 * sin, t2 * cos + t1 * sin], axis=-1)


def _to_chunks(t, n, c):
    b, l, h, dd = t.shape
    return t.reshape(b, n, c, h, dd).transpose(1, 0, 3, 2, 4)


def _from_chunks(o):
    n, b, h, c, dd = o.shape
    return o.transpose(1, 0, 3, 2, 4).reshape(b, n * c, h, dd)


def retention_chunked(q, k, v, log_gamma, s0):
    b, l, h, dk = q.shape
    c = math.gcd(l, RET_CHUNK)
    n = l // c
    idx = jnp.arange(c, dtype=jnp.float32)
    rel = idx[:, None] - idx[None, :]
    decay = jnp.where(rel >= 0, jnp.exp(jnp.maximum(rel, 0.0)[None] * log_gamma[:, None, None]), 0.0)
    q_decay = jnp.exp((idx + 1.0)[None, :] * log_gamma[:, None])[..., None]
    k_decay = jnp.exp((c - 1.0 - idx)[None, :] * log_gamma[:, None])[..., None]
    chunk_decay = jnp.exp(c * log_gamma)[:, None, None]

    def step(s, inp):
        qc, kc, vc = inp
        scores = jnp.einsum('bhid,bhjd->bhij', qc, kc) * decay
        o = jnp.einsum('bhij,bhjv->bhiv', scores, vc) + jnp.einsum('bhid,bhdv->bhiv', qc, s) * q_decay
        s = s * chunk_decay + jnp.einsum('bhjd,bhjv->bhdv', kc * k_decay, vc)
        return s, o

    s, o = lax.scan(step, s0, (_to_chunks(q, n, c), _to_chunks(k, n, c), _to_chunks(v, n, c)))
    return _from_chunks(o), s


def gla_chunked(q, k, v, log_a, s0):
    b, l, h, dk = q.shape
    c = math.gcd(l, GLA_CHUNK)
    n = l // c
    mask = jnp.tril(jnp.ones((c, c), dtype=bool))[:, :, None]

    def step(s, inp):
        qc, kc, vc, gc = inp
        bc = jnp.cumsum(gc, axis=2)
        diff = bc[:, :, :, None, :] - bc[:, :, None, :, :]
        gate = jnp.exp(jnp.where(mask, diff, -jnp.inf))
        scores = jnp.einsum('bhid,bhjd,bhijd->bhij', qc, kc, gate)
        o = jnp.einsum('bhij,bhjv->bhiv', scores, vc) + jnp.einsum('bhid,bhdv->bhiv', qc * jnp.exp(bc), s)
        b_last = bc[:, :, -1:, :]
        s = s * jnp.exp(b_last[:, :, 0, :])[..., None] + jnp.einsum('bhjd,bhjv->bhdv', kc * jnp.exp(b_last - bc), vc)
        return s, o

    s, o = lax.scan(step, s0, (_to_chunks(q, n, c), _to_chunks(k, n, c), _to_chunks(v, n, c), _to_chunks(log_a, n, c)))
    return _from_chunks(o), s


def retention_mixer(x, pos0, s0, wq, wk, wv, wg, wo):
    b, l, _ = x.shape
    f32 = jnp.float32
    q = (x @ wq).reshape(b, l, RET_HEADS, RET_DK).astype(f32)
    k = (x @ wk).reshape(b, l, RET_HEADS, RET_DK).astype(f32)
    v = (x @ wv).reshape(b, l, RET_HEADS, RET_DV).astype(f32)
    pos = jnp.arange(pos0, pos0 + l, dtype=jnp.int32).astype(f32)
    q = rotate_half(q, pos)
    k = rotate_half(k, pos) * (RET_DK ** -0.5)
    log_gamma = jnp.log1p(-(2.0 ** (-5.0 - jnp.arange(RET_HEADS, dtype=f32))))
    o, s = retention_chunked(q, k, v, log_gamma, s0.astype(f32))
    mu = jnp.mean(o, axis=-1, keepdims=True)
    var = jnp.mean(jnp.square(o - mu), axis=-1, keepdims=True)
    o = ((o - mu) * lax.rsqrt(var + NORM_EPS)).reshape(b, l, RET_HEADS * RET_DV)
    o = o * jax.nn.silu((x @ wg).astype(f32))
    return (o @ wo.astype(f32)).astype(x.dtype), s


def gla_mixer(x, s0, wq, wk, wv, wg, wa1, wa2, ba, norm_g, wo):
    b, l, _ = x.shape
    f32 = jnp.float32
    q = (x @ wq).reshape(b, l, GLA_HEADS, GLA_DK).astype(f32) * (GLA_DK ** -0.5)
    k = (x @ wk).reshape(b, l, GLA_HEADS, GLA_DK).astype(f32)
    v = (x @ wv).reshape(b, l, GLA_HEADS, GLA_DV).astype(f32)
    gate_logit = ((x @ wa1) @ wa2 + ba).astype(f32)
    log_a = (jax.nn.log_sigmoid(gate_logit) / GLA_GATE_NORM).reshape(b, l, GLA_HEADS, GLA_DK)
    o, s = gla_chunked(q, k, v, log_a, s0.astype(f32))
    o = o * lax.rsqrt(jnp.mean(o * o, axis=-1, keepdims=True) + NORM_EPS) * norm_g.astype(f32)
    o = o.reshape(b, l, GLA_HEADS * GLA_DV) * jax.nn.silu((x @ wg).astype(f32))
    return (o @ wo.astype(f32)).astype(x.dtype), s


def conv_ffn(x, buf, w_gate, w_up, conv_w, conv_b, w_down):
    l = x.shape[1]
    gpre = x @ w_gate
    cat = jnp.concatenate([buf.astype(gpre.dtype), gpre], axis=1)
    gconv = conv_b
    for j in range(CONV_W):
        gconv = gconv + conv_w[j] * cat[:, j:j + l]
    hdn = jax.nn.silu(gconv) * (x @ w_up)
    return hdn @ w_down, cat[:, l:]


def trunk(x, pos0, s5_re, s5_im, ret_s, gla_s, conv_buf, p):
    new_re, new_im, new_ret, new_gla, new_conv = [], [], [], [], []
    for layer in range(DEPTH):
        g = p['norm_g'][layer]
        kind = layer % N_MIXERS
        j = layer // N_MIXERS
        h = rmsnorm(x, g[0])
        if kind == 0:
            mix, hr, hi = s5_mixer(h, s5_re[j], s5_im[j], p['s5_lambda_re'][j], p['s5_lambda_im'][j],
                                   p['s5_b_re'][j], p['s5_b_im'][j], p['s5_c_re'][j], p['s5_c_im'][j],
                                   p['s5_log_dt'][j], p['s5_d'][j], p['s5_w_glu'][j])
            new_re.append(hr)
            new_im.append(hi)
        elif kind == 1:
            mix, s = retention_mixer(h, pos0, ret_s[j], p['ret_wq'][j], p['ret_wk'][j], p['ret_wv'][j],
                                     p['ret_wg'][j], p['ret_wo'][j])
            new_ret.append(s)
        else:
            mix, s = gla_mixer(h, gla_s[j], p['gla_wq'][j], p['gla_wk'][j], p['gla_wv'][j], p['gla_wg'][j],
                               p['gla_wa1'][j], p['gla_wa2'][j], p['gla_ba'][j], p['gla_norm_g'][j], p['gla_wo'][j])
            new_gla.append(s)
        x = x + rmsnorm(mix, g[1])
        h = rmsnorm(x, g[2])
        f, buf = conv_ffn(h, conv_buf[layer], p['ffn_w_gate'][layer], p['ffn_w_up'][layer],
                          p['ffn_conv_w'][layer], p['ffn_conv_b'][layer], p['ffn_w_down'][layer])
        new_conv.append(buf)
        x = x + rmsnorm(f, g[3])
    return x, jnp.stack(new_re), jnp.stack(new_im), jnp.stack(new_ret), jnp.stack(new_gla), jnp.stack(new_conv)


def setup_inputs(seed: int = 0) -> dict:
    key = jax.random.key(seed)
    it = iter(list(jax.random.split(key, 40)))

    def nrm(shape, scale):
        return scale * jax.random.normal(next(it), shape, jnp.float32)

    d = D_MODEL
    inp = {}
    inp['x_prompt'] = nrm((BATCH, SEQ, d), 1.0)
    inp['x_sample'] = nrm((DEC_BATCH, DEC_SEQ, d), 1.0)
    inp['state_s5_re'] = nrm((N_S5_LAYERS, DEC_BATCH, S5_GROUPS, S5_STATE), 0.1)
    inp['state_s5_im'] = nrm((N_S5_LAYERS, DEC_BATCH, S5_GROUPS, S5_STATE), 0.1)
    inp['state_ret'] = nrm((N_RET_LAYERS, DEC_BATCH, RET_HEADS, RET_DK, RET_DV), 0.5)
    inp['state_gla'] = nrm((N_GLA_LAYERS, DEC_BATCH, GLA_HEADS, GLA_DK, GLA_DV), 0.5)
    inp['cache_ffn_conv'] = nrm((DEPTH, DEC_BATCH, CONV_W - 1, D_FF), 1.0)
    inp['norm_g'] = 1.0 + nrm((DEPTH, 4, d), 0.01)
    inp['s5_lambda_re'] = -0.5 + nrm((N_S5_LAYERS, S5_GROUPS, S5_STATE), 0.01)
    inp['s5_lambda_im'] = jnp.pi * jnp.arange(S5_STATE, dtype=jnp.float32) + nrm((N_S5_LAYERS, S5_GROUPS, S5_STATE), 0.01)
    inp['s5_b_re'] = nrm((N_S5_LAYERS, S5_GROUPS, S5_STATE, S5_GROUP), (2.0 * S5_GROUP) ** -0.5)
    inp['s5_b_im'] = nrm((N_S5_LAYERS, S5_GROUPS, S5_STATE, S5_GROUP), (2.0 * S5_GROUP) ** -0.5)
    inp['s5_c_re'] = nrm((N_S5_LAYERS, S5_GROUPS, S5_GROUP, S5_STATE), (2.0 * S5_STATE) ** -0.5)
    inp['s5_c_im'] = nrm((N_S5_LAYERS, S5_GROUPS, S5_GROUP, S5_STATE), (2.0 * S5_STATE) ** -0.5)
    inp['s5_log_dt'] = jax.random.uniform(next(it), (N_S5_LAYERS, S5_GROUPS), jnp.float32,
                                          minval=math.log(S5_DT_MIN), maxval=math.log(S5_DT_MAX))
    inp['s5_d'] = nrm((N_S5_LAYERS, d), 1.0)
    inp['s5_w_glu'] = nrm((N_S5_LAYERS, d, 2 * d), d ** -0.5)
    inp['ret_wq'] = nrm((N_RET_LAYERS, d, RET_HEADS * RET_DK), d ** -0.5)
    inp['ret_wk'] = nrm((N_RET_LAYERS, d, RET_HEADS * RET_DK), d ** -0.5)
    inp['ret_wv'] = nrm((N_RET_LAYERS, d, RET_HEADS * RET_DV), d ** -0.5)
    inp['ret_wg'] = nrm((N_RET_LAYERS, d, RET_HEADS * RET_DV), d ** -0.5)
    inp['ret_wo'] = nrm((N_RET_LAYERS, RET_HEADS * RET_DV, d), (RET_HEADS * RET_DV) ** -0.5)
    inp['gla_wq'] = nrm((N_GLA_LAYERS, d, GLA_HEADS * GLA_DK), d ** -0.5)
    inp['gla_wk'] = nrm((N_GLA_LAYERS, d, GLA_HEADS * GLA_DK), d ** -0.5)
    inp['gla_wv'] = nrm((N_GLA_LAYERS, d, GLA_HEADS * GLA_DV), d ** -0.5)
    inp['gla_wg'] = nrm((N_GLA_LAYERS, d, GLA_HEADS * GLA_DV), d ** -0.5)
    inp['gla_wa1'] = nrm((N_GLA_LAYERS, d, GLA_GATE_RANK), d ** -0.5)
    inp['gla_wa2'] = nrm((N_GLA_LAYERS, GLA_GATE_RANK, GLA_HEADS * GLA_DK), GLA_GATE_RANK ** -0.5)
    inp['gla_ba'] = nrm((N_GLA_LAYERS, GLA_HEADS * GLA_DK), 0.01)
    inp['gla_norm_g'] = 1.0 + nrm((N_GLA_LAYERS, GLA_DV), 0.01)
    inp['gla_wo'] = nrm((N_GLA_LAYERS, GLA_HEADS * GLA_DV, d), (GLA_HEADS * GLA_DV) ** -0.5)
    inp['ffn_w_gate'] = nrm((DEPTH, d, D_FF), d ** -0.5)
    inp['ffn_w_up'] = nrm((DEPTH, d, D_FF), d ** -0.5)
    inp['ffn_conv_w'] = nrm((DEPTH, CONV_W, D_FF), CONV_W ** -0.5)
    inp['ffn_conv_b'] = nrm((DEPTH, D_FF), 0.01)
    inp['ffn_w_down'] = nrm((DEPTH, D_FF, d), D_FF ** -0.5)
    return inp


def reference(x_prompt, x_sample, state_s5_re, state_s5_im, state_ret, state_gla, cache_ffn_conv,
              norm_g, s5_lambda_re, s5_lambda_im, s5_b_re, s5_b_im, s5_c_re, s5_c_im, s5_log_dt, s5_d, s5_w_glu,
              ret_wq, ret_wk, ret_wv, ret_wg, ret_wo,
              gla_wq, gla_wk, gla_wv, gla_wg, gla_wa1, gla_wa2, gla_ba, gla_norm_g, gla_wo,
              ffn_w_gate, ffn_w_up, ffn_conv_w, ffn_conv_b, ffn_w_down):
    p = dict(norm_g=norm_g, s5_lambda_re=s5_lambda_re, s5_lambda_im=s5_lambda_im, s5_b_re=s5_b_re,
             s5_b_im=s5_b_im, s5_c_re=s5_c_re, s5_c_im=s5_c_im, s5_log_dt=s5_log_dt, s5_d=s5_d,
             s5_w_glu=s5_w_glu, ret_wq=ret_wq, ret_wk=ret_wk, ret_wv=ret_wv, ret_wg=ret_wg, ret_wo=ret_wo,
             gla_wq=gla_wq, gla_wk=gla_wk, gla_wv=gla_wv, gla_wg=gla_wg, gla_wa1=gla_wa1, gla_wa2=gla_wa2,
             gla_ba=gla_ba, gla_norm_g=gla_norm_g, gla_wo=gla_wo, ffn_w_gate=ffn_w_gate, ffn_w_up=ffn_w_up,
             ffn_conv_w=ffn_conv_w, ffn_conv_b=ffn_conv_b, ffn_w_down=ffn_w_down)
    bp = x_prompt.shape[0]
    f32 = jnp.float32
    z_re = jnp.zeros((N_S5_LAYERS, bp, S5_GROUPS, S5_STATE), f32)
    z_ret = jnp.zeros((N_RET_LAYERS, bp, RET_HEADS, RET_DK, RET_DV), f32)
    z_gla = jnp.zeros((N_GLA_LAYERS, bp, GLA_HEADS, GLA_DK, GLA_DV), f32)
    z_conv = jnp.zeros((DEPTH, bp, CONV_W - 1, D_FF), x_prompt.dtype)
    y_prompt, p_re, p_im, p_ret, p_gla, p_conv = trunk(x_prompt, 0, z_re, z_re, z_ret, z_gla, z_conv, p)
    y_sample, s_re, s_im, s_ret, s_gla, s_conv = trunk(x_sample, PAST_LEN, state_s5_re, state_s5_im,
                                                       state_ret, state_gla, cache_ffn_conv, p)
    return (y_prompt, y_sample, p_re, p_im, p_ret, p_gla, p_conv, s_re, s_im, s_ret, s_gla, s_conv)
```

```python
from contextlib import ExitStack
import numpy as np
import concourse.bass as bass
import concourse.mybir as mybir
from concourse.bass_utils import run_bass_kernel_spmd

F32 = mybir.dt.float32
BF16 = mybir.dt.bfloat16
ALU = mybir.AluOpType
AF = mybir.ActivationFunctionType

NCORES = 8
D = 1024
NCH = 8
LP = 2048
NS = 16
LS = 4
TS = NS * LS
T = LP + TS
DFF = 2816
NF = 22
DEPTH = 4
EPS = 1e-6
SBUF_TOP = 229344
GROUPS = [dict(p0=512 * i, pn=512, samp=(i == 3)) for i in range(4)]
GT = 576


class Sched:
    def __init__(self):
        self.ops = []

    redirect = None

    def flush(self, lst):
        assert self.redirect is None
        for a in lst:
            self.add(*a)

    def add(self, eng, fn, reads=(), writes=(), dma=False, n=1, final=False):
        if self.redirect is not None:
            self.redirect.append((eng, fn, reads, writes, dma, n, final))
            return

        def isar(h):
            h0 = h[0] if isinstance(h, tuple) else h
            return isinstance(h0, str) and h0.startswith("A:")
        reads = tuple(reads)
        writes = tuple(writes)
        if any(isar(h) for h in reads + writes):
            reads = reads + ("arena",)
        self.ops.append(dict(eng=eng, fn=fn, r=reads, w=writes, dma=dma, n=n, final=final))

    def emit(self, nc, ctx):
        ops = self.ops
        NPOOL = 8
        csem = {e: ctx.enter_context(nc.semaphore("s_" + e)) for e in ("pe", "act", "dve", "pool")}
        ccnt = {e: 0 for e in csem}
        dsem = {}
        for q in ("sp", "pool", "act"):
            dsem[q] = [ctx.enter_context(nc.semaphore("d_%s%d" % (q, i))) for i in range(NPOOL)]
        dval = {q: [0] * NPOOL for q in dsem}
        dlast = {q: [None] * NPOOL for q in dsem}
        dnext = {q: 0 for q in dsem}
        lw, rd = {}, {}
        done = [None] * len(ops)
        per_eng = {e: [] for e in ("pe", "act", "dve", "pool", "sp")}
        deps_of = [None] * len(ops)
        slot_of = [None] * len(ops)
        waited_idx = {e: {} for e in per_eng}
        targets_of = [None] * len(ops)
        needed = set()
        for i, op in enumerate(ops):
            deps = set()
            for h in op["r"]:
                if h in lw:
                    deps.add(lw[h])
            for h in op["w"]:
                if h in lw:
                    deps.add(lw[h])
                deps.update(rd.get(h, ()))
            for h in op["r"]:
                rd.setdefault(h, []).append(i)
            for h in op["w"]:
                lw[h] = i
                rd[h] = []
            e = op["eng"]
            if op["dma"]:
                kk = dnext[e]
                dnext[e] = (kk + 1) % NPOOL
                if dlast[e][kk] is not None:
                    deps.add(dlast[e][kk])
                dlast[e][kk] = i
                slot_of[i] = kk
            latest = {}
            for d in deps:
                od = ops[d]
                key = ("dma", od["eng"], slot_of[d]) if od["dma"] else ("cmp", od["eng"])
                if d > latest.get(key, -1):
                    latest[key] = d
            tg = []
            for key, d in sorted(latest.items(), key=lambda kv: kv[1]):
                if d > waited_idx[e].get(key, -1):
                    waited_idx[e][key] = d
                    tg.append(d)
                    needed.add(d)
            targets_of[i] = tg
            if op.get("final"):
                needed.add(i)
        for i, op in enumerate(ops):
            e = op["eng"]
            if op["dma"]:
                n = op.get("n", 1)
                dval[e][slot_of[i]] += 16 * n
                done[i] = (dsem[e][slot_of[i]], dval[e][slot_of[i]])
            elif i in needed:
                ccnt[e] += 1
                done[i] = (csem[e], ccnt[e])
        for i, op in enumerate(ops):
            e = op["eng"]
            waits = [done[d] for d in targets_of[i]]
            if op["dma"]:
                per_eng[e].append((waits, op, ("dma", slot_of[i], i)))
            else:
                per_eng[e].append((waits, op, ("cmp", i in needed, i)))
        self.n_signals = dict(ccnt)
        self.stats = {e: len(v) for e, v in per_eng.items()}

        def run(eng_name, eobj):
            for waits, op, (kind, slot, i) in per_eng[eng_name]:
                for s, v in waits:
                    eobj.wait_ge(s, v)
                if kind == "cmp":
                    ins = op["fn"](eobj)
                    if slot:
                        ins.then_inc(csem[eng_name], 1)
                else:
                    lst = op["fn"](eobj)
                    if not isinstance(lst, (list, tuple)):
                        lst = [lst]
                    assert len(lst) == op.get("n", 1)
                    for ins in lst:
                        ins.then_inc(dsem[eng_name][slot], 16)

        final_waits = {}
        for i, op in enumerate(ops):
            if op.get("final"):
                s, v = done[i]
                final_waits[id(s)] = (s, max(v, final_waits.get(id(s), (s, 0))[1]))

        with nc.Block() as block:
            @block.sync
            def _(e):
                run("sp", e)
                for s, v in final_waits.values():
                    e.wait_ge(s, v)

            @block.tensor
            def _(e):
                run("pe", e)

            @block.scalar
            def _(e):
                run("act", e)

            @block.vector
            def _(e):
                run("dve", e)

            @block.gpsimd
            def _(e):
                run("pool", e)


class K:
    pass


def build_program(flags):
    nc = bass.Bass("TRN2", target_bir_lowering=False)
    ctx = ExitStack()
    S = Sched()
    k = K()
    k.nc, k.S, k.ctx, k.flags = nc, S, ctx, flags

    def din(name, shape, dt=F32):
        return nc.dram_tensor(name, list(shape), dt, kind="ExternalInput").ap()

    def dout(name, shape, dt=F32):
        return nc.dram_tensor(name, list(shape), dt, kind="ExternalOutput").ap()

    def sb(name, shape, dt):
        return ctx.enter_context(nc.sbuf_tensor(name, list(shape), dt))

    k.din, k.dout, k.sb = din, dout, sb
    k.arena_cache = {}

    def carve_begin(phase):
        k.phase = phase
        k.arena_ptr = k.arena_base

    def carve(name, shape, dt):
        key = (k.phase, name)
        if key in k.arena_cache:
            return k.arena_cache[key]
        esz = 2 if dt == BF16 else 4
        nbytes = int(np.prod(shape[1:])) * esz
        off = (k.arena_ptr + 31) // 32 * 32
        k.arena_max = max(getattr(k, "arena_max", 0), off + nbytes)
        if off + nbytes > SBUF_TOP:
            print("ARENA OVERFLOW", k.phase, name, off + nbytes - SBUF_TOP)
            k.arena_over = True
        t = nc.alloc_sbuf_tensor_at("%s_%s" % (k.phase, name), list(shape), dt, offset=off)
        k.arena_ptr = off + nbytes
        k.arena_cache[key] = t
        return t

    def barrier():
        S.add("dve", lambda e: e.memset(k.bar[:], 0.0), writes=["arena"])
    k.carve_begin, k.carve, k.barrier = carve_begin, carve, barrier
    k.xT = din("xT", [128, NCH, T])
    k.yT = dout("yT", [128, NCH, T])
    k.ng_d = din("ng", [128, DEPTH * 4 * NCH])
    k.wgu_d = din("wgu", [DEPTH, NF, 128, 2048])
    k.wd_d = din("wd", [DEPTH, NF, 128, 1024])
    k.cw_d = din("cw", [128, DEPTH * NF * 4])
    k.cconv_d = din("A:cconv", [128, DEPTH * NF * NS * 2])
    k.oconv_d = dout("A:oconv", [128, DEPTH * NF * (NS + 1) * 2])

    k.ident_d = din("ident", [128, 128])
    k.cmask_d = din("cmask", [128, 10, 128])
    k.acol_d = din("acol", [128, 32])
    k.cos_d = din("cosT", [128, T])
    k.sin_d = din("sinT", [128, T])
    k.rmask_d = din("rmask", [128, GT])
    k.wret_d = din("wret", [4, 3, 128, 4096])
    k.wo_ret_d = din("wo_ret", [4, 128, 4096])
    k.wgqk_d = din("wgqk", [4, 128, 2048])
    k.wgvg_d = din("wgvg", [4, 128, 4096])
    k.wo_gla_d = din("wo_gla", [4, 128, 2048])
    k.wa1_d = din("wa1", [128, 8, 16])
    k.wa2_d = din("wa2", [16, 512])
    k.ba_d = din("ba", [128, 4])
    k.ngl_d = din("ngl", [128, 256])
    k.sret_d = din("sret", [NS, 4, 128, 2, 512])
    k.sgla_d = din("sgla", [NS, 4, 128, 1, 256])
    k.oret_p_d = dout("oret_p", [4, 128, 2, 512])
    k.oret_s_d = dout("oret_s", [NS, 4, 128, 2, 512])
    k.ogla_p_d = dout("ogla_p", [4, 128, 1, 256])
    k.ogla_s_d = dout("ogla_s", [NS, 4, 128, 1, 256])
    k.s5lamQ_d = din("s5lamQ", [2, 128, 2, 512])
    k.s5bQ_d = din("s5bQ", [2, 128, 2, 512])
    k.s5lamC_d = din("s5lamC", [2, 128, 2, 32])
    k.s5cC_d = din("s5cC", [2, 128, 8, 4, 2, 16])
    k.s5dt_d = din("s5dt", [2, 128, 8])
    k.s5d_d = din("s5d", [2, 128, 8])
    k.s5bd_d = din("s5bd", [128, 8])
    k.s5tau_d = din("s5tau", [128, 2, 32])
    k.s5h0_d = din("s5h0", [2, 128, NS, 2, 32])
    k.wglu_d = din("wglu", [2, 8, 128, 2048])
    k.os5p_d = dout("os5p", [2, 128, 2, 32])
    k.os5s_d = dout("os5s", [2, 128, NS, 2, 32])
    k.ident = sb("idents", [128, 128], BF16)
    k.acol = sb("acols", [128, 32], F32)
    k.x = sb("x", [128, NCH, T], F32)
    k.ng = sb("ngs", [128, DEPTH * 4 * NCH], F32)
    k.cw = sb("cws", [128, DEPTH * NF * 4], F32)
    k.ones = sb("ones", [128, 128], BF16)
    k.h = sb("h", [128, NCH, GT], BF16)
    k.sq = sb("sq", [128, NCH, 512], BF16)
    k.mixbuf = sb("mixbuf", [128, NCH, 512], F32)
    k.rstd = sb("rstd", [128, 512], F32)
    k.bar = sb("bar", [128, 8], F32)
    k.ps = [ctx.enter_context(nc.psum_tensor("ps%d" % i, [128, 512], F32)) for i in range(7)]
    k.pst = ctx.enter_context(nc.psum_tensor("ps7", [128, 1024], BF16))
    k.arena_base = SBUF_TOP - nc.sbuf_bytes_remaining + 64

    for c in range(NCH):
        S.add("sp", lambda e, c=c: e.dma_start(out=k.x[:, c, :], in_=k.xT[:, c, :]), writes=["x"], dma=True)
    S.add("sp", lambda e: e.dma_start(out=k.ng[:], in_=k.ng_d), writes=["ng"], dma=True)
    S.add("sp", lambda e: e.dma_start(out=k.cw[:], in_=k.cw_d), writes=["cw"], dma=True)
    S.add("dve", lambda e: e.memset(k.ones[:], 1.0), writes=["ones"])
    S.add("pool", lambda e: e.dma_start(out=k.ident[:], in_=k.ident_d), writes=["ident"], dma=True)
    S.add("sp", lambda e: e.dma_start(out=k.acol[:], in_=k.acol_d), writes=["acol"], dma=True)

    for layer in range(DEPTH):
        kind = layer % 3
        mix_on = flags.get(("s5", "ret", "gla")[kind], False)
        if mix_on:
            carve_begin("mix%d" % kind)
            mixer_setup(k, layer, kind)
            tl = [col_tiles(g) for g in GROUPS]
            prenorm(k, layer, 0, GROUPS[0], tl[0])
            for gi, g in enumerate(GROUPS):
                nxt = None
                if gi + 1 < len(GROUPS):
                    nxt = (lambda gi=gi, layer=layer, tl=tl: prenorm(k, layer, 0, GROUPS[gi + 1], tl[gi + 1]))
                mixer_group(k, layer, kind, gi, g, tl[gi], nxt)
            mixer_finish(k, layer, kind)
            barrier()
        carve_begin("ffn")
        ffn_setup(k)
        CL = NF * NS * 2
        S.add("sp", lambda e, layer=layer: e.dma_start(out=k.cconv[:], in_=k.cconv_d[:, layer * CL:(layer + 1) * CL]),
              writes=["A:cconv"], dma=True)
        tl = [col_tiles(g) for g in GROUPS]
        prenorm(k, layer, 2, GROUPS[0], tl[0])
        for gi, g in enumerate(GROUPS):
            nxt = None
            if gi + 1 < len(GROUPS):
                nxt = (lambda gi=gi, layer=layer, tl=tl: prenorm(k, layer, 2, GROUPS[gi + 1], tl[gi + 1]))
            ffn(k, layer, gi, g, tl[gi], nxt)
        OL = NF * (NS + 1) * 2
        S.add("sp", lambda e, layer=layer: e.dma_start(out=k.oconv_d[:, layer * OL:(layer + 1) * OL], in_=k.oconv[:]),
              reads=["A:oconv"], dma=True, final=True)
        barrier()
    for c in range(NCH):
        S.add("sp", lambda e, c=c: e.dma_start(out=k.yT[:, c, :], in_=k.x[:, c, :]), reads=["x"], dma=True, final=True)
    print("arena base", k.arena_base, "max", k.arena_max, "top", SBUF_TOP, "over", k.arena_max - SBUF_TOP)
    assert not getattr(k, "arena_over", False)
    S.emit(nc, ctx)
    ctx.close()
    return nc, S


def col_tiles(g):
    tl = []
    for i in range(g["pn"] // 512):
        tl.append((g["p0"] + 512 * i, 512 * i, 512))
    if g["samp"]:
        tl.append((LP, g["pn"], TS))
    return tl


def ngcol(k, layer, i, c):
    j = (layer * 4 + i) * NCH + c
    return k.ng[:, j:j + 1]


def rstd_from_sq(k, n, tagr):
    S = k.S
    st = k.ps[6]

    def mm(e):
        ins = None
        for c in range(NCH):
            ins = e.matmul(st[:, :n], k.ones[:], k.sq[:, c, :n], start=(c == 0), stop=(c == NCH - 1))
        return ins
    S.add("pe", mm, reads=["sq", "ones"], writes=["ps6"])
    S.add("dve", lambda e: e.tensor_scalar(k.rstd[:, :n], st[:, :n], 1.0 / D, EPS, ALU.mult, ALU.add),
          reads=["ps6"], writes=["rstd"])
    S.add("act", lambda e: e.activation(k.rstd[:, :n], k.rstd[:, :n], AF.Ln), reads=["rstd"], writes=["rstd"])
    S.add("act", lambda e: e.activation(k.rstd[:, :n], k.rstd[:, :n], AF.Exp, scale=-0.5), reads=["rstd"], writes=["rstd"])


def prenorm(k, layer, gidx, g, tiles):
    S = k.S
    for (c0, l0, n) in tiles:
        for c in range(NCH):
            S.add("act", lambda e, c=c, c0=c0, n=n: e.activation(k.sq[:, c, :n], k.x[:, c, c0:c0 + n], AF.Square),
                  reads=["x"], writes=["sq"])
        rstd_from_sq(k, n, "pre")
        for c in range(NCH):
            S.add("dve", lambda e, c=c, c0=c0, l0=l0, n=n: e.scalar_tensor_tensor(
                k.h[:, c, l0:l0 + n], k.x[:, c, c0:c0 + n], ngcol(k, layer, gidx, c), k.rstd[:, :n],
                ALU.mult, ALU.mult), reads=["x", "rstd", "ng"], writes=["h"])


def postnorm_residual(k, layer, gidx, c0, n):
    S = k.S
    for c in range(NCH):
        S.add("act", lambda e, c=c: e.activation(k.sq[:, c, :n], k.mixbuf[:, c, :n], AF.Square),
              reads=["mixbuf"], writes=["sq"])
    rstd_from_sq(k, n, "post")
    for c in range(NCH):
        S.add("dve", lambda e, c=c: e.scalar_tensor_tensor(
            k.mixbuf[:, c, :n], k.mixbuf[:, c, :n], ngcol(k, layer, gidx, c), k.rstd[:, :n],
            ALU.mult, ALU.mult), reads=["mixbuf", "rstd", "ng"], writes=["mixbuf"])
        S.add("dve", lambda e, c=c: e.tensor_add(k.x[:, c, c0:c0 + n], k.x[:, c, c0:c0 + n], k.mixbuf[:, c, :n]),
              reads=["mixbuf", "x"], writes=["x"])


def ffn_setup(k):
    c = k.carve
    k.wgu = [c("wgu%d" % i, [128, 2, NCH, 128], BF16) for i in range(3)]
    k.wd = c("wds", [128, NF, 1024], BF16)
    k.hid = c("A:hid", [128, NF, GT], BF16)
    k.G = [c("G%d" % i, [128, 2 + 512], F32) for i in range(2)]
    k.Gs = [c("Gs%d" % i, [128, NS, 6], F32) for i in range(2)]
    k.cv = [c("cv%d" % i, [128, 512], F32) for i in range(2)]
    k.carry = c("A:carry", [128, NF, 2], F32)
    k.cconv = c("cconvs", [128, NF * NS * 2], F32)
    k.oconv = c("oconvs", [128, NF * (NS + 1) * 2], F32)
    if not hasattr(k, "ffn_cnt"):
        k.ffn_cnt = 0


def ffn(k, layer, gi, g, tiles, mid=None):
    S, nc = k.S, k.nc
    pn = g["pn"]
    st1 = [[] for _ in range(NF)]
    st2 = [[] for _ in range(NF)]
    for f in range(NF):
        add1 = lambda *a, f=f, **kw: st1[f].append((a, kw))
        add2 = lambda *a, f=f, **kw: st2[f].append((a, kw))
        slot = k.ffn_cnt % 3
        gs = k.ffn_cnt % 2
        k.ffn_cnt += 1
        W = k.wgu[slot]
        add1("pool", lambda e, f=f, W=W: e.dma_start(
            out=W[:].rearrange("p a c j -> p (a c j)"), in_=k.wgu_d[layer, f]),
            writes=[("A:wgu", slot)], dma=True)
        if gi == 0:
            add1("pool", lambda e, f=f: e.dma_start(out=k.wd[:, f, :], in_=k.wd_d[layer, f]),
                 writes=[("A:wd", f)], dma=True)
        G, Gs, = k.G[gs], k.Gs[gs]
        cwb = (layer * NF + f) * 4
        w0, w1, w2, bb = (k.cw[:, cwb + j:cwb + j + 1] for j in range(4))
        if gi == 0:
            add1("pool", lambda e, G=G: e.memset(G[:, 0:2], 0.0), writes=[("A:G", gs)])
        else:
            add1("pool", lambda e, G=G, f=f: e.tensor_copy(G[:, 0:2], k.carry[:, f, :]),
                  reads=["A:carry"], writes=[("A:G", gs)])
        if g["samp"]:
            cb = f * NS * 2
            add1("pool", lambda e, Gs=Gs, cb=cb: e.tensor_copy(
                Gs[:, :, 0:2], k.cconv[:, cb:cb + NS * 2].rearrange("p (s r) -> p s r", r=2)),
                reads=["A:cconv"], writes=[("A:Gs", gs)])
        for ti, (c0, l0, n) in enumerate(tiles):
            par = (f + ti) % 2
            pa = k.ps[par]
            pb = k.ps[2 + par]
            pan, pbn = "ps%d" % par, "ps%d" % (2 + par)

            def mmg(e, W=W, pa=pa, a=0, l0=l0, n=n):
                ins = None
                for c in range(NCH):
                    ins = e.matmul(pa[:, :n], W[:, a, c, :], k.h[:, c, l0:l0 + n], start=(c == 0), stop=(c == NCH - 1))
                return ins
            add1("pe", mmg, reads=[("A:wgu", slot), "h"], writes=[pan])
            add1("pe", lambda e, W=W, pb=pb, l0=l0, n=n, mmg=mmg: mmg(e, W, pb, 1, l0, n), reads=[("A:wgu", slot), "h"], writes=[pbn])
            samp_tile = (c0 >= LP)
            cv = k.cv[par]
            cvn = ("A:cv", par)
            if not samp_tile:
                add1("act", lambda e, G=G, pa=pa, l0=l0, n=n: e.activation(G[:, 2 + l0:2 + l0 + n], pa[:, :n], AF.Copy),
                      reads=[pan], writes=[("A:G", gs)])
                src = [G[:, l0 + j:l0 + j + n] for j in range(3)]
                dst = cv[:, :n]
            else:
                add1("act", lambda e, Gs=Gs, pa=pa, n=n: e.activation(
                    Gs[:, :, 2:6], pa[:, :n].rearrange("p (s t) -> p s t", t=LS), AF.Copy),
                    reads=[pan], writes=[("A:Gs", gs)])
                src = [Gs[:, :, j:j + LS] for j in range(3)]
                dst = cv[:, :n].rearrange("p (s t) -> p s t", t=LS)
            Gn = ("A:Gs", gs) if samp_tile else ("A:G", gs)
            add2("dve", lambda e, dst=dst, s0=src[0], w0=w0, bb=bb: e.tensor_scalar(dst, s0, w0, bb, ALU.mult, ALU.add),
                  reads=[Gn, "cw"], writes=[cvn])
            add2("dve", lambda e, dst=dst, s1=src[1], w1=w1: e.scalar_tensor_tensor(dst, s1, w1, dst, ALU.mult, ALU.add),
                  reads=[Gn, "cw", cvn], writes=[cvn])
            add2("dve", lambda e, dst=dst, s2=src[2], w2=w2: e.scalar_tensor_tensor(dst, s2, w2, dst, ALU.mult, ALU.add),
                  reads=[Gn, "cw", cvn], writes=[cvn])
            add2("act", lambda e, cv=cv, n=n: e.activation(cv[:, :n], cv[:, :n], AF.Silu), reads=[cvn], writes=[cvn])
            add2("dve", lambda e, cv=cv, pb=pb, f=f, l0=l0, n=n: e.tensor_tensor(k.hid[:, f, l0:l0 + n], cv[:, :n], pb[:, :n], ALU.mult),
                  reads=[cvn, pbn], writes=["A:hid"])
        ob = f * (NS + 1) * 2
        if gi < len(GROUPS) - 1:
            add2("pool", lambda e, G=G, f=f: e.tensor_copy(k.carry[:, f, :], G[:, pn:pn + 2]),
                  reads=[("A:G", gs)], writes=["A:carry"])
        else:
            add2("pool", lambda e, G=G, ob=ob: e.tensor_copy(k.oconv[:, ob:ob + 2], G[:, pn:pn + 2]),
                  reads=[("A:G", gs)], writes=["A:oconv"])
        if g["samp"]:
            add2("pool", lambda e, Gs=Gs, ob=ob: e.tensor_copy(
                k.oconv[:, ob + 2:ob + 2 + NS * 2].rearrange("p (s r) -> p s r", r=2), Gs[:, :, 4:6]),
                reads=[("A:Gs", gs)], writes=["A:oconv"])
    def flush(lst):
        for (a, kw) in lst:
            S.add(*a, **kw)
    if len(tiles) == 1:
        flush(st1[0])
        for f in range(NF):
            if f + 1 < NF:
                flush(st1[f + 1])
            flush(st2[f])
    else:
        for f in range(NF):
            flush(st1[f])
            flush(st2[f])
    if mid is not None:
        mid()
    for ti, (c0, l0, n) in enumerate(tiles):
        for m in range(NCH):
            po = k.ps[4 + m % 2]
            pon = "ps%d" % (4 + m % 2)

            def mmd(e, po=po, m=m, l0=l0, n=n):
                ins = None
                for f in range(NF):
                    ins = e.matmul(po[:, :n], k.wd[:, f, m * 128:(m + 1) * 128], k.hid[:, f, l0:l0 + n],
                                   start=(f == 0), stop=(f == NF - 1))
                return ins
            S.add("pe", mmd, reads=[("A:wd", f) for f in range(NF)] + ["A:hid"], writes=[pon])
            S.add("act", lambda e, po=po, m=m, n=n: e.activation(k.mixbuf[:, m, :n], po[:, :n], AF.Copy),
                  reads=[pon], writes=["mixbuf"])
        postnorm_residual(k, layer, 3, c0, n)


def _feat_major(a):
    lead = a.shape[:-1]
    nch = a.shape[-1] // 128
    a = a.reshape(-1, nch, 128)
    a = np.transpose(a, (2, 1, 0))
    return np.ascontiguousarray(a.reshape((128, nch) + lead))


def host_inputs(inp):
    f = np.float32
    shared = {}
    ng = np.asarray(inp["norm_g"], f)
    shared["ng"] = np.ascontiguousarray(
        ng.reshape(DEPTH, 4, NCH, 128).transpose(3, 0, 1, 2).reshape(128, -1))
    wg = np.asarray(inp["ffn_w_gate"], f).reshape(DEPTH, NCH, 128, NF, 128)
    wu = np.asarray(inp["ffn_w_up"], f).reshape(DEPTH, NCH, 128, NF, 128)
    wgu = np.stack([wg, wu], axis=0)
    shared["wgu"] = np.ascontiguousarray(wgu.transpose(1, 4, 3, 0, 2, 5).reshape(DEPTH, NF, 128, 2048))
    shared["wd"] = np.ascontiguousarray(np.asarray(inp["ffn_w_down"], f).reshape(DEPTH, NF, 128, 1024))
    cwt = np.concatenate([np.asarray(inp["ffn_conv_w"], f), np.asarray(inp["ffn_conv_b"], f)[:, None, :]], axis=1)
    shared["cw"] = np.ascontiguousarray(
        cwt.reshape(DEPTH, 4, NF, 128).transpose(3, 0, 2, 1).reshape(128, -1))
    shared.update(host_consts())
    wq = np.asarray(inp["ret_wq"], f)[0].reshape(NCH, 128, 4, 256)
    wk = np.asarray(inp["ret_wk"], f)[0].reshape(NCH, 128, 4, 256)
    wv = np.asarray(inp["ret_wv"], f)[0].reshape(NCH, 128, 4, 512)
    wgr = np.asarray(inp["ret_wg"], f)[0].reshape(NCH, 128, 4, 512)
    wret = np.zeros((4, 3, 128, NCH, 512), f)
    wret[:, 0, :, :, 0:256] = wq.transpose(2, 1, 0, 3)
    wret[:, 0, :, :, 256:512] = wk.transpose(2, 1, 0, 3)
    wret[:, 1] = wv.transpose(2, 1, 0, 3)
    wret[:, 2] = wgr.transpose(2, 1, 0, 3)
    shared["wret"] = wret.reshape(4, 3, 128, 4096)
    wo = np.asarray(inp["ret_wo"], f)[0].reshape(16, 128, 4, 2, 128)
    shared["wo_ret"] = np.ascontiguousarray(wo.transpose(2, 1, 3, 0, 4).reshape(4, 128, 4096))
    gq = np.asarray(inp["gla_wq"], f)[0].reshape(NCH, 128, 4, 128)
    gk = np.asarray(inp["gla_wk"], f)[0].reshape(NCH, 128, 4, 128)
    gv = np.asarray(inp["gla_wv"], f)[0].reshape(NCH, 128, 4, 256)
    gg = np.asarray(inp["gla_wg"], f)[0].reshape(NCH, 128, 4, 256)
    wgqk = np.zeros((4, 128, NCH, 256), f)
    wgqk[..., 0:128] = gq.transpose(2, 1, 0, 3)
    wgqk[..., 128:256] = gk.transpose(2, 1, 0, 3)
    shared["wgqk"] = wgqk.reshape(4, 128, 2048)
    wgvg = np.zeros((4, 128, NCH, 512), f)
    wgvg[..., 0:256] = gv.transpose(2, 1, 0, 3)
    wgvg[..., 256:512] = gg.transpose(2, 1, 0, 3)
    shared["wgvg"] = wgvg.reshape(4, 128, 4096)
    go = np.asarray(inp["gla_wo"], f)[0].reshape(8, 128, 4, 2, 128)
    shared["wo_gla"] = np.ascontiguousarray(go.transpose(2, 1, 3, 0, 4).reshape(4, 128, 2048))
    shared["wa1"] = np.ascontiguousarray(np.asarray(inp["gla_wa1"], f)[0].reshape(NCH, 128, 16).transpose(1, 0, 2))
    shared["wa2"] = np.ascontiguousarray(np.asarray(inp["gla_wa2"], f)[0])
    shared["ba"] = np.ascontiguousarray(np.asarray(inp["gla_ba"], f)[0].reshape(4, 128).T)
    shared["ngl"] = np.ascontiguousarray(np.broadcast_to(np.asarray(inp["gla_norm_g"], f)[0][None, :], (128, 256)))
    lam = np.stack([np.asarray(inp["s5_lambda_re"], f), np.asarray(inp["s5_lambda_im"], f)], axis=1)
    lq = lam.reshape(2, 2, 8, 8, 1, 64)
    lq = np.broadcast_to(lq, (2, 2, 8, 8, 16, 64)).transpose(0, 3, 4, 1, 2, 5)
    shared["s5lamQ"] = np.ascontiguousarray(lq.reshape(2, 128, 2, 512))
    lc = lam.reshape(2, 2, 8, 8, 4, 16).transpose(0, 3, 5, 1, 2, 4)
    shared["s5lamC"] = np.ascontiguousarray(lc.reshape(2, 128, 2, 32))
    bb = np.stack([np.asarray(inp["s5_b_re"], f), np.asarray(inp["s5_b_im"], f)], axis=1)
    bq = bb.reshape(2, 2, 8, 8, 64, 16).transpose(0, 3, 5, 1, 2, 4)
    shared["s5bQ"] = np.ascontiguousarray(bq.reshape(2, 128, 2, 512))
    cc_ = np.stack([np.asarray(inp["s5_c_re"], f), np.asarray(inp["s5_c_im"], f)], axis=1)
    cq = cc_.reshape(2, 2, 8, 8, 16, 4, 16).transpose(0, 3, 6, 2, 5, 1, 4)
    shared["s5cC"] = np.ascontiguousarray(cq.reshape(2, 128, 8, 4, 2, 16))
    ldt = np.asarray(inp["s5_log_dt"], f).reshape(2, 8, 8, 1)
    shared["s5dt"] = np.ascontiguousarray(np.broadcast_to(ldt, (2, 8, 8, 16)).transpose(0, 2, 3, 1).reshape(2, 128, 8))
    shared["s5d"] = np.ascontiguousarray(np.asarray(inp["s5_d"], f).reshape(2, 8, 128).transpose(0, 2, 1))
    shared["s5bd"] = (np.arange(128)[:, None] // 16 == np.arange(8)[None, :]).astype(f)
    tau = np.zeros((128, 2, 32), f)
    tau[:, 0, :] = np.arange(1, 33)
    tau[:, 1, 1:] = 1.0
    shared["s5tau"] = tau
    wgl = np.asarray(inp["s5_w_glu"], f).reshape(2, 8, 128, 2, 8, 128)
    shared["wglu"] = np.ascontiguousarray(wgl.transpose(0, 4, 2, 3, 1, 5).reshape(2, 8, 128, 2048))
    s5st = np.stack([np.asarray(inp["state_s5_re"], f), np.asarray(inp["state_s5_im"], f)], axis=2)
    sret_all = np.asarray(inp["state_ret"], f)[0]
    sgla_all = np.asarray(inp["state_gla"], f)[0]
    per = []
    xp = np.asarray(inp["x_prompt"], f)
    xs = np.asarray(inp["x_sample"], f)
    cc = np.asarray(inp["cache_ffn_conv"], f)
    for c in range(NCORES):
        d = {}
        xa = np.concatenate([xp[c], xs[c * NS:(c + 1) * NS].reshape(TS, D)], axis=0)
        d["xT"] = np.ascontiguousarray(xa.reshape(T, NCH, 128).transpose(2, 1, 0))
        ccc = cc[:, c * NS:(c + 1) * NS]
        d["A:cconv"] = np.ascontiguousarray(
            ccc.reshape(DEPTH, NS, 2, NF, 128).transpose(4, 0, 3, 1, 2).reshape(128, -1))
        h0 = s5st[:, c * NS:(c + 1) * NS].reshape(2, NS, 2, 8, 8, 4, 16)
        d["s5h0"] = np.ascontiguousarray(h0.transpose(0, 4, 6, 1, 2, 3, 5).reshape(2, 128, NS, 2, 32))
        sr = sret_all[c * NS:(c + 1) * NS].reshape(NS, 4, 2, 128, 512)
        d["sret"] = np.ascontiguousarray(sr.transpose(0, 1, 3, 2, 4))
        d["sgla"] = np.ascontiguousarray(sgla_all[c * NS:(c + 1) * NS].reshape(NS, 4, 128, 1, 256))
        per.append(d)
    return shared, per


def host_consts():
    f = np.float32
    c = {}
    c["ident"] = np.eye(128, dtype=f)
    cm = np.zeros((128, 10, 128), np.float64)
    jj = np.arange(128)[:, None]
    ii = np.arange(128)[None, :]
    for h in range(4):
        gam = RET_G[h]
        cm[:, h, :] = np.where(ii >= jj, gam ** np.maximum(ii - jj, 0), 0.0)
        m = np.where((ii >= jj) & (ii // 4 == jj // 4) & (ii < 64) & (jj < 64), gam ** np.maximum(ii - jj, 0), 0.0)
        cm[:, 4 + h, :] = m
    cm[:, 8, :] = ((ii >= jj) & (ii // 64 == jj // 64))
    cm[:, 9, :] = ((ii >= jj) & (ii // 4 == jj // 4) & (ii < 64) & (jj < 64))
    c["cmask"] = cm.astype(f)
    ac = np.zeros((128, 32), np.float64)
    p = np.arange(128)
    for h in range(4):
        gam = RET_G[h]
        ac[:, h] = gam ** (p + 1.0)
        ac[:, 4 + h] = gam ** (127.0 - p)
        ac[:, 8 + h] = gam ** ((p % 4) + 1.0)
        ac[:, 12 + h] = gam ** (3.0 - (p % 4))
    for s_ in range(NS):
        ac[:, 16 + s_] = ((p // 4) == s_) & (p < 64)
    c["acol"] = ac.astype(f)
    freq = (f(1.0) / (f(10000.0) ** np.linspace(0.0, 1.0, 128, dtype=f))).astype(f)
    pos = np.concatenate([np.arange(LP), np.tile(16384 + np.arange(LS), NS)]).astype(f)
    ang = (pos[:, None] * freq[None, :]).astype(f).astype(np.float64)
    c["cosT"] = np.ascontiguousarray(np.cos(ang).T.astype(f))
    c["sinT"] = np.ascontiguousarray(np.sin(ang).T.astype(f))
    rm = np.ones((GT,), f)
    rm[0:512:64] = 0.0
    rm[512:576:4] = 0.0
    c["rmask"] = np.ascontiguousarray(np.broadcast_to(rm[None, :], (128, GT)))
    return c


def host_outputs(res):
    f = np.float32
    y_p = np.zeros((NCORES, LP, D), f)
    y_s = np.zeros((NCORES * NS, LS, D), f)
    conv_p = np.zeros((DEPTH, NCORES, 2, DFF), f)
    conv_s = np.zeros((DEPTH, NCORES * NS, 2, DFF), f)
    for c in range(NCORES):
        r = res[c]
        yT = np.asarray(r["yT"]).reshape(128, NCH, T)
        ya = yT.transpose(2, 1, 0).reshape(T, D)
        y_p[c] = ya[:LP]
        y_s[c * NS:(c + 1) * NS] = ya[LP:].reshape(NS, LS, D)
        oc = np.asarray(r["A:oconv"]).reshape(128, DEPTH, NF, NS + 1, 2)
        oc = oc.transpose(1, 3, 4, 2, 0).reshape(DEPTH, NS + 1, 2, DFF)
        conv_p[:, c] = oc[:, 0]
        conv_s[:, c * NS:(c + 1) * NS] = oc[:, 1:]
    ret_p = np.zeros((1, NCORES, 4, 256, 512), f)
    ret_s = np.zeros((1, NCORES * NS, 4, 256, 512), f)
    gla_p = np.zeros((1, NCORES, 4, 128, 256), f)
    gla_s = np.zeros((1, NCORES * NS, 4, 128, 256), f)
    for c in range(NCORES):
        r = res[c]
        ret_p[0, c] = np.asarray(r["oret_p"]).reshape(4, 128, 2, 512).transpose(0, 2, 1, 3).reshape(4, 256, 512)
        ret_s[0, c * NS:(c + 1) * NS] = np.asarray(r["oret_s"]).reshape(NS, 4, 128, 2, 512).transpose(0, 1, 3, 2, 4).reshape(NS, 4, 256, 512)
        gla_p[0, c] = np.asarray(r["ogla_p"]).reshape(4, 128, 256)
        gla_s[0, c * NS:(c + 1) * NS] = np.asarray(r["ogla_s"]).reshape(NS, 4, 128, 256)
    s5_p = np.zeros((2, 2, NCORES, 64, 64), f)
    s5_s = np.zeros((2, 2, NCORES * NS, 64, 64), f)
    for c in range(NCORES):
        r = res[c]
        a = np.asarray(r["os5p"]).reshape(2, 8, 16, 2, 8, 4)
        s5_p[:, :, c] = a.transpose(3, 0, 4, 1, 5, 2).reshape(2, 2, 64, 64)
        a = np.asarray(r["os5s"]).reshape(2, 8, 16, NS, 2, 8, 4)
        s5_s[:, :, c * NS:(c + 1) * NS] = a.transpose(4, 0, 3, 5, 1, 6, 2).reshape(2, 2, NS, 64, 64)
    return dict(s5re_p=s5_p[0], s5im_p=s5_p[1], s5re_s=s5_s[0], s5im_s=s5_s[1],
                y_p=y_p, y_s=y_s, conv_p=conv_p, conv_s=conv_s, ret_p=ret_p, ret_s=ret_s, gla_p=gla_p, gla_s=gla_s)


_CACHE = {}


def run(inputs, flags):
    key = tuple(sorted(flags.items()))
    if key not in _CACHE:
        _CACHE[key] = build_program(flags)
    nc, S = _CACHE[key]
    shared, per = host_inputs(inputs)
    in_maps = []
    for c in range(NCORES):
        m = dict(shared)
        m.update(per[c])
        in_maps.append(m)
    res = run_bass_kernel_spmd(nc, in_maps, core_ids=list(range(NCORES)))
    return host_outputs(res.results)


def kernel(**inputs):
    o = run(inputs, dict(s5=True, ret=True, gla=True))
    return (o["y_p"], o["y_s"], o["s5re_p"], o["s5im_p"], o["ret_p"], o["gla_p"], o["conv_p"],
            o["s5re_s"], o["s5im_s"], o["ret_s"], o["gla_s"], o["conv_s"])


RET_G = [1.0 - 2.0 ** (-5.0 - h) for h in range(4)]


def mixer_setup(k, layer, kind):
    if kind == 0:
        s5_setup(k, layer)
    else:
        attn_setup(k, layer, kind)


def mixer_group(k, layer, kind, gi, g, tiles, mid=None):
    if kind == 0:
        s5_group(k, layer, gi, g, tiles, mid)
    else:
        attn_group(k.attn_ns[kind], layer, kind, gi, g, tiles, mid)


def mixer_finish(k, layer, kind):
    if kind == 0:
        s5_finish(k, layer)
    else:
        attn_finish(k.attn_ns[kind], layer, kind)


def attn_cfg(kind):
    if kind == 1:
        return dict(NDC=2, DV=512, CH=128, NKC=16, nm="ret")
    return dict(NDC=1, DV=256, CH=64, NKC=8, nm="gla")


def attn_setup(k0, layer, kind):
    k = K()
    k.__dict__.update(k0.__dict__)
    if not hasattr(k0, "attn_ns"):
        k0.attn_ns = {}
    k0.attn_ns[kind] = k
    S, c = k.S, k.carve
    cf = attn_cfg(kind)
    NDC, DV = cf["NDC"], cf["DV"]
    k.acf = cf
    if kind == 1:
        k.sst_d, k.osts_d, k.ostp_d = k.sret_d, k.oret_s_d, k.oret_p_d
    else:
        k.sst_d, k.osts_d, k.ostp_d = k.sgla_d, k.ogla_s_d, k.ogla_p_d
    k.aw = [c("w%d" % i, [128, 8 * 512], BF16) for i in range(2)]
    k.aw_cnt = 0
    k.qT = c("qT", [128, NDC, GT], BF16)
    k.kT = c("kT", [128, NDC, GT], BF16)
    k.vtok = c("vtok", [128, 5, DV], BF16)
    k.gtok = c("gtok", [128, 5, DV], BF16)
    k.onT = c("onT", [128, cf["NKC"], GT], BF16)
    k.PT = [c("PT%d" % i, [128, 128], BF16) for i in range(2)]
    k.to = c("to", [128, DV], F32)
    k.tos = [k.to, c("to1", [128, DV], F32)] if kind == 2 else [k.to, k.to]
    k.on = c("on", [128, DV], BF16)
    k.og = c("og", [128, DV], BF16)
    k.ktok = c("ktok", [128, NDC, 128], BF16)
    k.Sst = c("Sst", [128, 4, NDC, DV], F32)
    k.Sb = c("Sb", [128, NDC, DV], BF16)
    k.S0b = [c("S0b%d" % i, [128, NDC, DV], BF16) for i in range(2)]
    k.S0f = [c("S0f%d" % i, [128, NDC, DV], F32) for i in range(2)]
    k.s0_cnt = 0
    k.qm = c("qm", [128, NDC, NS, 64], BF16)
    k.km = [c("km%d" % i, [128, NDC, 128], BF16) for i in range(2)]
    k.amask = c("amask", [128, 5, 128], BF16)
    k.st = c("st", [128, 8], F32)
    S.add("pool", lambda e: e.memset(k.Sst[:], 0.0), writes=["A:S"])
    S.add("pool", lambda e: e.memset(k.qm[:], 0.0), writes=["A:qm"])
    if kind == 1:
        k.rt = [c("rt%d" % i, [128, 512], F32) for i in range(2)]
        k.cs = c("cs", [128, 2, GT], F32)
        k.amaskS = c("amaskS", [128, 4, 64], BF16)
        S.add("pool", lambda e: e.dma_start(out=k.amask[:, 0:4, :], in_=k.cmask_d[:, 0:4, :]), writes=["A:amask"], dma=True)
        S.add("pool", lambda e: e.dma_start(out=k.amaskS[0:64, :, :], in_=k.cmask_d[0:64, 4:8, 0:64]), writes=["A:amask"], dma=True)
    else:
        k.amaskS = c("amaskS", [128, 1, 64], BF16)
        S.add("pool", lambda e: e.dma_start(out=k.amask[:, 0:1, :], in_=k.cmask_d[:, 8:9, :]), writes=["A:amask"], dma=True)
        S.add("pool", lambda e: e.dma_start(out=k.amaskS[0:64, :, :], in_=k.cmask_d[0:64, 9:10, 0:64]), writes=["A:amask"], dma=True)
        k.la = c("la", [128, GT], F32)
        k.EQ = c("EQ", [128, GT], F32)
        k.EK = c("EK", [128, GT], F32)
        k.rT = c("rT", [16, GT], BF16)
        k.wa1 = c("wa1", [128, 8, 16], BF16)
        k.wa2 = c("wa2", [16, 512], BF16)
        k.nba = c("nba", [128, 4], F32)
        k.ngl = c("ngl", [128, 256], F32)
        k.rmask = c("rmask", [128, GT], F32)
        S.add("pool", lambda e: e.dma_start(out=k.wa1[:], in_=k.wa1_d), writes=["A:wa"], dma=True)
        S.add("pool", lambda e: e.dma_start(out=k.wa2[:], in_=k.wa2_d), writes=["A:wa"], dma=True)
        S.add("sp", lambda e: e.dma_start(out=k.nba[:], in_=k.ba_d), writes=["A:nba"], dma=True)
        S.add("sp", lambda e: e.dma_start(out=k.ngl[:], in_=k.ngl_d), writes=["A:ngl"], dma=True)
        S.add("sp", lambda e: e.dma_start(out=k.rmask[:], in_=k.rmask_d), writes=["A:rmask"], dma=True)
        S.add("dve", lambda e: e.tensor_scalar_mul(k.nba[:], k.nba[:], -1.0), reads=["A:nba"], writes=["A:nba"])


def aw_load(k, src_ap, ncols):
    slot = k.aw_cnt % 2
    k.aw_cnt += 1
    W = k.aw[slot]
    hn = ("A:w", slot)
    k.S.add("pool", lambda e: e.dma_start(out=W[:, 0:ncols], in_=src_ap), writes=[hn], dma=True)
    return W, hn


def attn_group(k, layer, kind, gi, g, tiles, mid=None):
    S = k.S
    cf = k.acf
    NDC, DV, CH, NKC = cf["NDC"], cf["DV"], cf["CH"], cf["NKC"]
    NVC = DV // 128
    ret = (kind == 1)
    blocks = [(b * 128, 128) for b in range(g["pn"] // 128)]
    if g["samp"]:
        blocks.append((g["pn"], TS))
    if ret:
        for (c0, l0, n) in tiles:
            S.add("sp", lambda e, c0=c0, l0=l0, n=n: e.dma_start(out=k.cs[:, 0, l0:l0 + n], in_=k.cos_d[:, c0:c0 + n]),
                  writes=["A:cs"], dma=True)
            S.add("sp", lambda e, c0=c0, l0=l0, n=n: e.dma_start(out=k.cs[:, 1, l0:l0 + n], in_=k.sin_d[:, c0:c0 + n]),
                  writes=["A:cs"], dma=True)
    else:
        for (c0, l0, n) in tiles:
            def mmr(e, l0=l0, n=n):
                ins = None
                for c in range(NCH):
                    ins = e.matmul(k.ps[0][0:16, :n], k.wa1[:, c, :], k.h[:, c, l0:l0 + n], start=(c == 0), stop=(c == NCH - 1))
                return ins
            S.add("pe", mmr, reads=["A:wa", "h"], writes=["ps0"])
            S.add("act", lambda e, l0=l0, n=n: e.activation(k.rT[:, l0:l0 + n], k.ps[0][0:16, :n], AF.Copy),
                  reads=["ps0"], writes=["A:rT"])
    for h in range(4):
        if ret:
            W, wh = aw_load(k, k.wret_d[h, 0], 4096)
            Wv = W[:, 0:4096].rearrange("p (c j) -> p c j", j=512)
            for (c0, l0, n) in tiles:
                for part in range(4):
                    def mmp(e, part=part, l0=l0, n=n, Wv=Wv):
                        ins = None
                        for c in range(NCH):
                            ins = e.matmul(k.ps[part][:, :n], Wv[:, c, part * 128:(part + 1) * 128], k.h[:, c, l0:l0 + n],
                                           start=(c == 0), stop=(c == NCH - 1))
                        return ins
                    S.add("pe", mmp, reads=[wh, "h"], writes=["ps%d" % part])
                for (pa, pb, dst, dn, sc) in ((0, 1, k.qT, "A:qT", 1.0), (2, 3, k.kT, "A:kT", 1.0 / 16.0)):
                    A, B = k.ps[pa], k.ps[pb]
                    cosv, sinv = k.cs[:, 0, l0:l0 + n], k.cs[:, 1, l0:l0 + n]
                    t0, t1 = k.rt[0], k.rt[1]
                    for (x0, x1, tr0, tr1, dc, op) in ((A, B, cosv, sinv, 0, ALU.subtract), (B, A, cosv, sinv, 1, ALU.add)):
                        S.add("dve", lambda e, x0=x0, tr0=tr0, n=n, sc=sc, t0=t0: e.scalar_tensor_tensor(
                            t0[:, :n], x0[:, :n], sc, tr0, ALU.mult, ALU.mult),
                            reads=["ps%d" % pa, "ps%d" % pb, "A:cs"], writes=["A:rt0"])
                        S.add("dve", lambda e, x1=x1, tr1=tr1, n=n, sc=sc, t1=t1: e.scalar_tensor_tensor(
                            t1[:, :n], x1[:, :n], sc, tr1, ALU.mult, ALU.mult),
                            reads=["ps%d" % pa, "ps%d" % pb, "A:cs"], writes=["A:rt1"])
                        S.add("pool", lambda e, dst=dst, dc=dc, l0=l0, n=n, t0=t0, t1=t1, op=op: e.tensor_tensor(
                            dst[:, dc, l0:l0 + n], t0[:, :n], t1[:, :n], op),
                            reads=["A:rt0", "A:rt1"], writes=[dn])
        else:
            W, wh = aw_load(k, k.wgqk_d[h], 2048)
            Wv = W[:, 0:2048].rearrange("p (c j) -> p c j", j=256)
            for (c0, l0, n) in tiles:
                for part in range(2):
                    def mmp(e, part=part, l0=l0, n=n, Wv=Wv):
                        ins = None
                        for c in range(NCH):
                            ins = e.matmul(k.ps[part][:, :n], Wv[:, c, part * 128:(part + 1) * 128], k.h[:, c, l0:l0 + n],
                                           start=(c == 0), stop=(c == NCH - 1))
                        return ins
                    S.add("pe", mmp, reads=[wh, "h"], writes=["ps%d" % part])
                S.add("pe", lambda e, h=h, l0=l0, n=n: e.matmul(k.ps[2][:, :n], k.wa2[:, h * 128:(h + 1) * 128], k.rT[:, l0:l0 + n],
                                                               start=True, stop=True),
                      reads=["A:wa", "A:rT"], writes=["ps2"])
                S.add("act", lambda e, h=h, l0=l0, n=n: e.activation(k.la[:, l0:l0 + n], k.ps[2][:, :n], AF.Exp,
                                                                     bias=k.nba[:, h:h + 1], scale=-1.0),
                      reads=["ps2", "A:nba"], writes=["A:la"])
                S.add("dve", lambda e, l0=l0, n=n: e.tensor_scalar_add(k.la[:, l0:l0 + n], k.la[:, l0:l0 + n], 1.0),
                      reads=["A:la"], writes=["A:la"])
                S.add("act", lambda e, l0=l0, n=n: e.activation(k.la[:, l0:l0 + n], k.la[:, l0:l0 + n], AF.Ln),
                      reads=["A:la"], writes=["A:la"])
                S.add("dve", lambda e, l0=l0, n=n: e.tensor_tensor_scan(k.EQ[:, l0:l0 + n], k.rmask[:, l0:l0 + n], k.la[:, l0:l0 + n],
                                                                        0.0, ALU.mult, ALU.add),
                      reads=["A:la", "A:rmask"], writes=["A:EQ"])
                S.add("act", lambda e, l0=l0, n=n: e.activation(k.EK[:, l0:l0 + n], k.EQ[:, l0:l0 + n], AF.Exp, scale=1.0 / 16.0),
                      reads=["A:EQ"], writes=["A:EK"])
                S.add("act", lambda e, l0=l0, n=n: e.activation(k.EQ[:, l0:l0 + n], k.EQ[:, l0:l0 + n], AF.Exp, scale=-1.0 / 16.0),
                      reads=["A:EQ"], writes=["A:EQ"])
                S.add("dve", lambda e, l0=l0, n=n: e.scalar_tensor_tensor(k.qT[:, 0, l0:l0 + n], k.ps[0][:, :n], 128.0 ** -0.5,
                                                                          k.EQ[:, l0:l0 + n], ALU.mult, ALU.mult),
                      reads=["ps0", "A:EQ"], writes=["A:qT"])
                S.add("dve", lambda e, l0=l0, n=n: e.tensor_tensor(k.kT[:, 0, l0:l0 + n], k.ps[1][:, :n], k.EK[:, l0:l0 + n], ALU.mult),
                      reads=["ps1", "A:EK"], writes=["A:kT"])
        if ret:
            srcs = [(k.wret_d[h, 1], 4096, 512, 0, k.vtok, "A:vtok", AF.Copy), (k.wret_d[h, 2], 4096, 512, 0, k.gtok, "A:gtok", AF.Silu)]
        else:
            srcs = [(k.wgvg_d[h], 4096, 512, 0, k.vtok, "A:vtok", AF.Copy), (None, 4096, 512, 256, k.gtok, "A:gtok", AF.Silu)]
        for (src, ncols, wj, coff, dst, dn, fn) in srcs:
            if src is not None:
                W, wh = aw_load(k, src, ncols)
                Wv = W[:, 0:ncols].rearrange("p (c j) -> p c j", j=wj)
            for bi, (bl0, bn) in enumerate(blocks):
                pb_ = k.ps[4 + bi % 2]
                pbn = "ps%d" % (4 + bi % 2)

                def mmv(e, bl0=bl0, bn=bn, Wv=Wv, coff=coff, pb_=pb_):
                    ins = None
                    for c in range(NCH):
                        ins = e.matmul(pb_[0:bn, 0:DV], k.h[:, c, bl0:bl0 + bn], Wv[:, c, coff:coff + DV],
                                       start=(c == 0), stop=(c == NCH - 1))
                    return ins
                S.add("pe", mmv, reads=[wh, "h"], writes=[pbn])
                S.add("act", lambda e, dst=dst, bi=bi, bn=bn, pb_=pb_, fn=fn: e.activation(dst[0:bn, bi, :], pb_[0:bn, 0:DV], fn),
                      reads=[pbn], writes=[dn])
        S.add("act", lambda e, h=h: e.activation(k.Sb[:], k.Sst[:, h], AF.Copy), reads=["A:S"], writes=["A:Sb"])
        LP_, LQ_ = [[] for _ in blocks], [[] for _ in blocks]
        for bi, (bl0, bn) in enumerate(blocks):
            samp = (bn == TS)
            S.redirect = LP_[bi]
            po1 = k.ps[1 if bi % 2 == 0 else 4]
            po2 = k.ps[2 if bi % 2 == 0 else 5]
            p1n = "ps1" if bi % 2 == 0 else "ps4"
            p2n = "ps2" if bi % 2 == 0 else "ps5"
            tob = k.to if ret else k.tos[bi % 2]
            ton = "A:to" if ret else ("A:to", bi % 2)
            mask = k.amaskS[0:64, (h if ret else 0), :] if samp else k.amask[:, (h if ret else 0), :]
            PT = k.PT[bi % 2]
            ptn = ("A:PT", bi % 2)

            def mms(e, tob=tob, po1=po1, po2=po2, bl0=bl0, bn=bn):
                ins = None
                for dc in range(NDC):
                    ins = e.matmul(k.ps[0][0:bn, 0:bn], k.kT[:, dc, bl0:bl0 + bn], k.qT[:, dc, bl0:bl0 + bn],
                                   start=(dc == 0), stop=(dc == NDC - 1))
                return ins
            S.add("pe", mms, reads=["A:qT", "A:kT"], writes=["ps0"])
            S.add("dve", lambda e, tob=tob, po1=po1, po2=po2, PT=PT, bn=bn, mask=mask: e.tensor_tensor(PT[0:bn, 0:bn], k.ps[0][0:bn, 0:bn], mask, ALU.mult),
                  reads=["ps0", "A:amask"], writes=[ptn])
            if samp:
                for s in range(NS):
                    for dc in range(NDC):
                        S.add("pool", lambda e, tob=tob, po1=po1, po2=po2, s=s, dc=dc, bl0=bl0: e.tensor_copy(
                            k.qm[:, dc, s, 4 * s:4 * s + 4], k.qT[:, dc, bl0 + 4 * s:bl0 + 4 * s + 4]),
                            reads=["A:qT"], writes=["A:qm"])
            if ret or samp:
                S.add("pe", lambda e, tob=tob, po1=po1, po2=po2, PT=PT, bn=bn, bi=bi: e.matmul(po1[0:bn, 0:DV], PT[0:bn, 0:bn], k.vtok[0:bn, bi, :],
                                                                    start=True, stop=True),
                      reads=[ptn, "A:vtok"], writes=[p1n])
            if not samp:
                nchk = bn // CH
                if ret:
                    def mmst(e, tob=tob, po1=po1, po2=po2, bl0=bl0, bn=bn):
                        ins = None
                        for dc in range(NDC):
                            ins = e.matmul(po2[0:bn, 0:DV], k.qT[:, dc, bl0:bl0 + bn], k.Sb[:, dc, :],
                                           start=(dc == 0), stop=(dc == NDC - 1))
                        return ins
                    S.add("pe", mmst, reads=["A:qT", "A:Sb"], writes=[p2n])
            else:
                attn_sample_states(k, h, bi, bl0, po2, p2n)
            S.redirect = LQ_[bi]
            if ret or samp:
                S.add("act", lambda e, tob=tob, po1=po1, po2=po2, bn=bn: e.activation(tob[0:bn, :], po1[0:bn, 0:DV], AF.Copy), reads=[p1n], writes=[ton])
                if ret:
                    qd = k.acol[0:bn, (8 + h if samp else h):(8 + h if samp else h) + 1]
                    S.add("dve", lambda e, tob=tob, po1=po1, po2=po2, bn=bn, qd=qd: e.scalar_tensor_tensor(tob[0:bn, :], po2[0:bn, 0:DV], qd, tob[0:bn, :],
                                                                                ALU.mult, ALU.add),
                          reads=[p2n, ton, "acol"], writes=[ton])
                else:
                    S.add("dve", lambda e, tob=tob, po1=po1, po2=po2, bn=bn: e.tensor_tensor(tob[0:bn, :], po2[0:bn, 0:DV], tob[0:bn, :], ALU.add),
                          reads=[p2n, ton], writes=[ton])
            S.redirect = LP_[bi]
            if (not ret) and (not samp):
                for ci in range(bn // CH):
                    r0 = ci * CH
                    cl0 = bl0 + r0

                    def mmo(e, tob=tob, po1=po1, po2=po2, PT=PT, r0=r0, cl0=cl0, bi=bi):
                        e.matmul(po1[r0:r0 + CH, 0:DV], PT[:, r0:r0 + CH], k.vtok[:, bi, :], start=True, stop=False)
                        return e.matmul(po1[r0:r0 + CH, 0:DV], k.qT[:, 0, cl0:cl0 + CH], k.Sb[:, 0, :], start=False, stop=True)
                    S.add("pe", mmo, reads=[ptn, "A:vtok", "A:qT", "A:Sb"], writes=[p1n])
                    S.add("act", lambda e, tob=tob, po1=po1, po2=po2, r0=r0: e.activation(tob[r0:r0 + CH, :], po1[r0:r0 + CH, 0:DV], AF.Copy),
                          reads=[p1n], writes=[ton])
                    attn_state_update(k, h, bi, bl0, r0, CH, cl0 + CH - 1)
            S.redirect = LQ_[bi]
            S.add("dve", lambda e, tob=tob, po1=po1, po2=po2: e.memset(k.st[:, 0:2], 0.0), writes=["A:st"])
            if ret:
                S.add("act", lambda e, tob=tob, po1=po1, po2=po2, bn=bn: e.activation(k.on[0:bn, :], tob[0:bn, :], AF.Copy, accum_out=k.st[0:bn, 0:1]),
                      reads=[ton, "A:st"], writes=["A:on", "A:st"])
            S.add("act", lambda e, tob=tob, po1=po1, po2=po2, bn=bn: e.activation(k.on[0:bn, :], tob[0:bn, :], AF.Square, accum_out=k.st[0:bn, 1:2]),
                  reads=[ton, "A:st"], writes=["A:on", "A:st"])
            sts = ["A:st"]
            if ret:
                S.add("dve", lambda e, tob=tob, po1=po1, po2=po2, bn=bn: e.tensor_scalar_mul(k.st[0:bn, 2:3], k.st[0:bn, 0:1], 1.0 / DV), reads=sts, writes=sts)
                S.add("dve", lambda e, tob=tob, po1=po1, po2=po2, bn=bn: e.tensor_tensor(k.st[0:bn, 3:4], k.st[0:bn, 2:3], k.st[0:bn, 2:3], ALU.mult), reads=sts, writes=sts)
                S.add("dve", lambda e, tob=tob, po1=po1, po2=po2, bn=bn: e.scalar_tensor_tensor(k.st[0:bn, 4:5], k.st[0:bn, 1:2], 1.0 / DV, k.st[0:bn, 3:4],
                                                                     ALU.mult, ALU.subtract), reads=sts, writes=sts)
                S.add("dve", lambda e, tob=tob, po1=po1, po2=po2, bn=bn: e.tensor_scalar_add(k.st[0:bn, 4:5], k.st[0:bn, 4:5], EPS), reads=sts, writes=sts)
            else:
                S.add("dve", lambda e, tob=tob, po1=po1, po2=po2, bn=bn: e.tensor_scalar(k.st[0:bn, 4:5], k.st[0:bn, 1:2], 1.0 / DV, EPS, ALU.mult, ALU.add),
                      reads=sts, writes=sts)
            S.add("act", lambda e, tob=tob, po1=po1, po2=po2, bn=bn: e.activation(k.st[0:bn, 5:6], k.st[0:bn, 4:5], AF.Ln), reads=sts, writes=sts)
            S.add("act", lambda e, tob=tob, po1=po1, po2=po2, bn=bn: e.activation(k.st[0:bn, 5:6], k.st[0:bn, 5:6], AF.Exp, scale=-0.5), reads=sts, writes=sts)
            if ret:
                S.add("dve", lambda e, tob=tob, po1=po1, po2=po2, bn=bn: e.scalar_tensor_tensor(k.st[0:bn, 6:7], k.st[0:bn, 2:3], -1.0, k.st[0:bn, 5:6],
                                                                     ALU.mult, ALU.mult), reads=sts, writes=sts)
                S.add("act", lambda e, tob=tob, po1=po1, po2=po2, bn=bn: e.activation(k.on[0:bn, :], tob[0:bn, :], AF.Identity,
                                                           bias=k.st[0:bn, 6:7], scale=k.st[0:bn, 5:6]),
                      reads=[ton, "A:st"], writes=["A:on"])
                S.add("dve", lambda e, tob=tob, po1=po1, po2=po2, bn=bn, bi=bi: e.tensor_tensor(k.og[0:bn, :], k.on[0:bn, :], k.gtok[0:bn, bi, :], ALU.mult),
                      reads=["A:on", "A:gtok"], writes=["A:og"])
            else:
                S.add("act", lambda e, tob=tob, po1=po1, po2=po2, bn=bn: e.activation(tob[0:bn, :], tob[0:bn, :], AF.Identity, scale=k.st[0:bn, 5:6]),
                      reads=[ton, "A:st"], writes=[ton])
                S.add("dve", lambda e, tob=tob, po1=po1, po2=po2, bn=bn: e.tensor_tensor(tob[0:bn, :], tob[0:bn, :], k.ngl[0:bn, :], ALU.mult),
                      reads=[ton, "A:ngl"], writes=[ton])
                S.add("dve", lambda e, tob=tob, po1=po1, po2=po2, bn=bn, bi=bi: e.tensor_tensor(k.og[0:bn, :], tob[0:bn, :], k.gtok[0:bn, bi, :], ALU.mult),
                      reads=[ton, "A:gtok"], writes=["A:og"])

            def mmt(e, tob=tob, po1=po1, po2=po2, bn=bn):
                ins = None
                for vc in range(NVC):
                    ins = e.transpose(k.pst[:, vc * 128:vc * 128 + bn], k.og[0:bn, vc * 128:(vc + 1) * 128], k.ident[0:bn, 0:bn])
                return ins
            S.add("pe", mmt, reads=["A:og", "ident"], writes=["pst"])
            S.add("act", lambda e, tob=tob, po1=po1, po2=po2, h=h, bl0=bl0, bn=bn: e.activation(
                k.onT[:, h * NVC:(h + 1) * NVC, bl0:bl0 + bn],
                k.pst[:, 0:NVC * 128].rearrange("p (v t) -> p v t", t=128)[:, :, 0:bn], AF.Copy),
                reads=["pst"], writes=["A:onT"])
            S.redirect = LP_[bi]
            if ret and not samp:
                attn_state_update(k, h, bi, bl0, 0, 128, None)
        S.redirect = None
        S.flush(LP_[0])
        for bi in range(len(blocks)):
            if bi + 1 < len(blocks):
                S.flush(LP_[bi + 1])
            S.flush(LQ_[bi])
    if mid is not None:
        mid()
    wod = k.wo_ret_d if ret else k.wo_gla_d
    for (c0, l0, n) in tiles:
        for mp in range(4):
            W, wh = aw_load(k, wod[mp], 2 * NKC * 128)
            Wv = W[:, 0:2 * NKC * 128].rearrange("p (a c j) -> p a c j", a=2, j=128)
            for mi in range(2):
                m = mp * 2 + mi
                po = k.ps[4 + m % 2]
                pon = "ps%d" % (4 + m % 2)

                def mmw(e, Wv=Wv, mi=mi, po=po, l0=l0, n=n):
                    ins = None
                    for kc in range(NKC):
                        ins = e.matmul(po[:, :n], Wv[:, mi, kc, :], k.onT[:, kc, l0:l0 + n], start=(kc == 0), stop=(kc == NKC - 1))
                    return ins
                S.add("pe", mmw, reads=[wh, "A:onT"], writes=[pon])
                S.add("act", lambda e, po=po, m=m, n=n: e.activation(k.mixbuf[:, m, :n], po[:, :n], AF.Copy),
                      reads=[pon], writes=["mixbuf"])
        postnorm_residual(k, layer, 1, c0, n)


def attn_state_update(k, h, bi, bl0, r0, ch, last_col):
    S = k.S
    cf = k.acf
    NDC, DV = cf["NDC"], cf["DV"]
    ret = (cf["nm"] == "ret")
    if r0 == 0:
        def mmk(e, bl0=bl0):
            ins = None
            for dc in range(NDC):
                ins = e.transpose(k.pst[:, 512 + dc * 128:512 + (dc + 1) * 128], k.kT[:, dc, bl0:bl0 + 128], k.ident[:])
            return ins
        S.add("pe", mmk, reads=["A:kT", "ident"], writes=["pst"])
        src = k.pst[:, 512:512 + NDC * 128].rearrange("p (c d) -> p c d", d=128)
        if ret:
            S.add("dve", lambda e, h=h, src=src: e.tensor_scalar_mul(k.ktok[:], src, k.acol[:, 4 + h:5 + h]),
                  reads=["pst", "acol"], writes=["A:ktok"])
        else:
            S.add("dve", lambda e, src=src: e.tensor_copy(k.ktok[:], src), reads=["pst"], writes=["A:ktok"])
    for dc in range(NDC):
        pd = k.ps[3] if dc == 0 else k.ps[6]
        pdn = "ps3" if dc == 0 else "ps6"
        S.add("pe", lambda e, dc=dc, pd=pd, r0=r0, ch=ch, bi=bi: e.matmul(
            pd[:, 0:DV], k.ktok[r0:r0 + ch, dc, :], k.vtok[r0:r0 + ch, bi, :], start=True, stop=True),
            reads=["A:ktok", "A:vtok"], writes=[pdn])
        if ret:
            gC = RET_G[h] ** 128
            S.add("dve", lambda e, dc=dc, pd=pd, h=h, gC=gC: e.scalar_tensor_tensor(
                k.Sst[:, h, dc, :], k.Sst[:, h, dc, :], gC, pd[:, 0:DV], ALU.mult, ALU.add),
                reads=[pdn, "A:S"], writes=["A:S"])
        else:
            S.add("dve", lambda e, dc=dc, pd=pd, h=h: e.tensor_tensor(k.Sst[:, h, dc, :], k.Sst[:, h, dc, :], pd[:, 0:DV], ALU.add),
                  reads=[pdn, "A:S"], writes=["A:S"])
            S.add("dve", lambda e, dc=dc, h=h, last_col=last_col: e.tensor_scalar_mul(
                k.Sst[:, h, dc, :], k.Sst[:, h, dc, :], k.EQ[:, last_col:last_col + 1]),
                reads=["A:S", "A:EQ"], writes=["A:S"])
        S.add("act", lambda e, dc=dc, h=h: e.activation(k.Sb[:, dc, :], k.Sst[:, h, dc, :], AF.Copy), reads=["A:S"], writes=["A:Sb"])


def attn_sample_states(k, h, bi, bl0, po2, p2n):
    S = k.S
    cf = k.acf
    NDC, DV = cf["NDC"], cf["DV"]
    ret = (cf["nm"] == "ret")

    def mmk(e):
        ins = None
        for dc in range(NDC):
            ins = e.transpose(k.pst[0:64, 512 + dc * 128:512 + (dc + 1) * 128], k.kT[:, dc, bl0:bl0 + 64], k.ident[:])
        return ins
    S.add("pe", mmk, reads=["A:kT", "ident"], writes=["pst"])
    src = k.pst[0:64, 512:512 + NDC * 128].rearrange("p (c d) -> p c d", d=128)
    if ret:
        S.add("dve", lambda e: e.tensor_scalar_mul(k.ktok[0:64], src, k.acol[0:64, 12 + h:13 + h]),
              reads=["pst", "acol"], writes=["A:ktok"])
    else:
        S.add("dve", lambda e: e.tensor_copy(k.ktok[0:64], src), reads=["pst"], writes=["A:ktok"])
    out_d = k.osts_d
    for s in range(NS):
        sl = k.s0_cnt % 2
        k.s0_cnt += 1
        km = k.km[sl]
        S.add("sp", lambda e, sl=sl, s=s: e.dma_start(out=k.S0f[sl][:], in_=k.sst_d[s, h]), writes=[("A:S0f", sl)], dma=True)
        S.add("act", lambda e, sl=sl: e.activation(k.S0b[sl][:].rearrange("p c v -> p (c v)"),
                                                   k.S0f[sl][:].rearrange("p c v -> p (c v)"), AF.Copy),
              reads=[("A:S0f", sl)], writes=[("A:S0b", sl)])

        def mmq(e, s=s, sl=sl, po2=po2):
            ins = None
            for dc in range(NDC):
                ins = e.matmul(po2[0:64, 0:DV], k.qm[:, dc, s, :], k.S0b[sl][:, dc, :],
                               start=(s == 0 and dc == 0), stop=(s == NS - 1 and dc == NDC - 1))
            return ins
        S.add("pe", mmq, reads=["A:qm", ("A:S0b", sl)], writes=[p2n])
        S.add("pool", lambda e, km=km, s=s: e.tensor_scalar_mul(km[0:64], k.ktok[0:64], k.acol[0:64, 16 + s:17 + s]),
              reads=["A:ktok", "acol"], writes=[("A:km", sl)])
        for dc in range(NDC):
            pd = k.ps[3] if dc == 0 else k.ps[6]
            pdn = "ps3" if dc == 0 else "ps6"
            S.add("pe", lambda e, dc=dc, pd=pd, km=km: e.matmul(pd[:, 0:DV], km[0:64, dc, :], k.vtok[0:64, bi, :], start=True, stop=True),
                  reads=[("A:km", sl), "A:vtok"], writes=[pdn])
            if ret:
                gC = RET_G[h] ** 4
                S.add("dve", lambda e, dc=dc, pd=pd, sl=sl, gC=gC: e.scalar_tensor_tensor(
                    k.S0f[sl][:, dc, :], k.S0f[sl][:, dc, :], gC, pd[:, 0:DV], ALU.mult, ALU.add),
                    reads=[pdn, ("A:S0f", sl)], writes=[("A:S0f", sl)])
            else:
                lc = bl0 + 4 * s + 3
                S.add("dve", lambda e, dc=dc, pd=pd, sl=sl: e.tensor_tensor(k.S0f[sl][:, dc, :], k.S0f[sl][:, dc, :], pd[:, 0:DV], ALU.add),
                      reads=[pdn, ("A:S0f", sl)], writes=[("A:S0f", sl)])
                S.add("dve", lambda e, dc=dc, sl=sl, lc=lc: e.tensor_scalar_mul(k.S0f[sl][:, dc, :], k.S0f[sl][:, dc, :], k.EQ[:, lc:lc + 1]),
                      reads=[("A:S0f", sl), "A:EQ"], writes=[("A:S0f", sl)])
        S.add("sp", lambda e, sl=sl, s=s: e.dma_start(out=out_d[s, h], in_=k.S0f[sl][:]), reads=[("A:S0f", sl)], dma=True, final=True)


def attn_finish(k, layer, kind):
    S = k.S
    for h in range(4):
        S.add("sp", lambda e, h=h: e.dma_start(out=k.ostp_d[h], in_=k.Sst[:, h]), reads=["A:S"], dma=True, final=True)


PI = float(np.pi)
I32 = mybir.dt.int32


def s5_setup(k, layer):
    S, c = k.S, k.carve
    j = layer // 3
    k.s5j = j
    k.Wm = c("Wm", [128, 8, 4, 2, 128], BF16)
    k.Cm = c("Cm", [128, 8, 4, 2, 128], BF16)
    k.tab = c("tab", [128, 3, 32, 32], F32)
    k.rr = c("rr", [128, 32], F32)
    k.sprev = c("sprev", [128, 2, 32], F32)
    k.tA = c("tA", [128, 4, 32], F32)
    k.tau = c("tau", [128, 2, 32], F32)
    off_bu = (k.arena_ptr + 31) // 32 * 32
    k.BU = c("BU", [128, 8, 4, 2, 32], F32)
    k.BU2 = c("BU2", [128, 8, 4, 2, 32], F32)
    k.BUs = [k.BU, k.BU2]
    k.Hb = c("Hb", [128, 8, 4, 2, 32], BF16)
    k.ZR = [c("ZR%d" % i, [128, 32, 32], F32) for i in range(2)]
    k.ZI = [c("ZI%d" % i, [128, 32, 32], F32) for i in range(2)]
    k.yg = c("yg", [128, NCH, GT], BF16)
    k.yt = [k.rstd[:, 0:256].rearrange("p (c t) -> p c t", t=32), k.rstd[:, 256:512].rearrange("p (c t) -> p c t", t=32)]
    k.ysg = k.rstd
    off_yt = k.arena_ptr
    k.aC = c("aC", [128, 2, 32], F32)
    k.scd = k.mixbuf[:, 0, :].rearrange("p (a b) -> p a b", a=2)
    k.scp = k.mixbuf[:, 1, :].rearrange("p (a b) -> p a b", a=2)
    k.h0 = c("h0", [128, NS, 2, 32], F32)
    k.osS = k.h0
    k.osp = c("osp", [128, 2, 32], F32)
    k.d5 = c("d5", [128, 8], F32)
    k.bd = c("bd", [128, 8], F32)
    k.dtq = c("dtq", [128, 8], F32)
    k.lamC = c("lamC", [128, 2, 32], F32)
    k.wgl = [c("wgl%d" % i, [128, 2, 8, 128], BF16) for i in range(2)]
    k.wgl_cnt = 0
    k.sg = k.ysg
    end_ptr = k.arena_ptr
    k.arena_ptr = off_bu
    lamQ = c("lamQ", [128, 2, 512], F32)
    bQ = c("bQ", [128, 2, 512], F32)
    bbQ = c("bbQ", [128, 2, 512], F32)
    aQ = c("aQ", [128, 3, 512], F32)
    off_tq = (k.arena_ptr + 31) // 32 * 32
    tq = [c("tq%d" % i, [128, 512], F32) for i in range(5)]
    tqi = c("tqi", [128, 512], I32)
    tc_ = [c("tc%d" % i, [128, 32], F32) for i in range(5)]
    tci = c("tci", [128, 32], I32)
    aC3 = c("aC3", [128, 3, 32], F32)
    k.cC = c("cC", [128, 8, 4, 2, 16], F32)
    assert k.arena_ptr <= off_yt, (k.arena_ptr, off_yt)
    k.arena_ptr = off_tq
    tabt = c("tabt", [128, 1024], F32)
    tabi = c("tabi", [128, 1024], I32)
    k.arena_ptr = end_ptr
    P = "A:prep"
    S.add("sp", lambda e: e.dma_start(out=lamQ[:], in_=k.s5lamQ_d[j]), writes=[P], dma=True)
    S.add("sp", lambda e: e.dma_start(out=bQ[:], in_=k.s5bQ_d[j]), writes=[P], dma=True)
    S.add("sp", lambda e: e.dma_start(out=k.lamC[:], in_=k.s5lamC_d[j]), writes=[P], dma=True)
    S.add("sp", lambda e: e.dma_start(out=k.cC[:], in_=k.s5cC_d[j]), writes=[P], dma=True)
    S.add("sp", lambda e: e.dma_start(out=k.dtq[:], in_=k.s5dt_d[j]), writes=[P], dma=True)
    S.add("sp", lambda e: e.dma_start(out=k.d5[:], in_=k.s5d_d[j]), writes=["A:d5"], dma=True)
    S.add("sp", lambda e: e.dma_start(out=k.bd[:], in_=k.s5bd_d), writes=[P], dma=True)
    S.add("sp", lambda e: e.dma_start(out=k.tau[:], in_=k.s5tau_d), writes=[P], dma=True)
    S.add("sp", lambda e: e.dma_start(out=k.h0[:], in_=k.s5h0_d[j]), writes=["A:h0"], dma=True)
    S.add("act", lambda e: e.activation(k.dtq[:], k.dtq[:], AF.Exp), reads=[P], writes=[P])

    def cexp(lr, li, X, out3, tmp, tmpi):
        def v(t):
            return t.rearrange("p (c x) -> p c x", x=X)
        dtb = k.dtq[:, :].unsqueeze(2).broadcast_to([128, 8, X])
        zr, zi, w, w2, em = tmp
        ops = []
        A = lambda eng, fn: S.add(eng, fn, reads=[P], writes=[P])
        A("dve", lambda e: e.tensor_tensor(v(zr[:]), v(lr), dtb, ALU.mult))
        A("dve", lambda e: e.tensor_tensor(v(zi[:]), v(li), dtb, ALU.mult))
        A("dve", lambda e: e.tensor_scalar(em[:], zr[:], 0.25, 1.0, ALU.mult, ALU.add))
        A("dve", lambda e: e.tensor_tensor(em[:], em[:], zr[:], ALU.mult))
        A("dve", lambda e: e.tensor_scalar(em[:], em[:], 1.0 / 3.0, 1.0, ALU.mult, ALU.add))
        A("dve", lambda e: e.tensor_tensor(em[:], em[:], zr[:], ALU.mult))
        A("dve", lambda e: e.tensor_scalar(em[:], em[:], 0.5, 1.0, ALU.mult, ALU.add))
        A("dve", lambda e: e.tensor_tensor(em[:], em[:], zr[:], ALU.mult))
        A("act", lambda e: e.activation(zr[:], zr[:], AF.Exp))

        def sin_of(dst, src, scale):
            A("dve", lambda e: e.tensor_scalar_mul(dst[:], src[:], scale / (2.0 * PI)))
            A("dve", lambda e: e.tensor_copy(tmpi[:], dst[:]))
            A("dve", lambda e: e.tensor_copy(dst[:], tmpi[:]))
            A("dve", lambda e: e.tensor_scalar_mul(dst[:], dst[:], -2.0 * PI))
            A("dve", lambda e: e.scalar_tensor_tensor(dst[:], src[:], scale, dst[:], ALU.mult, ALU.add))
            A("dve", lambda e: e.tensor_scalar(dst[:], dst[:], -PI, PI, ALU.max, ALU.min))
            A("act", lambda e: e.activation(dst[:], dst[:], AF.Sin))
        sin_of(w, zi, 1.0)
        sin_of(w2, zi, 0.5)
        A("dve", lambda e: e.tensor_tensor(out3[:, 1, :], zr[:], w[:], ALU.mult))
        A("dve", lambda e: e.tensor_tensor(w2[:], w2[:], w2[:], ALU.mult))
        A("dve", lambda e: e.tensor_tensor(w2[:], w2[:], zr[:], ALU.mult))
        A("dve", lambda e: e.scalar_tensor_tensor(out3[:, 2, :], w2[:], -2.0, em[:], ALU.mult, ALU.add))
        A("dve", lambda e: e.tensor_scalar_add(out3[:, 0, :], out3[:, 2, :], 1.0))
    cexp(lamQ[:, 0, :], lamQ[:, 1, :], 64, aQ, tq, tqi)
    cexp(k.lamC[:, 0, :], k.lamC[:, 1, :], 4, aC3, tc_, tci)
    A = lambda eng, fn: S.add(eng, fn, reads=[P], writes=[P])
    S.add("dve", lambda e: e.tensor_copy(k.aC[:], aC3[:, 0:2, :]), reads=[P], writes=["A:aC"])
    mag, theta = tc_[0], tc_[1]
    S.add("dve", lambda e: e.tensor_copy(k.rr[:], mag[:]), reads=[P], writes=["A:tab"])
    th_b = theta[:, :].unsqueeze(2).broadcast_to([128, 32, 32])
    tau_b = k.tau[:, 0, :].unsqueeze(1).broadcast_to([128, 32, 32])
    msk_b = k.tau[:, 1, :].unsqueeze(1).broadcast_to([128, 32, 32])
    mag_b = mag[:, :].unsqueeze(2).broadcast_to([128, 32, 32])
    ang = k.tab[:, 2].rearrange("p a b -> p (a b)")
    S.add("dve", lambda e: e.tensor_tensor(k.tab[:, 2], th_b, tau_b, ALU.mult), reads=[P], writes=["A:tab"])

    def sin_tab(dst, shift):
        A2 = lambda eng, fn: S.add(eng, fn, reads=[P, "A:tab"], writes=[P, "A:tab"])
        A2("dve", lambda e: e.tensor_scalar(tabt[:], ang, 1.0 / (2.0 * PI), shift / (2.0 * PI), ALU.mult, ALU.add))
        A2("dve", lambda e: e.tensor_copy(tabi[:], tabt[:]))
        A2("dve", lambda e: e.tensor_copy(tabt[:], tabi[:]))
        A2("dve", lambda e: e.tensor_scalar(tabt[:], tabt[:], -2.0 * PI, shift, ALU.mult, ALU.add))
        A2("dve", lambda e: e.tensor_tensor(tabt[:], tabt[:], ang, ALU.add))
        A2("dve", lambda e: e.tensor_scalar(tabt[:], tabt[:], -PI, PI, ALU.max, ALU.min))
        A2("act", lambda e: e.activation(dst, tabt[:], AF.Sin))
    sin_tab(k.tab[:, 1].rearrange("p a b -> p (a b)"), 0.0)
    sin_tab(k.tab[:, 0].rearrange("p a b -> p (a b)"), PI / 2.0)
    S.add("dve", lambda e: e.tensor_tensor(k.tab[:, 2], mag_b, msk_b, ALU.mult), reads=[P, "A:tab"], writes=["A:tab"])
    S.add("dve", lambda e: e.memset(k.sprev[:], 0.0), writes=["A:sprev"])
    lr, li = lamQ[:, 0, :], lamQ[:, 1, :]
    ar1, ai = aQ[:, 2, :], aQ[:, 1, :]
    t0, t1, t2, t3, t4 = tq
    A("dve", lambda e: e.tensor_tensor(t0[:], ar1, lr, ALU.mult))
    A("dve", lambda e: e.tensor_tensor(t1[:], ai, li, ALU.mult))
    A("dve", lambda e: e.tensor_tensor(t0[:], t0[:], t1[:], ALU.add))
    A("dve", lambda e: e.tensor_tensor(t1[:], ai, lr, ALU.mult))
    A("dve", lambda e: e.tensor_tensor(t2[:], ar1, li, ALU.mult))
    A("dve", lambda e: e.tensor_tensor(t1[:], t1[:], t2[:], ALU.subtract))
    A("dve", lambda e: e.tensor_tensor(t2[:], lr, lr, ALU.mult))
    A("dve", lambda e: e.tensor_tensor(t3[:], li, li, ALU.mult))
    A("dve", lambda e: e.tensor_tensor(t2[:], t2[:], t3[:], ALU.add))
    A("dve", lambda e: e.reciprocal(t2[:], t2[:]))
    A("dve", lambda e: e.tensor_tensor(t0[:], t0[:], t2[:], ALU.mult))
    A("dve", lambda e: e.tensor_tensor(t1[:], t1[:], t2[:], ALU.mult))
    A("dve", lambda e: e.tensor_tensor(t2[:], t0[:], bQ[:, 0, :], ALU.mult))
    A("dve", lambda e: e.tensor_tensor(t3[:], t1[:], bQ[:, 1, :], ALU.mult))
    A("dve", lambda e: e.tensor_tensor(bbQ[:, 0, :], t2[:], t3[:], ALU.subtract))
    A("dve", lambda e: e.tensor_tensor(t2[:], t0[:], bQ[:, 1, :], ALU.mult))
    A("dve", lambda e: e.tensor_tensor(t3[:], t1[:], bQ[:, 0, :], ALU.mult))
    A("dve", lambda e: e.tensor_tensor(bbQ[:, 1, :], t2[:], t3[:], ALU.add))
    bdb = k.bd[:, :].unsqueeze(1).unsqueeze(3)
    for cc in range(8):
        for ri in range(2):
            src = bbQ[:, ri, cc * 64:(cc + 1) * 64].rearrange("p (m x) -> p m x", x=16).unsqueeze(2).broadcast_to([128, 4, 8, 16])
            dst = k.Wm[:, cc, :, ri, :].rearrange("p m (g x) -> p m g x", x=16)
            S.add("dve", lambda e, src=src, dst=dst: e.tensor_tensor(dst, src, bdb.broadcast_to([128, 4, 8, 16]), ALU.mult),
                  reads=[P], writes=["A:Wm"])
    S.add("dve", lambda e: e.tensor_scalar_mul(k.cC[:, :, :, 1, :], k.cC[:, :, :, 1, :], -1.0), reads=[P], writes=[P])
    for cc in range(8):
        src = k.cC[:, cc].rearrange("p m r x -> p (m r) x").unsqueeze(2).broadcast_to([128, 8, 8, 16])
        dst = k.Cm[:, cc].rearrange("p m r (g x) -> p (m r) g x", x=16)
        S.add("dve", lambda e, src=src, dst=dst: e.tensor_tensor(dst, src, bdb.broadcast_to([128, 8, 8, 16]), ALU.mult),
              reads=[P], writes=["A:Cm"])
    k.barrier()


def s5_group(k, layer, gi, g, tiles, mid=None):
    S = k.S
    j = k.s5j
    BS = 32
    blocks = [(b * BS, False, 0) for b in range(g["pn"] // BS)]
    if g["samp"]:
        blocks += [(g["pn"], True, 0), (g["pn"] + BS, True, 8)]
    BUv = k.BU[:].rearrange("p c m r t -> p (c m) r t")
    Hbv = k.Hb[:].rearrange("p c m r t -> p (c m) r t")
    ar, ai = k.aC[:, 0, :], k.aC[:, 1, :]
    cosT, sinT, RT = k.tab[:, 0], k.tab[:, 1], k.tab[:, 2]
    mb = [k.mixbuf[:, 2 * i:2 * i + 2, :].rearrange("p a (b c) -> p (a b) c", c=32) for i in range(4)]
    if not hasattr(k, "s5blk"):
        k.s5blk = 0
    zbs = []
    for _ in blocks:
        zbs.append(k.s5blk % 2)
        k.s5blk += 1

    def stageA(bi):
        (bl0, samp, s0) = blocks[bi]
        zb = zbs[bi]
        ZR, ZI = k.ZR[zb], k.ZI[zb]
        zrn, zin = ("A:ZR", zb), ("A:ZI", zb)
        BUb = k.BUs[bi % 2]
        bun = ("A:BU", bi % 2)
        BUv = BUb[:].rearrange("p c m r t -> p (c m) r t")
        BUr, BUi = BUv[:, :, 0, :], BUv[:, :, 1, :]
        for cc in range(8):
            pb = k.ps[cc % 4]
            pbn = "ps%d" % (cc % 4)

            def mmb(e, cc=cc, pb=pb, bl0=bl0):
                ins = None
                for m in range(4):
                    for ri in range(2):
                        o = (m * 2 + ri) * BS
                        ins = e.matmul(pb[:, o:o + BS], k.Wm[:, cc, m, ri, :], k.h[:, cc, bl0:bl0 + BS], start=True, stop=True)
                return ins
            S.add("pe", mmb, reads=["A:Wm", "h"], writes=[pbn])
            S.add("act", lambda e, cc=cc, pb=pb: e.activation(BUb[:, cc].rearrange("p m r t -> p (m r t)"), pb[:, 0:8 * BS], AF.Copy),
                  reads=[pbn], writes=[bun])
        if not samp:
            t1, t2, t3, t4 = mb
            S.add("pool", lambda e: e.tensor_tensor(t1, BUr, cosT, ALU.mult), reads=[bun, "A:tab", "mixbuf"], writes=[("mbt", 0)])
            S.add("pool", lambda e: e.tensor_tensor(t2, BUi, sinT, ALU.mult), reads=[bun, "A:tab", "mixbuf"], writes=[("mbt", 1)])
            S.add("pool", lambda e, ZR=ZR: e.tensor_tensor(ZR[:], t1, t2, ALU.add), reads=[("mbt", 0), ("mbt", 1)], writes=[zrn])
            S.add("pool", lambda e: e.tensor_tensor(t1, BUi, cosT, ALU.mult), reads=[bun, "A:tab", "mixbuf"], writes=[("mbt", 0)])
            S.add("pool", lambda e: e.tensor_tensor(t2, BUr, sinT, ALU.mult), reads=[bun, "A:tab", "mixbuf"], writes=[("mbt", 1)])
            S.add("pool", lambda e, ZI=ZI: e.tensor_tensor(ZI[:], t1, t2, ALU.subtract), reads=[("mbt", 0), ("mbt", 1)], writes=[zin])


    def stageB(bi):
        (bl0, samp, s0) = blocks[bi]
        zb = zbs[bi]
        ZR, ZI = k.ZR[zb], k.ZI[zb]
        zrn, zin = ("A:ZR", zb), ("A:ZI", zb)
        BUb = k.BUs[bi % 2]
        bun = ("A:BU", bi % 2)
        BUv = BUb[:].rearrange("p c m r t -> p (c m) r t")
        BUr, BUi = BUv[:, :, 0, :], BUv[:, :, 1, :]
        t1, t2, t3, t4 = mb
        if samp:
            H8 = 8
            for t in range(LS):
                def prev(ri, t=t):
                    if t == 0:
                        return k.h0[:, s0:s0 + H8, ri, :].rearrange("p s x -> p x s")
                    return (ZR if ri == 0 else ZI)[:, :, t - 1:BS:4]
                arb = ar.unsqueeze(2).broadcast_to([128, 32, H8])
                aib = ai.unsqueeze(2).broadcast_to([128, 32, H8])
                T = lambda buf, i: buf[:, i, :].rearrange("p (x s) -> p x s", s=8)
                rd = ["A:h0", "A:aC", zrn, zin, "mixbuf"]
                S.add("dve", lambda e, p=prev(0), arb=arb: e.tensor_tensor(T(k.scd, 0), arb, p, ALU.mult), reads=rd, writes=["A:scd"])
                S.add("dve", lambda e, p=prev(1), aib=aib: e.tensor_tensor(T(k.scd, 1), aib, p, ALU.mult), reads=rd, writes=["A:scd"])
                S.add("dve", lambda e: e.tensor_tensor(T(k.scd, 0), T(k.scd, 0), T(k.scd, 1), ALU.subtract), reads=["A:scd"], writes=["A:scd"])
                S.add("pool", lambda e, p=prev(0), aib=aib: e.tensor_tensor(T(k.scp, 0), aib, p, ALU.mult), reads=rd, writes=["A:scp"])
                S.add("pool", lambda e, p=prev(1), arb=arb: e.tensor_tensor(T(k.scp, 1), arb, p, ALU.mult), reads=rd, writes=["A:scp"])
                S.add("pool", lambda e: e.tensor_tensor(T(k.scp, 0), T(k.scp, 0), T(k.scp, 1), ALU.add), reads=["A:scp"], writes=["A:scp"])
                S.add("dve", lambda e, o=ZR[:, :, t:BS:4], b=BUr[:, :, t:BS:4]: e.tensor_tensor(o, T(k.scd, 0), b, ALU.add),
                      reads=["A:scd", bun], writes=[zrn])
                S.add("pool", lambda e, o=ZI[:, :, t:BS:4], b=BUi[:, :, t:BS:4]: e.tensor_tensor(o, T(k.scp, 0), b, ALU.add),
                      reads=["A:scp", bun], writes=[zin])
            S.add("dve", lambda e, ZR=ZR, s0=s0: e.tensor_copy(k.h0[:, s0:s0 + H8, 0, :].rearrange("p s x -> p x s"), ZR[:, :, 3:BS:4]),
                  reads=[zrn], writes=["A:h0"])
            S.add("dve", lambda e, ZI=ZI, s0=s0: e.tensor_copy(k.h0[:, s0:s0 + H8, 1, :].rearrange("p s x -> p x s"), ZI[:, :, 3:BS:4]),
                  reads=[zin], writes=["A:h0"])
            S.add("act", lambda e, ZR=ZR: e.activation(Hbv[:, :, 0, :], ZR[:], AF.Copy), reads=[zrn], writes=["A:Hb"])
            S.add("act", lambda e, ZI=ZI: e.activation(Hbv[:, :, 1, :], ZI[:], AF.Copy), reads=[zin], writes=["A:Hb"])
        else:
            S.add("dve", lambda e: e.tensor_tensor(k.tA[:, 0:2, :], k.sprev[:], k.rr[:, :].unsqueeze(1).broadcast_to([128, 2, 32]), ALU.mult),
                  reads=["A:sprev", "A:tab"], writes=["A:tA"])
            S.add("dve", lambda e, ZR=ZR: e.tensor_tensor(ZR[:, :, 0], ZR[:, :, 0], k.tA[:, 0, :], ALU.add), reads=["A:tA", zrn], writes=[zrn])
            S.add("dve", lambda e, ZI=ZI: e.tensor_tensor(ZI[:, :, 0], ZI[:, :, 0], k.tA[:, 1, :], ALU.add), reads=["A:tA", zin], writes=[zin])
            Rf = RT.rearrange("p a b -> p (a b)")
            S.add("dve", lambda e, ZR=ZR: e.tensor_tensor_scan(ZR[:].rearrange("p a b -> p (a b)"), Rf, ZR[:].rearrange("p a b -> p (a b)"), 0.0, ALU.mult, ALU.add),
                  reads=[zrn, "A:tab"], writes=[zrn])
            S.add("dve", lambda e, ZI=ZI: e.tensor_tensor_scan(ZI[:].rearrange("p a b -> p (a b)"), Rf, ZI[:].rearrange("p a b -> p (a b)"), 0.0, ALU.mult, ALU.add),
                  reads=[zin, "A:tab"], writes=[zin])
            cl, sl_ = cosT[:, :, BS - 1], sinT[:, :, BS - 1]
            S.add("dve", lambda e, ZR=ZR: e.tensor_tensor(k.tA[:, 0, :], ZR[:, :, BS - 1], cl, ALU.mult), reads=[zrn, "A:tab"], writes=["A:tA"])
            S.add("dve", lambda e, ZI=ZI: e.tensor_tensor(k.tA[:, 1, :], ZI[:, :, BS - 1], sl_, ALU.mult), reads=[zin, "A:tab"], writes=["A:tA"])
            S.add("dve", lambda e, ZR=ZR: e.tensor_tensor(k.tA[:, 2, :], ZR[:, :, BS - 1], sl_, ALU.mult), reads=[zrn, "A:tab"], writes=["A:tA"])
            S.add("dve", lambda e, ZI=ZI: e.tensor_tensor(k.tA[:, 3, :], ZI[:, :, BS - 1], cl, ALU.mult), reads=[zin, "A:tab"], writes=["A:tA"])
            S.add("dve", lambda e: e.tensor_tensor(k.sprev[:, 0, :], k.tA[:, 0, :], k.tA[:, 1, :], ALU.subtract), reads=["A:tA"], writes=["A:sprev"])
            S.add("dve", lambda e: e.tensor_tensor(k.sprev[:, 1, :], k.tA[:, 2, :], k.tA[:, 3, :], ALU.add), reads=["A:tA"], writes=["A:sprev"])
            S.add("dve", lambda e, ZR=ZR: e.tensor_tensor(t3, ZR[:], cosT, ALU.mult), reads=[zrn, "A:tab", "mixbuf"], writes=[("mbt", 2)])
            S.add("dve", lambda e, ZI=ZI: e.tensor_tensor(t4, ZI[:], sinT, ALU.mult), reads=[zin, "A:tab", "mixbuf"], writes=[("mbt", 3)])
            S.add("dve", lambda e: e.tensor_tensor(Hbv[:, :, 0, :], t3, t4, ALU.subtract), reads=[("mbt", 2), ("mbt", 3)], writes=["A:Hb"])
            S.add("dve", lambda e, ZR=ZR: e.tensor_tensor(t3, ZR[:], sinT, ALU.mult), reads=[zrn, "A:tab", "mixbuf"], writes=[("mbt", 2)])
            S.add("dve", lambda e, ZI=ZI: e.tensor_tensor(t4, ZI[:], cosT, ALU.mult), reads=[zin, "A:tab", "mixbuf"], writes=[("mbt", 3)])
            S.add("dve", lambda e: e.tensor_tensor(Hbv[:, :, 1, :], t3, t4, ALU.add), reads=[("mbt", 2), ("mbt", 3)], writes=["A:Hb"])

    def stageC(bi):
        (bl0, samp, s0) = blocks[bi]
        def mmy(e):
            ins = None
            for cc in range(8):
                for m in range(4):
                    for ri in range(2):
                        ins = e.matmul(k.ps[4][:, cc * BS:(cc + 1) * BS], k.Cm[:, cc, m, ri, :], k.Hb[:, cc, m, ri, :],
                                       start=(m == 0 and ri == 0), stop=(m == 3 and ri == 1))
            return ins
        S.add("pe", mmy, reads=["A:Cm", "A:Hb"], writes=["ps4"])
        y0, y1 = k.yt
        d5b = k.d5[:, :].unsqueeze(2).broadcast_to([128, 8, BS])
        S.add("pool", lambda e, bl0=bl0: e.tensor_tensor(y0, k.h[:, :, bl0:bl0 + BS], d5b, ALU.mult), reads=["h", "A:d5"], writes=["rstd"])
        S.add("dve", lambda e: e.tensor_tensor(y0, y0, k.ps[4][:, 0:8 * BS].rearrange("p (c t) -> p c t", t=BS), ALU.add),
              reads=["rstd", "ps4"], writes=["rstd"])
        S.add("act", lambda e: e.activation(y1, y0, AF.Square, scale=0.21145921592590845), reads=["rstd"], writes=["rstd"])
        S.add("pool", lambda e: e.tensor_scalar_add(y1, y1, 1.0), reads=["rstd"], writes=["rstd"])
        S.add("pool", lambda e: e.tensor_tensor(y1, y1, y0, ALU.mult), reads=["rstd"], writes=["rstd"])
        S.add("act", lambda e: e.activation(y1, y1, AF.Sigmoid, scale=1.5957691216057308), reads=["rstd"], writes=["rstd"])
        S.add("pool", lambda e, bl0=bl0: e.tensor_tensor(k.yg[:, :, bl0:bl0 + BS], y1, y0, ALU.mult), reads=["rstd", "rstd"], writes=["A:yg"])

    stageA(0)
    for bi in range(len(blocks)):
        if bi + 1 < len(blocks):
            stageA(bi + 1)
        stageB(bi)
        stageC(bi)
    if mid is not None:
        mid()
    for (c0, l0, n) in tiles:
        for m in range(NCH):
            sl = k.wgl_cnt % 2
            k.wgl_cnt += 1
            W = k.wgl[sl]
            wh = ("A:wgl", sl)
            S.add("pool", lambda e, W=W, m=m: e.dma_start(out=W[:].rearrange("p a c j -> p (a c j)"), in_=k.wglu_d[j, m]), writes=[wh], dma=True)
            pa, pb = k.ps[m % 2], k.ps[2 + m % 2]
            pan, pbn = "ps%d" % (m % 2), "ps%d" % (2 + m % 2)
            for (a, pp, ppn) in ((0, pa, pan), (1, pb, pbn)):
                def mmg(e, W=W, a=a, pp=pp, l0=l0, n=n):
                    ins = None
                    for c in range(NCH):
                        ins = e.matmul(pp[:, :n], W[:, a, c, :], k.yg[:, c, l0:l0 + n], start=(c == 0), stop=(c == NCH - 1))
                    return ins
                S.add("pe", mmg, reads=[wh, "A:yg"], writes=[ppn])
            sgv = k.rstd[:]
            S.add("act", lambda e, pb=pb, n=n, sgv=sgv: e.activation(sgv[:, :n], pb[:, :n], AF.Sigmoid), reads=[pbn], writes=["rstd"])
            S.add("dve", lambda e, pa=pa, m=m, n=n, sgv=sgv: e.tensor_tensor(k.mixbuf[:, m, :n], pa[:, :n], sgv[:, :n], ALU.mult),
                  reads=[pan, "rstd"], writes=["mixbuf"])
        postnorm_residual(k, layer, 1, c0, n)


def s5_finish(k, layer):
    S = k.S
    j = k.s5j
    S.add("sp", lambda e: e.dma_start(out=k.os5p_d[j], in_=k.sprev[:]), reads=["A:sprev"], dma=True, final=True)
    S.add("sp", lambda e: e.dma_start(out=k.os5s_d[j], in_=k.osS[:]), reads=["A:h0"], dma=True, final=True)
```

```python
from contextlib import ExitStack
import numpy as np
import concourse.bass as bass
import concourse.mybir as mybir
from concourse.bass_utils import run_bass_kernel_spmd

F32 = mybir.dt.float32
BF16 = mybir.dt.bfloat16
ALU = mybir.AluOpType
AF = mybir.ActivationFunctionType

NCORES = 8
D = 1024
NCH = 8
LP = 2048
NS = 16
LS = 4
TS = NS * LS
T = LP + TS
DFF = 2816
NF = 22
DEPTH = 4
EPS = 1e-6
SBUF_TOP = 229344
GROUPS = [dict(p0=512 * i, pn=512, samp=(i == 3)) for i in range(4)]
GT = 576


class Sched:
    def __init__(self):
        self.ops = []

    redirect = None

    def flush(self, lst):
        assert self.redirect is None
        for a in lst:
            self.add(*a)

    def add(self, eng, fn, reads=(), writes=(), dma=False, n=1, final=False):
        if self.redirect is not None:
            self.redirect.append((eng, fn, reads, writes, dma, n, final))
            return

        def isar(h):
            h0 = h[0] if isinstance(h, tuple) else h
            return isinstance(h0, str) and h0.startswith("A:")
        reads = tuple(reads)
        writes = tuple(writes)
        if any(isar(h) for h in reads + writes):
            reads = reads + ("arena",)
        self.ops.append(dict(eng=eng, fn=fn, r=reads, w=writes, dma=dma, n=n, final=final))

    def emit(self, nc, ctx):
        ops = self.ops
        NPOOL = 8
        csem = {e: ctx.enter_context(nc.semaphore("s_" + e)) for e in ("pe", "act", "dve", "pool")}
        ccnt = {e: 0 for e in csem}
        dsem = {}
        for q in ("sp", "pool", "act"):
            dsem[q] = [ctx.enter_context(nc.semaphore("d_%s%d" % (q, i))) for i in range(NPOOL)]
        dval = {q: [0] * NPOOL for q in dsem}
        dlast = {q: [None] * NPOOL for q in dsem}
        dnext = {q: 0 for q in dsem}
        lw, rd = {}, {}
        done = [None] * len(ops)
        per_eng = {e: [] for e in ("pe", "act", "dve", "pool", "sp")}
        deps_of = [None] * len(ops)
        slot_of = [None] * len(ops)
        waited_idx = {e: {} for e in per_eng}
        targets_of = [None] * len(ops)
        needed = set()
        for i, op in enumerate(ops):
            deps = set()
            for h in op["r"]:
                if h in lw:
                    deps.add(lw[h])
            for h in op["w"]:
                if h in lw:
                    deps.add(lw[h])
                deps.update(rd.get(h, ()))
            for h in op["r"]:
                rd.setdefault(h, []).append(i)
            for h in op["w"]:
                lw[h] = i
                rd[h] = []
            e = op["eng"]
            if op["dma"]:
                kk = dnext[e]
                dnext[e] = (kk + 1) % NPOOL
                if dlast[e][kk] is not None:
                    deps.add(dlast[e][kk])
                dlast[e][kk] = i
                slot_of[i] = kk
            latest = {}
            for d in deps:
                od = ops[d]
                key = ("dma", od["eng"], slot_of[d]) if od["dma"] else ("cmp", od["eng"])
                if d > latest.get(key, -1):
                    latest[key] = d
            tg = []
            for key, d in sorted(latest.items(), key=lambda kv: kv[1]):
                if d > waited_idx[e].get(key, -1):
                    waited_idx[e][key] = d
                    tg.append(d)
                    needed.add(d)
            targets_of[i] = tg
            if op.get("final"):
                needed.add(i)
        for i, op in enumerate(ops):
            e = op["eng"]
            if op["dma"]:
                n = op.get("n", 1)
                dval[e][slot_of[i]] += 16 * n
                done[i] = (dsem[e][slot_of[i]], dval[e][slot_of[i]])
            elif i in needed:
                ccnt[e] += 1
                done[i] = (csem[e], ccnt[e])
        for i, op in enumerate(ops):
            e = op["eng"]
            waits = [done[d] for d in targets_of[i]]
            if op["dma"]:
                per_eng[e].append((waits, op, ("dma", slot_of[i], i)))
            else:
                per_eng[e].append((waits, op, ("cmp", i in needed, i)))
        self.n_signals = dict(ccnt)
        self.stats = {e: len(v) for e, v in per_eng.items()}

        def run(eng_name, eobj):
            for waits, op, (kind, slot, i) in per_eng[eng_name]:
                for s, v in waits:
                    eobj.wait_ge(s, v)
                if kind == "cmp":
                    ins = op["fn"](eobj)
                    if slot:
                        ins.then_inc(csem[eng_name], 1)
                else:
                    lst = op["fn"](eobj)
                    if not isinstance(lst, (list, tuple)):
                        lst = [lst]
                    assert len(lst) == op.get("n", 1)
                    for ins in lst:
                        ins.then_inc(dsem[eng_name][slot], 16)

        final_waits = {}
        for i, op in enumerate(ops):
            if op.get("final"):
                s, v = done[i]
                final_waits[id(s)] = (s, max(v, final_waits.get(id(s), (s, 0))[1]))

        with nc.Block() as block:
            @block.sync
            def _(e):
                run("sp", e)
                for s, v in final_waits.values():
                    e.wait_ge(s, v)

            @block.tensor
            def _(e):
                run("pe", e)

            @block.scalar
            def _(e):
                run("act", e)

            @block.vector
            def _(e):
                run("dve", e)

            @block.gpsimd
            def _(e):
                run("pool", e)


class K:
    pass


def build_program(flags):
    nc = bass.Bass("TRN2", target_bir_lowering=False)
    ctx = ExitStack()
    S = Sched()
    k = K()
    k.nc, k.S, k.ctx, k.flags = nc, S, ctx, flags

    def din(name, shape, dt=F32):
        return nc.dram_tensor(name, list(shape), dt, kind="ExternalInput").ap()

    def dout(name, shape, dt=F32):
        return nc.dram_tensor(name, list(shape), dt, kind="ExternalOutput").ap()

    def sb(name, shape, dt):
        return ctx.enter_context(nc.sbuf_tensor(name, list(shape), dt))

    k.din, k.dout, k.sb = din, dout, sb
    k.arena_cache = {}

    def carve_begin(phase):
        k.phase = phase
        k.arena_ptr = k.arena_base

    def carve(name, shape, dt):
        key = (k.phase, name)
        if key in k.arena_cache:
            return k.arena_cache[key]
        esz = 2 if dt == BF16 else 4
        nbytes = int(np.prod(shape[1:])) * esz
        off = (k.arena_ptr + 31) // 32 * 32
        k.arena_max = max(getattr(k, "arena_max", 0), off + nbytes)
        if off + nbytes > SBUF_TOP:
            print("ARENA OVERFLOW", k.phase, name, off + nbytes - SBUF_TOP)
            k.arena_over = True
        t = nc.alloc_sbuf_tensor_at("%s_%s" % (k.phase, name), list(shape), dt, offset=off)
        k.arena_ptr = off + nbytes
        k.arena_cache[key] = t
        return t

    def barrier():
        S.add("dve", lambda e: e.memset(k.bar[:], 0.0), writes=["arena"])
    k.carve_begin, k.carve, k.barrier = carve_begin, carve, barrier
    k.xT = din("xT", [128, NCH, T])
    k.yT = dout("yT", [128, NCH, T])
    k.ng_d = din("ng", [128, DEPTH * 4 * NCH])
    k.wgu_d = din("wgu", [DEPTH, NF, 128, 2048])
    k.wd_d = din("wd", [DEPTH, NF, 128, 1024])
    k.cw_d = din("cw", [128, DEPTH * NF * 4])
    k.cconv_d = din("A:cconv", [128, DEPTH * NF * NS * 2])
    k.oconv_d = dout("A:oconv", [128, DEPTH * NF * (NS + 1) * 2])

    k.ident_d = din("ident", [128, 128])
    k.cmask_d = din("cmask", [128, 10, 128])
    k.acol_d = din("acol", [128, 32])
    k.cos_d = din("cosT", [128, T])
    k.sin_d = din("sinT", [128, T])
    k.rmask_d = din("rmask", [128, GT])
    k.wret_d = din("wret", [4, 3, 128, 4096])
    k.wo_ret_d = din("wo_ret", [4, 128, 4096])
    k.wgqk_d = din("wgqk", [4, 128, 2048])
    k.wgvg_d = din("wgvg", [4, 128, 4096])
    k.wo_gla_d = din("wo_gla", [4, 128, 2048])
    k.wa1_d = din("wa1", [128, 8, 16])
    k.wa2_d = din("wa2", [16, 512])
    k.ba_d = din("ba", [128, 4])
    k.ngl_d = din("ngl", [128, 256])
    k.sret_d = din("sret", [NS, 4, 128, 2, 512])
    k.sgla_d = din("sgla", [NS, 4, 128, 1, 256])
    k.oret_p_d = dout("oret_p", [4, 128, 2, 512])
    k.oret_s_d = dout("oret_s", [NS, 4, 128, 2, 512])
    k.ogla_p_d = dout("ogla_p", [4, 128, 1, 256])
    k.ogla_s_d = dout("ogla_s", [NS, 4, 128, 1, 256])
    k.s5lamQ_d = din("s5lamQ", [2, 128, 2, 512])
    k.s5bQ_d = din("s5bQ", [2, 128, 2, 512])
    k.s5lamC_d = din("s5lamC", [2, 128, 2, 32])
    k.s5cC_d = din("s5cC", [2, 128, 8, 4, 2, 16])
    k.s5dt_d = din("s5dt", [2, 128, 8])
    k.s5d_d = din("s5d", [2, 128, 8])
    k.s5bd_d = din("s5bd", [128, 8])
    k.s5tau_d = din("s5tau", [128, 2, 32])
    k.s5h0_d = din("s5h0", [2, 128, NS, 2, 32])
    k.wglu_d = din("wglu", [2, 8, 128, 2048])
    k.os5p_d = dout("os5p", [2, 128, 2, 32])
    k.os5s_d = dout("os5s", [2, 128, NS, 2, 32])
    k.ident = sb("idents", [128, 128], BF16)
    k.acol = sb("acols", [128, 32], F32)
    k.x = sb("x", [128, NCH, T], F32)
    k.ng = sb("ngs", [128, DEPTH * 4 * NCH], F32)
    k.cw = sb("cws", [128, DEPTH * NF * 4], F32)
    k.ones = sb("ones", [128, 128], BF16)
    k.h = sb("h", [128, NCH, GT], BF16)
    k.sq = sb("sq", [128, NCH, 512], BF16)
    k.mixbuf = sb("mixbuf", [128, NCH, 512], F32)
    k.rstd = sb("rstd", [128, 512], F32)
    k.bar = sb("bar", [128, 8], F32)
    k.ps = [ctx.enter_context(nc.psum_tensor("ps%d" % i, [128, 512], F32)) for i in range(7)]
    k.pst = ctx.enter_context(nc.psum_tensor("ps7", [128, 1024], BF16))
    k.arena_base = SBUF_TOP - nc.sbuf_bytes_remaining + 64

    for c in range(NCH):
        S.add("sp", lambda e, c=c: e.dma_start(out=k.x[:, c, :], in_=k.xT[:, c, :]), writes=["x"], dma=True)
    S.add("sp", lambda e: e.dma_start(out=k.ng[:], in_=k.ng_d), writes=["ng"], dma=True)
    S.add("sp", lambda e: e.dma_start(out=k.cw[:], in_=k.cw_d), writes=["cw"], dma=True)
    S.add("dve", lambda e: e.memset(k.ones[:], 1.0), writes=["ones"])
    S.add("pool", lambda e: e.dma_start(out=k.ident[:], in_=k.ident_d), writes=["ident"], dma=True)
    S.add("sp", lambda e: e.dma_start(out=k.acol[:], in_=k.acol_d), writes=["acol"], dma=True)

    for layer in range(DEPTH):
        kind = layer % 3
        mix_on = flags.get(("s5", "ret", "gla")[kind], False)
        if mix_on:
            carve_begin("mix%d" % kind)
            mixer_setup(k, layer, kind)
            tl = [col_tiles(g) for g in GROUPS]
            prenorm(k, layer, 0, GROUPS[0], tl[0])
            for gi, g in enumerate(GROUPS):
                nxt = None
                if gi + 1 < len(GROUPS):
                    nxt = (lambda gi=gi, layer=layer, tl=tl: prenorm(k, layer, 0, GROUPS[gi + 1], tl[gi + 1]))
                mixer_group(k, layer, kind, gi, g, tl[gi], nxt)
            mixer_finish(k, layer, kind)
            barrier()
        carve_begin("ffn")
        ffn_setup(k)
        CL = NF * NS * 2
        S.add("sp", lambda e, layer=layer: e.dma_start(out=k.cconv[:], in_=k.cconv_d[:, layer * CL:(layer + 1) * CL]),
              writes=["A:cconv"], dma=True)
        tl = [col_tiles(g) for g in GROUPS]
        prenorm(k, layer, 2, GROUPS[0], tl[0])
        for gi, g in enumerate(GROUPS):
            nxt = None
            if gi + 1 < len(GROUPS):
                nxt = (lambda gi=gi, layer=layer, tl=tl: prenorm(k, layer, 2, GROUPS[gi + 1], tl[gi + 1]))
            ffn(k, layer, gi, g, tl[gi], nxt)
        OL = NF * (NS + 1) * 2
        S.add("sp", lambda e, layer=layer: e.dma_start(out=k.oconv_d[:, layer * OL:(layer + 1) * OL], in_=k.oconv[:]),
              reads=["A:oconv"], dma=True, final=True)
        barrier()
    for c in range(NCH):
        S.add("sp", lambda e, c=c: e.dma_start(out=k.yT[:, c, :], in_=k.x[:, c, :]), reads=["x"], dma=True, final=True)
    print("arena base", k.arena_base, "max", k.arena_max, "top", SBUF_TOP, "over", k.arena_max - SBUF_TOP)
    assert not getattr(k, "arena_over", False)
    S.emit(nc, ctx)
    ctx.close()
    return nc, S


def col_tiles(g):
    tl = []
    for i in range(g["pn"] // 512):
        tl.append((g["p0"] + 512 * i, 512 * i, 512))
    if g["samp"]:
        tl.append((LP, g["pn"], TS))
    return tl


def ngcol(k, layer, i, c):
    j = (layer * 4 + i) * NCH + c
    return k.ng[:, j:j + 1]


def rstd_from_sq(k, n, tagr):
    S = k.S
    st = k.ps[6]

    def mm(e):
        ins = None
        for c in range(NCH):
            ins = e.matmul(st[:, :n], k.ones[:], k.sq[:, c, :n], start=(c == 0), stop=(c == NCH - 1))
        return ins
    S.add("pe", mm, reads=["sq", "ones"], writes=["ps6"])
    S.add("dve", lambda e: e.tensor_scalar(k.rstd[:, :n], st[:, :n], 1.0 / D, EPS, ALU.mult, ALU.add),
          reads=["ps6"], writes=["rstd"])
    S.add("act", lambda e: e.activation(k.rstd[:, :n], k.rstd[:, :n], AF.Ln), reads=["rstd"], writes=["rstd"])
    S.add("act", lambda e: e.activation(k.rstd[:, :n], k.rstd[:, :n], AF.Exp, scale=-0.5), reads=["rstd"], writes=["rstd"])


def prenorm(k, layer, gidx, g, tiles):
    S = k.S
    for (c0, l0, n) in tiles:
        for c in range(NCH):
            S.add("act", lambda e, c=c, c0=c0, n=n: e.activation(k.sq[:, c, :n], k.x[:, c, c0:c0 + n], AF.Square),
                  reads=["x"], writes=["sq"])
        rstd_from_sq(k, n, "pre")
        for c in range(NCH):
            S.add("dve", lambda e, c=c, c0=c0, l0=l0, n=n: e.scalar_tensor_tensor(
                k.h[:, c, l0:l0 + n], k.x[:, c, c0:c0 + n], ngcol(k, layer, gidx, c), k.rstd[:, :n],
                ALU.mult, ALU.mult), reads=["x", "rstd", "ng"], writes=["h"])


def postnorm_residual(k, layer, gidx, c0, n):
    S = k.S
    for c in range(NCH):
        S.add("act", lambda e, c=c: e.activation(k.sq[:, c, :n], k.mixbuf[:, c, :n], AF.Square),
              reads=["mixbuf"], writes=["sq"])
    rstd_from_sq(k, n, "post")
    for c in range(NCH):
        S.add("dve", lambda e, c=c: e.scalar_tensor_tensor(
            k.mixbuf[:, c, :n], k.mixbuf[:, c, :n], ngcol(k, layer, gidx, c), k.rstd[:, :n],
            ALU.mult, ALU.mult), reads=["mixbuf", "rstd", "ng"], writes=["mixbuf"])
        S.add("dve", lambda e, c=c: e.tensor_add(k.x[:, c, c0:c0 + n], k.x[:, c, c0:c0 + n], k.mixbuf[:, c, :n]),
              reads=["mixbuf", "x"], writes=["x"])


def ffn_setup(k):
    c = k.carve
    k.wgu = [c("wgu%d" % i, [128, 2, NCH, 128], BF16) for i in range(3)]
    k.wd = c("wds", [128, NF, 1024], BF16)
    k.hid = c("A:hid", [128, NF, GT], BF16)
    k.G = [c("G%d" % i, [128, 2 + 512], F32) for i in range(2)]
    k.Gs = [c("Gs%d" % i, [128, NS, 6], F32) for i in range(2)]
    k.cv = [c("cv%d" % i, [128, 512], F32) for i in range(2)]
    k.carry = c("A:carry", [128, NF, 2], F32)
    k.cconv = c("cconvs", [128, NF * NS * 2], F32)
    k.oconv = c("oconvs", [128, NF * (NS + 1) * 2], F32)
    if not hasattr(k, "ffn_cnt"):
        k.ffn_cnt = 0


def ffn(k, layer, gi, g, tiles, mid=None):
    S, nc = k.S, k.nc
    pn = g["pn"]
    st1 = [[] for _ in range(NF)]
    st2 = [[] for _ in range(NF)]
    for f in range(NF):
        add1 = lambda *a, f=f, **kw: st1[f].append((a, kw))
        add2 = lambda *a, f=f, **kw: st2[f].append((a, kw))
        slot = k.ffn_cnt % 3
        gs = k.ffn_cnt % 2
        k.ffn_cnt += 1
        W = k.wgu[slot]
        add1("pool", lambda e, f=f, W=W: e.dma_start(
            out=W[:].rearrange("p a c j -> p (a c j)"), in_=k.wgu_d[layer, f]),
            writes=[("A:wgu", slot)], dma=True)
        if gi == 0:
            add1("pool", lambda e, f=f: e.dma_start(out=k.wd[:, f, :], in_=k.wd_d[layer, f]),
                 writes=[("A:wd", f)], dma=True)
        G, Gs, = k.G[gs], k.Gs[gs]
        cwb = (layer * NF + f) * 4
        w0, w1, w2, bb = (k.cw[:, cwb + j:cwb + j + 1] for j in range(4))
        if gi == 0:
            add1("pool", lambda e, G=G: e.memset(G[:, 0:2], 0.0), writes=[("A:G", gs)])
        else:
            add1("pool", lambda e, G=G, f=f: e.tensor_copy(G[:, 0:2], k.carry[:, f, :]),
                  reads=["A:carry"], writes=[("A:G", gs)])
        if g["samp"]:
            cb = f * NS * 2
            add1("pool", lambda e, Gs=Gs, cb=cb: e.tensor_copy(
                Gs[:, :, 0:2], k.cconv[:, cb:cb + NS * 2].rearrange("p (s r) -> p s r", r=2)),
                reads=["A:cconv"], writes=[("A:Gs", gs)])
        for ti, (c0, l0, n) in enumerate(tiles):
            par = (f + ti) % 2
            pa = k.ps[par]
            pb = k.ps[2 + par]
            pan, pbn = "ps%d" % par, "ps%d" % (2 + par)

            def mmg(e, W=W, pa=pa, a=0, l0=l0, n=n):
                ins = None
                for c in range(NCH):
                    ins = e.matmul(pa[:, :n], W[:, a, c, :], k.h[:, c, l0:l0 + n], start=(c == 0), stop=(c == NCH - 1))
                return ins
            add1("pe", mmg, reads=[("A:wgu", slot), "h"], writes=[pan])
            add1("pe", lambda e, W=W, pb=pb, l0=l0, n=n, mmg=mmg: mmg(e, W, pb, 1, l0, n), reads=[("A:wgu", slot), "h"], writes=[pbn])
            samp_tile = (c0 >= LP)
            cv = k.cv[par]
            cvn = ("A:cv", par)
            if not samp_tile:
                add1("act", lambda e, G=G, pa=pa, l0=l0, n=n: e.activation(G[:, 2 + l0:2 + l0 + n], pa[:, :n], AF.Copy),
                      reads=[pan], writes=[("A:G", gs)])
                src = [G[:, l0 + j:l0 + j + n] for j in range(3)]
                dst = cv[:, :n]
            else:
                add1("act", lambda e, Gs=Gs, pa=pa, n=n: e.activation(
                    Gs[:, :, 2:6], pa[:, :n].rearrange("p (s t) -> p s t", t=LS), AF.Copy),
                    reads=[pan], writes=[("A:Gs", gs)])
                src = [Gs[:, :, j:j + LS] for j in range(3)]
                dst = cv[:, :n].rearrange("p (s t) -> p s t", t=LS)
            Gn = ("A:Gs", gs) if samp_tile else ("A:G", gs)
            add2("dve", lambda e, dst=dst, s0=src[0], w0=w0, bb=bb: e.tensor_scalar(dst, s0, w0, bb, ALU.mult, ALU.add),
                  reads=[Gn, "cw"], writes=[cvn])
            add2("dve", lambda e, dst=dst, s1=src[1], w1=w1: e.scalar_tensor_tensor(dst, s1, w1, dst, ALU.mult, ALU.add),
                  reads=[Gn, "cw", cvn], writes=[cvn])
            add2("dve", lambda e, dst=dst, s2=src[2], w2=w2: e.scalar_tensor_tensor(dst, s2, w2, dst, ALU.mult, ALU.add),
                  reads=[Gn, "cw", cvn], writes=[cvn])
            add2("act", lambda e, cv=cv, n=n: e.activation(cv[:, :n], cv[:, :n], AF.Silu), reads=[cvn], writes=[cvn])
            add2("dve", lambda e, cv=cv, pb=pb, f=f, l0=l0, n=n: e.tensor_tensor(k.hid[:, f, l0:l0 + n], cv[:, :n], pb[:, :n], ALU.mult),
                  reads=[cvn, pbn], writes=["A:hid"])
        ob = f * (NS + 1) * 2
        if gi < len(GROUPS) - 1:
            add2("pool", lambda e, G=G, f=f: e.tensor_copy(k.carry[:, f, :], G[:, pn:pn + 2]),
                  reads=[("A:G", gs)], writes=["A:carry"])
        else:
            add2("pool", lambda e, G=G, ob=ob: e.tensor_copy(k.oconv[:, ob:ob + 2], G[:, pn:pn + 2]),
                  reads=[("A:G", gs)], writes=["A:oconv"])
        if g["samp"]:
            add2("pool", lambda e, Gs=Gs, ob=ob: e.tensor_copy(
                k.oconv[:, ob + 2:ob + 2 + NS * 2].rearrange("p (s r) -> p s r", r=2), Gs[:, :, 4:6]),
                reads=[("A:Gs", gs)], writes=["A:oconv"])
    def flush(lst):
        for (a, kw) in lst:
            S.add(*a, **kw)
    if len(tiles) == 1:
        flush(st1[0])
        for f in range(NF):
            if f + 1 < NF:
                flush(st1[f + 1])
            flush(st2[f])
    else:
        for f in range(NF):
            flush(st1[f])
            flush(st2[f])
    if mid is not None:
        mid()
    for ti, (c0, l0, n) in enumerate(tiles):
        for m in range(NCH):
            po = k.ps[4 + m % 2]
            pon = "ps%d" % (4 + m % 2)

            def mmd(e, po=po, m=m, l0=l0, n=n):
                ins = None
                for f in range(NF):
                    ins = e.matmul(po[:, :n], k.wd[:, f, m * 128:(m + 1) * 128], k.hid[:, f, l0:l0 + n],
                                   start=(f == 0), stop=(f == NF - 1))
                return ins
            S.add("pe", mmd, reads=[("A:wd", f) for f in range(NF)] + ["A:hid"], writes=[pon])
            S.add("act", lambda e, po=po, m=m, n=n: e.activation(k.mixbuf[:, m, :n], po[:, :n], AF.Copy),
                  reads=[pon], writes=["mixbuf"])
        postnorm_residual(k, layer, 3, c0, n)


def _feat_major(a):
    lead = a.shape[:-1]
    nch = a.shape[-1] // 128
    a = a.reshape(-1, nch, 128)
    a = np.transpose(a, (2, 1, 0))
    return np.ascontiguousarray(a.reshape((128, nch) + lead))


def host_inputs(inp):
    f = np.float32
    shared = {}
    ng = np.asarray(inp["norm_g"], f)
    shared["ng"] = np.ascontiguousarray(
        ng.reshape(DEPTH, 4, NCH, 128).transpose(3, 0, 1, 2).reshape(128, -1))
    wg = np.asarray(inp["ffn_w_gate"], f).reshape(DEPTH, NCH, 128, NF, 128)
    wu = np.asarray(inp["ffn_w_up"], f).reshape(DEPTH, NCH, 128, NF, 128)
    wgu = np.stack([wg, wu], axis=0)
    shared["wgu"] = np.ascontiguousarray(wgu.transpose(1, 4, 3, 0, 2, 5).reshape(DEPTH, NF, 128, 2048))
    shared["wd"] = np.ascontiguousarray(np.asarray(inp["ffn_w_down"], f).reshape(DEPTH, NF, 128, 1024))
    cwt = np.concatenate([np.asarray(inp["ffn_conv_w"], f), np.asarray(inp["ffn_conv_b"], f)[:, None, :]], axis=1)
    shared["cw"] = np.ascontiguousarray(
        cwt.reshape(DEPTH, 4, NF, 128).transpose(3, 0, 2, 1).reshape(128, -1))
    shared.update(host_consts())
    wq = np.asarray(inp["ret_wq"], f)[0].reshape(NCH, 128, 4, 256)
    wk = np.asarray(inp["ret_wk"], f)[0].reshape(NCH, 128, 4, 256)
    wv = np.asarray(inp["ret_wv"], f)[0].reshape(NCH, 128, 4, 512)
    wgr = np.asarray(inp["ret_wg"], f)[0].reshape(NCH, 128, 4, 512)
    wret = np.zeros((4, 3, 128, NCH, 512), f)
    wret[:, 0, :, :, 0:256] = wq.transpose(2, 1, 0, 3)
    wret[:, 0, :, :, 256:512] = wk.transpose(2, 1, 0, 3)
    wret[:, 1] = wv.transpose(2, 1, 0, 3)
    wret[:, 2] = wgr.transpose(2, 1, 0, 3)
    shared["wret"] = wret.reshape(4, 3, 128, 4096)
    wo = np.asarray(inp["ret_wo"], f)[0].reshape(16, 128, 4, 2, 128)
    shared["wo_ret"] = np.ascontiguousarray(wo.transpose(2, 1, 3, 0, 4).reshape(4, 128, 4096))
    gq = np.asarray(inp["gla_wq"], f)[0].reshape(NCH, 128, 4, 128)
    gk = np.asarray(inp["gla_wk"], f)[0].reshape(NCH, 128, 4, 128)
    gv = np.asarray(inp["gla_wv"], f)[0].reshape(NCH, 128, 4, 256)
    gg = np.asarray(inp["gla_wg"], f)[0].reshape(NCH, 128, 4, 256)
    wgqk = np.zeros((4, 128, NCH, 256), f)
    wgqk[..., 0:128] = gq.transpose(2, 1, 0, 3)
    wgqk[..., 128:256] = gk.transpose(2, 1, 0, 3)
    shared["wgqk"] = wgqk.reshape(4, 128, 2048)
    wgvg = np.zeros((4, 128, NCH, 512), f)
    wgvg[..., 0:256] = gv.transpose(2, 1, 0, 3)
    wgvg[..., 256:512] = gg.transpose(2, 1, 0, 3)
    shared["wgvg"] = wgvg.reshape(4, 128, 4096)
    go = np.asarray(inp["gla_wo"], f)[0].reshape(8, 128, 4, 2, 128)
    shared["wo_gla"] = np.ascontiguousarray(go.transpose(2, 1, 3, 0, 4).reshape(4, 128, 2048))
    shared["wa1"] = np.ascontiguousarray(np.asarray(inp["gla_wa1"], f)[0].reshape(NCH, 128, 16).transpose(1, 0, 2))
    shared["wa2"] = np.ascontiguousarray(np.asarray(inp["gla_wa2"], f)[0])
    shared["ba"] = np.ascontiguousarray(np.asarray(inp["gla_ba"], f)[0].reshape(4, 128).T)
    shared["ngl"] = np.ascontiguousarray(np.broadcast_to(np.asarray(inp["gla_norm_g"], f)[0][None, :], (128, 256)))
    lam = np.stack([np.asarray(inp["s5_lambda_re"], f), np.asarray(inp["s5_lambda_im"], f)], axis=1)
    lq = lam.reshape(2, 2, 8, 8, 1, 64)
    lq = np.broadcast_to(lq, (2, 2, 8, 8, 16, 64)).transpose(0, 3, 4, 1, 2, 5)
    shared["s5lamQ"] = np.ascontiguousarray(lq.reshape(2, 128, 2, 512))
    lc = lam.reshape(2, 2, 8, 8, 4, 16).transpose(0, 3, 5, 1, 2, 4)
    shared["s5lamC"] = np.ascontiguousarray(lc.reshape(2, 128, 2, 32))
    bb = np.stack([np.asarray(inp["s5_b_re"], f), np.asarray(inp["s5_b_im"], f)], axis=1)
    bq = bb.reshape(2, 2, 8, 8, 64, 16).transpose(0, 3, 5, 1, 2, 4)
    shared["s5bQ"] = np.ascontiguousarray(bq.reshape(2, 128, 2, 512))
    cc_ = np.stack([np.asarray(inp["s5_c_re"], f), np.asarray(inp["s5_c_im"], f)], axis=1)
    cq = cc_.reshape(2, 2, 8, 8, 16, 4, 16).transpose(0, 3, 6, 2, 5, 1, 4)
    shared["s5cC"] = np.ascontiguousarray(cq.reshape(2, 128, 8, 4, 2, 16))
    ldt = np.asarray(inp["s5_log_dt"], f).reshape(2, 8, 8, 1)
    shared["s5dt"] = np.ascontiguousarray(np.broadcast_to(ldt, (2, 8, 8, 16)).transpose(0, 2, 3, 1).reshape(2, 128, 8))
    shared["s5d"] = np.ascontiguousarray(np.asarray(inp["s5_d"], f).reshape(2, 8, 128).transpose(0, 2, 1))
    shared["s5bd"] = (np.arange(128)[:, None] // 16 == np.arange(8)[None, :]).astype(f)
    tau = np.zeros((128, 2, 32), f)
    tau[:, 0, :] = np.arange(1, 33)
    tau[:, 1, 1:] = 1.0
    shared["s5tau"] = tau
    wgl = np.asarray(inp["s5_w_glu"], f).reshape(2, 8, 128, 2, 8, 128)
    shared["wglu"] = np.ascontiguousarray(wgl.transpose(0, 4, 2, 3, 1, 5).reshape(2, 8, 128, 2048))
    s5st = np.stack([np.asarray(inp["state_s5_re"], f), np.asarray(inp["state_s5_im"], f)], axis=2)
    sret_all = np.asarray(inp["state_ret"], f)[0]
    sgla_all = np.asarray(inp["state_gla"], f)[0]
    per = []
    xp = np.asarray(inp["x_prompt"], f)
    xs = np.asarray(inp["x_sample"], f)
    cc = np.asarray(inp["cache_ffn_conv"], f)
    for c in range(NCORES):
        d = {}
        xa = np.concatenate([xp[c], xs[c * NS:(c + 1) * NS].reshape(TS, D)], axis=0)
        d["xT"] = np.ascontiguousarray(xa.reshape(T, NCH, 128).transpose(2, 1, 0))
        ccc = cc[:, c * NS:(c + 1) * NS]
        d["A:cconv"] = np.ascontiguousarray(
            ccc.reshape(DEPTH, NS, 2, NF, 128).transpose(4, 0, 3, 1, 2).reshape(128, -1))
        h0 = s5st[:, c * NS:(c + 1) * NS].reshape(2, NS, 2, 8, 8, 4, 16)
        d["s5h0"] = np.ascontiguousarray(h0.transpose(0, 4, 6, 1, 2, 3, 5).reshape(2, 128, NS, 2, 32))
        sr = sret_all[c * NS:(c + 1) * NS].reshape(NS, 4, 2, 128, 512)
        d["sret"] = np.ascontiguousarray(sr.transpose(0, 1, 3, 2, 4))
        d["sgla"] = np.ascontiguousarray(sgla_all[c * NS:(c + 1) * NS].reshape(NS, 4, 128, 1, 256))
        per.append(d)
    return shared, per


def host_consts():
    f = np.float32
    c = {}
    c["ident"] = np.eye(128, dtype=f)
    cm = np.zeros((128, 10, 128), np.float64)
    jj = np.arange(128)[:, None]
    ii = np.arange(128)[None, :]
    for h in range(4):
        gam = RET_G[h]
        cm[:, h, :] = np.where(ii >= jj, gam ** np.maximum(ii - jj, 0), 0.0)
        m = np.where((ii >= jj) & (ii // 4 == jj // 4) & (ii < 64) & (jj < 64), gam ** np.maximum(ii - jj, 0), 0.0)
        cm[:, 4 + h, :] = m
    cm[:, 8, :] = ((ii >= jj) & (ii // 64 == jj // 64))
    cm[:, 9, :] = ((ii >= jj) & (ii // 4 == jj // 4) & (ii < 64) & (jj < 64))
    c["cmask"] = cm.astype(f)
    ac = np.zeros((128, 32), np.float64)
    p = np.arange(128)
    for h in range(4):
        gam = RET_G[h]
        ac[:, h] = gam ** (p + 1.0)
        ac[:, 4 + h] = gam ** (127.0 - p)
        ac[:, 8 + h] = gam ** ((p % 4) + 1.0)
        ac[:, 12 + h] = gam ** (3.0 - (p % 4))
    for s_ in range(NS):
        ac[:, 16 + s_] = ((p // 4) == s_) & (p < 64)
    c["acol"] = ac.astype(f)
    freq = (f(1.0) / (f(10000.0) ** np.linspace(0.0, 1.0, 128, dtype=f))).astype(f)
    pos = np.concatenate([np.arange(LP), np.tile(16384 + np.arange(LS), NS)]).astype(f)
    ang = (pos[:, None] * freq[None, :]).astype(f).astype(np.float64)
    c["cosT"] = np.ascontiguousarray(np.cos(ang).T.astype(f))
    c["sinT"] = np.ascontiguousarray(np.sin(ang).T.astype(f))
    rm = np.ones((GT,), f)
    rm[0:512:64] = 0.0
    rm[512:576:4] = 0.0
    c["rmask"] = np.ascontiguousarray(np.broadcast_to(rm[None, :], (128, GT)))
    return c


def host_outputs(res):
    f = np.float32
    y_p = np.zeros((NCORES, LP, D), f)
    y_s = np.zeros((NCORES * NS, LS, D), f)
    conv_p = np.zeros((DEPTH, NCORES, 2, DFF), f)
    conv_s = np.zeros((DEPTH, NCORES * NS, 2, DFF), f)
    for c in range(NCORES):
        r = res[c]
        yT = np.asarray(r["yT"]).reshape(128, NCH, T)
        ya = yT.transpose(2, 1, 0).reshape(T, D)
        y_p[c] = ya[:LP]
        y_s[c * NS:(c + 1) * NS] = ya[LP:].reshape(NS, LS, D)
        oc = np.asarray(r["A:oconv"]).reshape(128, DEPTH, NF, NS + 1, 2)
        oc = oc.transpose(1, 3, 4, 2, 0).reshape(DEPTH, NS + 1, 2, DFF)
        conv_p[:, c] = oc[:, 0]
        conv_s[:, c * NS:(c + 1) * NS] = oc[:, 1:]
    ret_p = np.zeros((1, NCORES, 4, 256, 512), f)
    ret_s = np.zeros((1, NCORES * NS, 4, 256, 512), f)
    gla_p = np.zeros((1, NCORES, 4, 128, 256), f)
    gla_s = np.zeros((1, NCORES * NS, 4, 128, 256), f)
    for c in range(NCORES):
        r = res[c]
        ret_p[0, c] = np.asarray(r["oret_p"]).reshape(4, 128, 2, 512).transpose(0, 2, 1, 3).reshape(4, 256, 512)
        ret_s[0, c * NS:(c + 1) * NS] = np.asarray(r["oret_s"]).reshape(NS, 4, 128, 2, 512).transpose(0, 1, 3, 2, 4).reshape(NS, 4, 256, 512)
        gla_p[0, c] = np.asarray(r["ogla_p"]).reshape(4, 128, 256)
        gla_s[0, c * NS:(c + 1) * NS] = np.asarray(r["ogla_s"]).reshape(NS, 4, 128, 256)
    s5_p = np.zeros((2, 2, NCORES, 64, 64), f)
    s5_s = np.zeros((2, 2, NCORES * NS, 64, 64), f)
    for c in range(NCORES):
        r = res[c]
        a = np.asarray(r["os5p"]).reshape(2, 8, 16, 2, 8, 4)
        s5_p[:, :, c] = a.transpose(3, 0, 4, 1, 5, 2).reshape(2, 2, 64, 64)
        a = np.asarray(r["os5s"]).reshape(2, 8, 16, NS, 2, 8, 4)
        s5_s[:, :, c * NS:(c + 1) * NS] = a.transpose(4, 0, 3, 5, 1, 6, 2).reshape(2, 2, NS, 64, 64)
    return dict(s5re_p=s5_p[0], s5im_p=s5_p[1], s5re_s=s5_s[0], s5im_s=s5_s[1],
                y_p=y_p, y_s=y_s, conv_p=conv_p, conv_s=conv_s, ret_p=ret_p, ret_s=ret_s, gla_p=gla_p, gla_s=gla_s)


_CACHE = {}


def run(inputs, flags):
    key = tuple(sorted(flags.items()))
    if key not in _CACHE:
        _CACHE[key] = build_program(flags)
    nc, S = _CACHE[key]
    shared, per = host_inputs(inputs)
    in_maps = []
    for c in range(NCORES):
        m = dict(shared)
        m.update(per[c])
        in_maps.append(m)
    res = run_bass_kernel_spmd(nc, in_maps, core_ids=list(range(NCORES)))
    return host_outputs(res.results)


def kernel(**inputs):
    o = run(inputs, dict(s5=True, ret=True, gla=True))
    return (o["y_p"], o["y_s"], o["s5re_p"], o["s5im_p"], o["ret_p"], o["gla_p"], o["conv_p"],
            o["s5re_s"], o["s5im_s"], o["ret_s"], o["gla_s"], o["conv_s"])


RET_G = [1.0 - 2.0 ** (-5.0 - h) for h in range(4)]


def mixer_setup(k, layer, kind):
    if kind == 0:
        s5_setup(k, layer)
    else:
        attn_setup(k, layer, kind)


def mixer_group(k, layer, kind, gi, g, tiles, mid=None):
    if kind == 0:
        s5_group(k, layer, gi, g, tiles, mid)
    else:
        attn_group(k.attn_ns[kind], layer, kind, gi, g, tiles, mid)


def mixer_finish(k, layer, kind):
    if kind == 0:
        s5_finish(k, layer)
    else:
        attn_finish(k.attn_ns[kind], layer, kind)


def attn_cfg(kind):
    if kind == 1:
        return dict(NDC=2, DV=512, CH=128, NKC=16, nm="ret")
    return dict(NDC=1, DV=256, CH=64, NKC=8, nm="gla")


def attn_setup(k0, layer, kind):
    k = K()
    k.__dict__.update(k0.__dict__)
    if not hasattr(k0, "attn_ns"):
        k0.attn_ns = {}
    k0.attn_ns[kind] = k
    S, c = k.S, k.carve
    cf = attn_cfg(kind)
    NDC, DV = cf["NDC"], cf["DV"]
    k.acf = cf
    if kind == 1:
        k.sst_d, k.osts_d, k.ostp_d = k.sret_d, k.oret_s_d, k.oret_p_d
    else:
        k.sst_d, k.osts_d, k.ostp_d = k.sgla_d, k.ogla_s_d, k.ogla_p_d
    k.aw = [c("w%d" % i, [128, 8 * 512], BF16) for i in range(2)]
    k.aw_cnt = 0
    k.qT = c("qT", [128, NDC, GT], BF16)
    k.kT = c("kT", [128, NDC, GT], BF16)
    k.vtok = c("vtok", [128, 5, DV], BF16)
    k.gtok = c("gtok", [128, 5, DV], BF16)
    k.onT = c("onT", [128, cf["NKC"], GT], BF16)
    k.PT = [c("PT%d" % i, [128, 128], BF16) for i in range(2)]
    k.to = c("to", [128, DV], F32)
    k.tos = [k.to, c("to1", [128, DV], F32)] if kind == 2 else [k.to, k.to]
    k.on = c("on", [128, DV], BF16)
    k.og = c("og", [128, DV], BF16)
    k.ktok = c("ktok", [128, NDC, 128], BF16)
    k.Sst = c("Sst", [128, 4, NDC, DV], F32)
    k.Sb = c("Sb", [128, NDC, DV], BF16)
    k.S0b = [c("S0b%d" % i, [128, NDC, DV], BF16) for i in range(2)]
    k.S0f = [c("S0f%d" % i, [128, NDC, DV], F32) for i in range(2)]
    k.s0_cnt = 0
    k.qm = c("qm", [128, NDC, NS, 64], BF16)
    k.km = [c("km%d" % i, [128, NDC, 128], BF16) for i in range(2)]
    k.amask = c("amask", [128, 5, 128], BF16)
    k.st = c("st", [128, 8], F32)
    S.add("pool", lambda e: e.memset(k.Sst[:], 0.0), writes=["A:S"])
    S.add("pool", lambda e: e.memset(k.qm[:], 0.0), writes=["A:qm"])
    if kind == 1:
        k.rt = [c("rt%d" % i, [128, 512], F32) for i in range(2)]
        k.cs = c("cs", [128, 2, GT], F32)
        k.amaskS = c("amaskS", [128, 4, 64], BF16)
        S.add("pool", lambda e: e.dma_start(out=k.amask[:, 0:4, :], in_=k.cmask_d[:, 0:4, :]), writes=["A:amask"], dma=True)
        S.add("pool", lambda e: e.dma_start(out=k.amaskS[0:64, :, :], in_=k.cmask_d[0:64, 4:8, 0:64]), writes=["A:amask"], dma=True)
    else:
        k.amaskS = c("amaskS", [128, 1, 64], BF16)
        S.add("pool", lambda e: e.dma_start(out=k.amask[:, 0:1, :], in_=k.cmask_d[:, 8:9, :]), writes=["A:amask"], dma=True)
        S.add("pool", lambda e: e.dma_start(out=k.amaskS[0:64, :, :], in_=k.cmask_d[0:64, 9:10, 0:64]), writes=["A:amask"], dma=True)
        k.la = c("la", [128, GT], F32)
        k.EQ = c("EQ", [128, GT], F32)
        k.EK = c("EK", [128, GT], F32)
        k.rT = c("rT", [16, GT], BF16)
        k.wa1 = c("wa1", [128, 8, 16], BF16)
        k.wa2 = c("wa2", [16, 512], BF16)
        k.nba = c("nba", [128, 4], F32)
        k.ngl = c("ngl", [128, 256], F32)
        k.rmask = c("rmask", [128, GT], F32)
        S.add("pool", lambda e: e.dma_start(out=k.wa1[:], in_=k.wa1_d), writes=["A:wa"], dma=True)
        S.add("pool", lambda e: e.dma_start(out=k.wa2[:], in_=k.wa2_d), writes=["A:wa"], dma=True)
        S.add("sp", lambda e: e.dma_start(out=k.nba[:], in_=k.ba_d), writes=["A:nba"], dma=True)
        S.add("sp", lambda e: e.dma_start(out=k.ngl[:], in_=k.ngl_d), writes=["A:ngl"], dma=True)
        S.add("sp", lambda e: e.dma_start(out=k.rmask[:], in_=k.rmask_d), writes=["A:rmask"], dma=True)
        S.add("dve", lambda e: e.tensor_scalar_mul(k.nba[:], k.nba[:], -1.0), reads=["A:nba"], writes=["A:nba"])


def aw_load(k, src_ap, ncols):
    slot = k.aw_cnt % 2
    k.aw_cnt += 1
    W = k.aw[slot]
    hn = ("A:w", slot)
    k.S.add("pool", lambda e: e.dma_start(out=W[:, 0:ncols], in_=src_ap), writes=[hn], dma=True)
    return W, hn


def attn_group(k, layer, kind, gi, g, tiles, mid=None):
    S = k.S
    cf = k.acf
    NDC, DV, CH, NKC = cf["NDC"], cf["DV"], cf["CH"], cf["NKC"]
    NVC = DV // 128
    ret = (kind == 1)
    blocks = [(b * 128, 128) for b in range(g["pn"] // 128)]
    if g["samp"]:
        blocks.append((g["pn"], TS))
    if ret:
        for (c0, l0, n) in tiles:
            S.add("sp", lambda e, c0=c0, l0=l0, n=n: e.dma_start(out=k.cs[:, 0, l0:l0 + n], in_=k.cos_d[:, c0:c0 + n]),
                  writes=["A:cs"], dma=True)
            S.add("sp", lambda e, c0=c0, l0=l0, n=n: e.dma_start(out=k.cs[:, 1, l0:l0 + n], in_=k.sin_d[:, c0:c0 + n]),
                  writes=["A:cs"], dma=True)
    else:
        for (c0, l0, n) in tiles:
            def mmr(e, l0=l0, n=n):
                ins = None
                for c in range(NCH):
                    ins = e.matmul(k.ps[0][0:16, :n], k.wa1[:, c, :], k.h[:, c, l0:l0 + n], start=(c == 0), stop=(c == NCH - 1))
                return ins
            S.add("pe", mmr, reads=["A:wa", "h"], writes=["ps0"])
            S.add("act", lambda e, l0=l0, n=n: e.activation(k.rT[:, l0:l0 + n], k.ps[0][0:16, :n], AF.Copy),
                  reads=["ps0"], writes=["A:rT"])
    for h in range(4):
        if ret:
            W, wh = aw_load(k, k.wret_d[h, 0], 4096)
            Wv = W[:, 0:4096].rearrange("p (c j) -> p c j", j=512)
            for (c0, l0, n) in tiles:
                for part in range(4):
                    def mmp(e, part=part, l0=l0, n=n, Wv=Wv):
                        ins = None
                        for c in range(NCH):
                            ins = e.matmul(k.ps[part][:, :n], Wv[:, c, part * 128:(part + 1) * 128], k.h[:, c, l0:l0 + n],
                                           start=(c == 0), stop=(c == NCH - 1))
                        return ins
                    S.add("pe", mmp, reads=[wh, "h"], writes=["ps%d" % part])
                for (pa, pb, dst, dn, sc) in ((0, 1, k.qT, "A:qT", 1.0), (2, 3, k.kT, "A:kT", 1.0 / 16.0)):
                    A, B = k.ps[pa], k.ps[pb]
                    cosv, sinv = k.cs[:, 0, l0:l0 + n], k.cs[:, 1, l0:l0 + n]
                    t0, t1 = k.rt[0], k.rt[1]
                    for (x0, x1, tr0, tr1, dc, op) in ((A, B, cosv, sinv, 0, ALU.subtract), (B, A, cosv, sinv, 1, ALU.add)):
                        S.add("dve", lambda e, x0=x0, tr0=tr0, n=n, sc=sc, t0=t0: e.scalar_tensor_tensor(
                            t0[:, :n], x0[:, :n], sc, tr0, ALU.mult, ALU.mult),
                            reads=["ps%d" % pa, "ps%d" % pb, "A:cs"], writes=["A:rt0"])
                        S.add("dve", lambda e, x1=x1, tr1=tr1, n=n, sc=sc, t1=t1: e.scalar_tensor_tensor(
                            t1[:, :n], x1[:, :n], sc, tr1, ALU.mult, ALU.mult),
                            reads=["ps%d" % pa, "ps%d" % pb, "A:cs"], writes=["A:rt1"])
                        S.add("pool", lambda e, dst=dst, dc=dc, l0=l0, n=n, t0=t0, t1=t1, op=op: e.tensor_tensor(
                            dst[:, dc, l0:l0 + n], t0[:, :n], t1[:, :n], op),
                            reads=["A:rt0", "A:rt1"], writes=[dn])
        else:
            W, wh = aw_load(k, k.wgqk_d[h], 2048)
            Wv = W[:, 0:2048].rearrange("p (c j) -> p c j", j=256)
            for (c0, l0, n) in tiles:
                for part in range(2):
                    def mmp(e, part=part, l0=l0, n=n, Wv=Wv):
                        ins = None
                        for c in range(NCH):
                            ins = e.matmul(k.ps[part][:, :n], Wv[:, c, part * 128:(part + 1) * 128], k.h[:, c, l0:l0 + n],
                                           start=(c == 0), stop=(c == NCH - 1))
                        return ins
                    S.add("pe", mmp, reads=[wh, "h"], writes=["ps%d" % part])
                S.add("pe", lambda e, h=h, l0=l0, n=n: e.matmul(k.ps[2][:, :n], k.wa2[:, h * 128:(h + 1) * 128], k.rT[:, l0:l0 + n],
                                                               start=True, stop=True),
                      reads=["A:wa", "A:rT"], writes=["ps2"])
                S.add("act", lambda e, h=h, l0=l0, n=n: e.activation(k.la[:, l0:l0 + n], k.ps[2][:, :n], AF.Exp,
                                                                     bias=k.nba[:, h:h + 1], scale=-1.0),
                      reads=["ps2", "A:nba"], writes=["A:la"])
                S.add("dve", lambda e, l0=l0, n=n: e.tensor_scalar_add(k.la[:, l0:l0 + n], k.la[:, l0:l0 + n], 1.0),
                      reads=["A:la"], writes=["A:la"])
                S.add("act", lambda e, l0=l0, n=n: e.activation(k.la[:, l0:l0 + n], k.la[:, l0:l0 + n], AF.Ln),
                      reads=["A:la"], writes=["A:la"])
                S.add("dve", lambda e, l0=l0, n=n: e.tensor_tensor_scan(k.EQ[:, l0:l0 + n], k.rmask[:, l0:l0 + n], k.la[:, l0:l0 + n],
                                                                        0.0, ALU.mult, ALU.add),
                      reads=["A:la", "A:rmask"], writes=["A:EQ"])
                S.add("act", lambda e, l0=l0, n=n: e.activation(k.EK[:, l0:l0 + n], k.EQ[:, l0:l0 + n], AF.Exp, scale=1.0 / 16.0),
                      reads=["A:EQ"], writes=["A:EK"])
                S.add("act", lambda e, l0=l0, n=n: e.activation(k.EQ[:, l0:l0 + n], k.EQ[:, l0:l0 + n], AF.Exp, scale=-1.0 / 16.0),
                      reads=["A:EQ"], writes=["A:EQ"])
                S.add("dve", lambda e, l0=l0, n=n: e.scalar_tensor_tensor(k.qT[:, 0, l0:l0 + n], k.ps[0][:, :n], 128.0 ** -0.5,
                                                                          k.EQ[:, l0:l0 + n], ALU.mult, ALU.mult),
                      reads=["ps0", "A:EQ"], writes=["A:qT"])
                S.add("dve", lambda e, l0=l0, n=n: e.tensor_tensor(k.kT[:, 0, l0:l0 + n], k.ps[1][:, :n], k.EK[:, l0:l0 + n], ALU.mult),
                      reads=["ps1", "A:EK"], writes=["A:kT"])
        if ret:
            srcs = [(k.wret_d[h, 1], 4096, 512, 0, k.vtok, "A:vtok", AF.Copy), (k.wret_d[h, 2], 4096, 512, 0, k.gtok, "A:gtok", AF.Silu)]
        else:
            srcs = [(k.wgvg_d[h], 4096, 512, 0, k.vtok, "A:vtok", AF.Copy), (None, 4096, 512, 256, k.gtok, "A:gtok", AF.Silu)]
        for (src, ncols, wj, coff, dst, dn, fn) in srcs:
            if src is not None:
                W, wh = aw_load(k, src, ncols)
                Wv = W[:, 0:ncols].rearrange("p (c j) -> p c j", j=wj)
            for bi, (bl0, bn) in enumerate(blocks):
                pb_ = k.ps[4 + bi % 2]
                pbn = "ps%d" % (4 + bi % 2)

                def mmv(e, bl0=bl0, bn=bn, Wv=Wv, coff=coff, pb_=pb_):
                    ins = None
                    for c in range(NCH):
                        ins = e.matmul(pb_[0:bn, 0:DV], k.h[:, c, bl0:bl0 + bn], Wv[:, c, coff:coff + DV],
                                       start=(c == 0), stop=(c == NCH - 1))
                    return ins
                S.add("pe", mmv, reads=[wh, "h"], writes=[pbn])
                S.add("act", lambda e, dst=dst, bi=bi, bn=bn, pb_=pb_, fn=fn: e.activation(dst[0:bn, bi, :], pb_[0:bn, 0:DV], fn),
                      reads=[pbn], writes=[dn])
        S.add("act", lambda e, h=h: e.activation(k.Sb[:], k.Sst[:, h], AF.Copy), reads=["A:S"], writes=["A:Sb"])
        LP_, LQ_ = [[] for _ in blocks], [[] for _ in blocks]
        for bi, (bl0, bn) in enumerate(blocks):
            samp = (bn == TS)
            S.redirect = LP_[bi]
            po1 = k.ps[1 if bi % 2 == 0 else 4]
            po2 = k.ps[2 if bi % 2 == 0 else 5]
            p1n = "ps1" if bi % 2 == 0 else "ps4"
            p2n = "ps2" if bi % 2 == 0 else "ps5"
            tob = k.to if ret else k.tos[bi % 2]
            ton = "A:to" if ret else ("A:to", bi % 2)
            mask = k.amaskS[0:64, (h if ret else 0), :] if samp else k.amask[:, (h if ret else 0), :]
            PT = k.PT[bi % 2]
            ptn = ("A:PT", bi % 2)

            def mms(e, tob=tob, po1=po1, po2=po2, bl0=bl0, bn=bn):
                ins = None
                for dc in range(NDC):
                    ins = e.matmul(k.ps[0][0:bn, 0:bn], k.kT[:, dc, bl0:bl0 + bn], k.qT[:, dc, bl0:bl0 + bn],
                                   start=(dc == 0), stop=(dc == NDC - 1))
                return ins
            S.add("pe", mms, reads=["A:qT", "A:kT"], writes=["ps0"])
            S.add("dve", lambda e, tob=tob, po1=po1, po2=po2, PT=PT, bn=bn, mask=mask: e.tensor_tensor(PT[0:bn, 0:bn], k.ps[0][0:bn, 0:bn], mask, ALU.mult),
                  reads=["ps0", "A:amask"], writes=[ptn])
            if samp:
                for s in range(NS):
                    for dc in range(NDC):
                        S.add("pool", lambda e, tob=tob, po1=po1, po2=po2, s=s, dc=dc, bl0=bl0: e.tensor_copy(
                            k.qm[:, dc, s, 4 * s:4 * s + 4], k.qT[:, dc, bl0 + 4 * s:bl0 + 4 * s + 4]),
                            reads=["A:qT"], writes=["A:qm"])
            if ret or samp:
                S.add("pe", lambda e, tob=tob, po1=po1, po2=po2, PT=PT, bn=bn, bi=bi: e.matmul(po1[0:bn, 0:DV], PT[0:bn, 0:bn], k.vtok[0:bn, bi, :],
                                                                    start=True, stop=True),
                      reads=[ptn, "A:vtok"], writes=[p1n])
            if not samp:
                nchk = bn // CH
                if ret:
                    def mmst(e, tob=tob, po1=po1, po2=po2, bl0=bl0, bn=bn):
                        ins = None
                        for dc in range(NDC):
                            ins = e.matmul(po2[0:bn, 0:DV], k.qT[:, dc, bl0:bl0 + bn], k.Sb[:, dc, :],
                                           start=(dc == 0), stop=(dc == NDC - 1))
                        return ins
                    S.add("pe", mmst, reads=["A:qT", "A:Sb"], writes=[p2n])
            else:
                attn_sample_states(k, h, bi, bl0, po2, p2n)
            S.redirect = LQ_[bi]
            if ret or samp:
                S.add("act", lambda e, tob=tob, po1=po1, po2=po2, bn=bn: e.activation(tob[0:bn, :], po1[0:bn, 0:DV], AF.Copy), reads=[p1n], writes=[ton])
                if ret:
                    qd = k.acol[0:bn, (8 + h if samp else h):(8 + h if samp else h) + 1]
                    S.add("dve", lambda e, tob=tob, po1=po1, po2=po2: e.memset(k.st[:, 0:2], 0.0), writes=["A:st"])
                    S.add("dve", lambda e, tob=tob, po1=po1, po2=po2, bn=bn, qd=qd: e.scalar_tensor_tensor(tob[0:bn, :], po2[0:bn, 0:DV], qd, tob[0:bn, :],
                                                                                ALU.mult, ALU.add, accum_out=k.st[0:bn, 0:1]),
                          reads=[p2n, ton, "acol", "A:st"], writes=[ton, "A:st"])
                else:
                    S.add("dve", lambda e, tob=tob, po1=po1, po2=po2, bn=bn: e.tensor_tensor(tob[0:bn, :], po2[0:bn, 0:DV], tob[0:bn, :], ALU.add),
                          reads=[p2n, ton], writes=[ton])
            S.redirect = LP_[bi]
            if (not ret) and (not samp):
                for ci in range(bn // CH):
                    r0 = ci * CH
                    cl0 = bl0 + r0

                    def mmo(e, tob=tob, po1=po1, po2=po2, PT=PT, r0=r0, cl0=cl0, bi=bi):
                        e.matmul(po1[r0:r0 + CH, 0:DV], PT[:, r0:r0 + CH], k.vtok[:, bi, :], start=True, stop=False)
                        return e.matmul(po1[r0:r0 + CH, 0:DV], k.qT[:, 0, cl0:cl0 + CH], k.Sb[:, 0, :], start=False, stop=True)
                    S.add("pe", mmo, reads=[ptn, "A:vtok", "A:qT", "A:Sb"], writes=[p1n])
                    S.add("act", lambda e, tob=tob, po1=po1, po2=po2, r0=r0: e.activation(tob[r0:r0 + CH, :], po1[r0:r0 + CH, 0:DV], AF.Copy),
                          reads=[p1n], writes=[ton])
                    attn_state_update(k, h, bi, bl0, r0, CH, cl0 + CH - 1)
            S.redirect = LQ_[bi]
            if not ret:
                S.add("dve", lambda e, tob=tob, po1=po1, po2=po2: e.memset(k.st[:, 0:2], 0.0), writes=["A:st"])
            S.add("act", lambda e, tob=tob, po1=po1, po2=po2, bn=bn: e.activation(k.on[0:bn, :], tob[0:bn, :], AF.Square, accum_out=k.st[0:bn, 1:2]),
                  reads=[ton, "A:st"], writes=["A:on", "A:st"])
            sts = ["A:st"]
            if ret:
                S.add("dve", lambda e, tob=tob, po1=po1, po2=po2, bn=bn: e.tensor_scalar_mul(k.st[0:bn, 2:3], k.st[0:bn, 0:1], 1.0 / DV), reads=sts, writes=sts)
                S.add("dve", lambda e, tob=tob, po1=po1, po2=po2, bn=bn: e.tensor_tensor(k.st[0:bn, 3:4], k.st[0:bn, 2:3], k.st[0:bn, 2:3], ALU.mult), reads=sts, writes=sts)
                S.add("dve", lambda e, tob=tob, po1=po1, po2=po2, bn=bn: e.scalar_tensor_tensor(k.st[0:bn, 4:5], k.st[0:bn, 1:2], 1.0 / DV, k.st[0:bn, 3:4],
                                                                     ALU.mult, ALU.subtract), reads=sts, writes=sts)
                S.add("dve", lambda e, tob=tob, po1=po1, po2=po2, bn=bn: e.tensor_scalar_add(k.st[0:bn, 4:5], k.st[0:bn, 4:5], EPS), reads=sts, writes=sts)
            else:
                S.add("dve", lambda e, tob=tob, po1=po1, po2=po2, bn=bn: e.tensor_scalar(k.st[0:bn, 4:5], k.st[0:bn, 1:2], 1.0 / DV, EPS, ALU.mult, ALU.add),
                      reads=sts, writes=sts)
            S.add("act", lambda e, tob=tob, po1=po1, po2=po2, bn=bn: e.activation(k.st[0:bn, 5:6], k.st[0:bn, 4:5], AF.Ln), reads=sts, writes=sts)
            S.add("act", lambda e, tob=tob, po1=po1, po2=po2, bn=bn: e.activation(k.st[0:bn, 5:6], k.st[0:bn, 5:6], AF.Exp, scale=-0.5), reads=sts, writes=sts)
            if ret:
                S.add("dve", lambda e, tob=tob, po1=po1, po2=po2, bn=bn: e.scalar_tensor_tensor(k.st[0:bn, 6:7], k.st[0:bn, 2:3], -1.0, k.st[0:bn, 5:6],
                                                                     ALU.mult, ALU.mult), reads=sts, writes=sts)
                S.add("act", lambda e, tob=tob, po1=po1, po2=po2, bn=bn: e.activation(k.on[0:bn, :], tob[0:bn, :], AF.Identity,
                                                           bias=k.st[0:bn, 6:7], scale=k.st[0:bn, 5:6]),
                      reads=[ton, "A:st"], writes=["A:on"])
                S.add("dve", lambda e, tob=tob, po1=po1, po2=po2, bn=bn, bi=bi: e.tensor_tensor(k.og[0:bn, :], k.on[0:bn, :], k.gtok[0:bn, bi, :], ALU.mult),
                      reads=["A:on", "A:gtok"], writes=["A:og"])
            else:
                S.add("act", lambda e, tob=tob, po1=po1, po2=po2, bn=bn: e.activation(tob[0:bn, :], tob[0:bn, :], AF.Identity, scale=k.st[0:bn, 5:6]),
                      reads=[ton, "A:st"], writes=[ton])
                S.add("dve", lambda e, tob=tob, po1=po1, po2=po2, bn=bn: e.tensor_tensor(tob[0:bn, :], tob[0:bn, :], k.ngl[0:bn, :], ALU.mult),
                      reads=[ton, "A:ngl"], writes=[ton])
                S.add("dve", lambda e, tob=tob, po1=po1, po2=po2, bn=bn, bi=bi: e.tensor_tensor(k.og[0:bn, :], tob[0:bn, :], k.gtok[0:bn, bi, :], ALU.mult),
                      reads=[ton, "A:gtok"], writes=["A:og"])

            def mmt(e, tob=tob, po1=po1, po2=po2, bn=bn):
                ins = None
                for vc in range(NVC):
                    ins = e.transpose(k.pst[:, vc * 128:vc * 128 + bn], k.og[0:bn, vc * 128:(vc + 1) * 128], k.ident[0:bn, 0:bn])
                return ins
            S.add("pe", mmt, reads=["A:og", "ident"], writes=["pst"])
            S.add("act", lambda e, tob=tob, po1=po1, po2=po2, h=h, bl0=bl0, bn=bn: e.activation(
                k.onT[:, h * NVC:(h + 1) * NVC, bl0:bl0 + bn],
                k.pst[:, 0:NVC * 128].rearrange("p (v t) -> p v t", t=128)[:, :, 0:bn], AF.Copy),
                reads=["pst"], writes=["A:onT"])
            S.redirect = LP_[bi]
            if ret and not samp:
                attn_state_update(k, h, bi, bl0, 0, 128, None)
        S.redirect = None
        S.flush(LP_[0])
        for bi in range(len(blocks)):
            if bi + 1 < len(blocks):
                S.flush(LP_[bi + 1])
            S.flush(LQ_[bi])
    if mid is not None:
        mid()
    wod = k.wo_ret_d if ret else k.wo_gla_d
    for (c0, l0, n) in tiles:
        for mp in range(4):
            W, wh = aw_load(k, wod[mp], 2 * NKC * 128)
            Wv = W[:, 0:2 * NKC * 128].rearrange("p (a c j) -> p a c j", a=2, j=128)
            for mi in range(2):
                m = mp * 2 + mi
                po = k.ps[4 + m % 2]
                pon = "ps%d" % (4 + m % 2)

                def mmw(e, Wv=Wv, mi=mi, po=po, l0=l0, n=n):
                    ins = None
                    for kc in range(NKC):
                        ins = e.matmul(po[:, :n], Wv[:, mi, kc, :], k.onT[:, kc, l0:l0 + n], start=(kc == 0), stop=(kc == NKC - 1))
                    return ins
                S.add("pe", mmw, reads=[wh, "A:onT"], writes=[pon])
                S.add("act", lambda e, po=po, m=m, n=n: e.activation(k.mixbuf[:, m, :n], po[:, :n], AF.Copy),
                      reads=[pon], writes=["mixbuf"])
        postnorm_residual(k, layer, 1, c0, n)


def attn_state_update(k, h, bi, bl0, r0, ch, last_col):
    S = k.S
    cf = k.acf
    NDC, DV = cf["NDC"], cf["DV"]
    ret = (cf["nm"] == "ret")
    if r0 == 0:
        def mmk(e, bl0=bl0):
            ins = None
            for dc in range(NDC):
                ins = e.transpose(k.pst[:, 512 + dc * 128:512 + (dc + 1) * 128], k.kT[:, dc, bl0:bl0 + 128], k.ident[:])
            return ins
        S.add("pe", mmk, reads=["A:kT", "ident"], writes=["pst"])
        src = k.pst[:, 512:512 + NDC * 128].rearrange("p (c d) -> p c d", d=128)
        if ret:
            S.add("dve", lambda e, h=h, src=src: e.tensor_scalar_mul(k.ktok[:], src, k.acol[:, 4 + h:5 + h]),
                  reads=["pst", "acol"], writes=["A:ktok"])
        else:
            S.add("dve", lambda e, src=src: e.tensor_copy(k.ktok[:], src), reads=["pst"], writes=["A:ktok"])
    for dc in range(NDC):
        pd = k.ps[3] if dc == 0 else k.ps[6]
        pdn = "ps3" if dc == 0 else "ps6"
        S.add("pe", lambda e, dc=dc, pd=pd, r0=r0, ch=ch, bi=bi: e.matmul(
            pd[:, 0:DV], k.ktok[r0:r0 + ch, dc, :], k.vtok[r0:r0 + ch, bi, :], start=True, stop=True),
            reads=["A:ktok", "A:vtok"], writes=[pdn])
        if ret:
            gC = RET_G[h] ** 128
            S.add("dve", lambda e, dc=dc, pd=pd, h=h, gC=gC: e.scalar_tensor_tensor(
                k.Sst[:, h, dc, :], k.Sst[:, h, dc, :], gC, pd[:, 0:DV], ALU.mult, ALU.add),
                reads=[pdn, "A:S"], writes=["A:S"])
        else:
            S.add("dve", lambda e, dc=dc, pd=pd, h=h: e.tensor_tensor(k.Sst[:, h, dc, :], k.Sst[:, h, dc, :], pd[:, 0:DV], ALU.add),
                  reads=[pdn, "A:S"], writes=["A:S"])
            S.add("dve", lambda e, dc=dc, h=h, last_col=last_col: e.tensor_scalar_mul(
                k.Sst[:, h, dc, :], k.Sst[:, h, dc, :], k.EQ[:, last_col:last_col + 1]),
                reads=["A:S", "A:EQ"], writes=["A:S"])
        S.add("act", lambda e, dc=dc, h=h: e.activation(k.Sb[:, dc, :], k.Sst[:, h, dc, :], AF.Copy), reads=["A:S"], writes=["A:Sb"])


def attn_sample_states(k, h, bi, bl0, po2, p2n):
    S = k.S
    cf = k.acf
    NDC, DV = cf["NDC"], cf["DV"]
    ret = (cf["nm"] == "ret")

    def mmk(e):
        ins = None
        for dc in range(NDC):
            ins = e.transpose(k.pst[0:64, 512 + dc * 128:512 + (dc + 1) * 128], k.kT[:, dc, bl0:bl0 + 64], k.ident[:])
        return ins
    S.add("pe", mmk, reads=["A:kT", "ident"], writes=["pst"])
    src = k.pst[0:64, 512:512 + NDC * 128].rearrange("p (c d) -> p c d", d=128)
    if ret:
        S.add("dve", lambda e: e.tensor_scalar_mul(k.ktok[0:64], src, k.acol[0:64, 12 + h:13 + h]),
              reads=["pst", "acol"], writes=["A:ktok"])
    else:
        S.add("dve", lambda e: e.tensor_copy(k.ktok[0:64], src), reads=["pst"], writes=["A:ktok"])
    out_d = k.osts_d
    for s in range(NS):
        sl = k.s0_cnt % 2
        k.s0_cnt += 1
        km = k.km[sl]
        S.add("sp", lambda e, sl=sl, s=s: e.dma_start(out=k.S0f[sl][:], in_=k.sst_d[s, h]), writes=[("A:S0f", sl)], dma=True)
        S.add("act", lambda e, sl=sl: e.activation(k.S0b[sl][:].rearrange("p c v -> p (c v)"),
                                                   k.S0f[sl][:].rearrange("p c v -> p (c v)"), AF.Copy),
              reads=[("A:S0f", sl)], writes=[("A:S0b", sl)])

        def mmq(e, s=s, sl=sl, po2=po2):
            ins = None
            for dc in range(NDC):
                ins = e.matmul(po2[0:64, 0:DV], k.qm[:, dc, s, :], k.S0b[sl][:, dc, :],
                               start=(s == 0 and dc == 0), stop=(s == NS - 1 and dc == NDC - 1))
            return ins
        S.add("pe", mmq, reads=["A:qm", ("A:S0b", sl)], writes=[p2n])
        S.add("pool", lambda e, km=km, s=s: e.tensor_scalar_mul(km[0:64], k.ktok[0:64], k.acol[0:64, 16 + s:17 + s]),
              reads=["A:ktok", "acol"], writes=[("A:km", sl)])
        for dc in range(NDC):
            pd = k.ps[3] if dc == 0 else k.ps[6]
            pdn = "ps3" if dc == 0 else "ps6"
            S.add("pe", lambda e, dc=dc, pd=pd, km=km: e.matmul(pd[:, 0:DV], km[0:64, dc, :], k.vtok[0:64, bi, :], start=True, stop=True),
                  reads=[("A:km", sl), "A:vtok"], writes=[pdn])
            if ret:
                gC = RET_G[h] ** 4
                S.add("dve", lambda e, dc=dc, pd=pd, sl=sl, gC=gC: e.scalar_tensor_tensor(
                    k.S0f[sl][:, dc, :], k.S0f[sl][:, dc, :], gC, pd[:, 0:DV], ALU.mult, ALU.add),
                    reads=[pdn, ("A:S0f", sl)], writes=[("A:S0f", sl)])
            else:
                lc = bl0 + 4 * s + 3
                S.add("dve", lambda e, dc=dc, pd=pd, sl=sl: e.tensor_tensor(k.S0f[sl][:, dc, :], k.S0f[sl][:, dc, :], pd[:, 0:DV], ALU.add),
                      reads=[pdn, ("A:S0f", sl)], writes=[("A:S0f", sl)])
                S.add("dve", lambda e, dc=dc, sl=sl, lc=lc: e.tensor_scalar_mul(k.S0f[sl][:, dc, :], k.S0f[sl][:, dc, :], k.EQ[:, lc:lc + 1]),
                      reads=[("A:S0f", sl), "A:EQ"], writes=[("A:S0f", sl)])
        S.add("sp", lambda e, sl=sl, s=s: e.dma_start(out=out_d[s, h], in_=k.S0f[sl][:]), reads=[("A:S0f", sl)], dma=True, final=True)


def attn_finish(k, layer, kind):
    S = k.S
    for h in range(4):
        S.add("sp", lambda e, h=h: e.dma_start(out=k.ostp_d[h], in_=k.Sst[:, h]), reads=["A:S"], dma=True, final=True)


PI = float(np.pi)
I32 = mybir.dt.int32


def s5_setup(k, layer):
    S, c = k.S, k.carve
    j = layer // 3
    k.s5j = j
    k.Wm = c("Wm", [128, 8, 4, 2, 128], BF16)
    k.Cm = c("Cm", [128, 8, 4, 2, 128], BF16)
    k.tab = c("tab", [128, 3, 32, 32], F32)
    k.rr = c("rr", [128, 32], F32)
    k.sprev = c("sprev", [128, 2, 32], F32)
    k.tA = c("tA", [128, 4, 32], F32)
    k.tau = c("tau", [128, 2, 32], F32)
    off_bu = (k.arena_ptr + 31) // 32 * 32
    k.BU = c("BU", [128, 8, 4, 2, 32], F32)
    k.BU2 = c("BU2", [128, 8, 4, 2, 32], F32)
    k.BUs = [k.BU, k.BU2]
    k.Hb = c("Hb", [128, 8, 4, 2, 32], BF16)
    k.ZR = [c("ZR%d" % i, [128, 32, 32], F32) for i in range(2)]
    k.ZI = [c("ZI%d" % i, [128, 32, 32], F32) for i in range(2)]
    k.yg = c("yg", [128, NCH, GT], BF16)
    k.yt = [k.rstd[:, 0:256].rearrange("p (c t) -> p c t", t=32), k.rstd[:, 256:512].rearrange("p (c t) -> p c t", t=32)]
    k.ysg = k.rstd
    off_yt = k.arena_ptr
    k.aC = c("aC", [128, 2, 32], F32)
    k.scd = k.mixbuf[:, 0, :].rearrange("p (a b) -> p a b", a=2)
    k.scp = k.mixbuf[:, 1, :].rearrange("p (a b) -> p a b", a=2)
    k.h0 = c("h0", [128, NS, 2, 32], F32)
    k.osS = k.h0
    k.osp = c("osp", [128, 2, 32], F32)
    k.d5 = c("d5", [128, 8], F32)
    k.bd = c("bd", [128, 8], F32)
    k.dtq = c("dtq", [128, 8], F32)
    k.lamC = c("lamC", [128, 2, 32], F32)
    k.wgl = [c("wgl%d" % i, [128, 2, 8, 128], BF16) for i in range(2)]
    k.wgl_cnt = 0
    k.sg = k.ysg
    end_ptr = k.arena_ptr
    k.arena_ptr = off_bu
    lamQ = c("lamQ", [128, 2, 512], F32)
    bQ = c("bQ", [128, 2, 512], F32)
    bbQ = c("bbQ", [128, 2, 512], F32)
    aQ = c("aQ", [128, 3, 512], F32)
    off_tq = (k.arena_ptr + 31) // 32 * 32
    tq = [c("tq%d" % i, [128, 512], F32) for i in range(5)]
    tqi = c("tqi", [128, 512], I32)
    tc_ = [c("tc%d" % i, [128, 32], F32) for i in range(5)]
    tci = c("tci", [128, 32], I32)
    aC3 = c("aC3", [128, 3, 32], F32)
    k.cC = c("cC", [128, 8, 4, 2, 16], F32)
    assert k.arena_ptr <= off_yt, (k.arena_ptr, off_yt)
    k.arena_ptr = off_tq
    tabt = c("tabt", [128, 1024], F32)
    tabi = c("tabi", [128, 1024], I32)
    k.arena_ptr = end_ptr
    P = "A:prep"
    S.add("sp", lambda e: e.dma_start(out=lamQ[:], in_=k.s5lamQ_d[j]), writes=[P], dma=True)
    S.add("sp", lambda e: e.dma_start(out=bQ[:], in_=k.s5bQ_d[j]), writes=[P], dma=True)
    S.add("sp", lambda e: e.dma_start(out=k.lamC[:], in_=k.s5lamC_d[j]), writes=[P], dma=True)
    S.add("sp", lambda e: e.dma_start(out=k.cC[:], in_=k.s5cC_d[j]), writes=[P], dma=True)
    S.add("sp", lambda e: e.dma_start(out=k.dtq[:], in_=k.s5dt_d[j]), writes=[P], dma=True)
    S.add("sp", lambda e: e.dma_start(out=k.d5[:], in_=k.s5d_d[j]), writes=["A:d5"], dma=True)
    S.add("sp", lambda e: e.dma_start(out=k.bd[:], in_=k.s5bd_d), writes=[P], dma=True)
    S.add("sp", lambda e: e.dma_start(out=k.tau[:], in_=k.s5tau_d), writes=[P], dma=True)
    S.add("sp", lambda e: e.dma_start(out=k.h0[:], in_=k.s5h0_d[j]), writes=["A:h0"], dma=True)
    S.add("act", lambda e: e.activation(k.dtq[:], k.dtq[:], AF.Exp), reads=[P], writes=[P])

    def cexp(lr, li, X, out3, tmp, tmpi):
        def v(t):
            return t.rearrange("p (c x) -> p c x", x=X)
        dtb = k.dtq[:, :].unsqueeze(2).broadcast_to([128, 8, X])
        zr, zi, w, w2, em = tmp
        ops = []
        A = lambda eng, fn: S.add(eng, fn, reads=[P], writes=[P])
        A("dve", lambda e: e.tensor_tensor(v(zr[:]), v(lr), dtb, ALU.mult))
        A("dve", lambda e: e.tensor_tensor(v(zi[:]), v(li), dtb, ALU.mult))
        A("dve", lambda e: e.tensor_scalar(em[:], zr[:], 0.25, 1.0, ALU.mult, ALU.add))
        A("dve", lambda e: e.tensor_tensor(em[:], em[:], zr[:], ALU.mult))
        A("dve", lambda e: e.tensor_scalar(em[:], em[:], 1.0 / 3.0, 1.0, ALU.mult, ALU.add))
        A("dve", lambda e: e.tensor_tensor(em[:], em[:], zr[:], ALU.mult))
        A("dve", lambda e: e.tensor_scalar(em[:], em[:], 0.5, 1.0, ALU.mult, ALU.add))
        A("dve", lambda e: e.tensor_tensor(em[:], em[:], zr[:], ALU.mult))
        A("act", lambda e: e.activation(zr[:], zr[:], AF.Exp))

        def sin_of(dst, src, scale):
            A("dve", lambda e: e.tensor_scalar_mul(dst[:], src[:], scale / (2.0 * PI)))
            A("dve", lambda e: e.tensor_copy(tmpi[:], dst[:]))
            A("dve", lambda e: e.tensor_copy(dst[:], tmpi[:]))
            A("dve", lambda e: e.tensor_scalar_mul(dst[:], dst[:], -2.0 * PI))
            A("dve", lambda e: e.scalar_tensor_tensor(dst[:], src[:], scale, dst[:], ALU.mult, ALU.add))
            A("dve", lambda e: e.tensor_scalar(dst[:], dst[:], -PI, PI, ALU.max, ALU.min))
            A("act", lambda e: e.activation(dst[:], dst[:], AF.Sin))
        sin_of(w, zi, 1.0)
        sin_of(w2, zi, 0.5)
        A("dve", lambda e: e.tensor_tensor(out3[:, 1, :], zr[:], w[:], ALU.mult))
        A("dve", lambda e: e.tensor_tensor(w2[:], w2[:], w2[:], ALU.mult))
        A("dve", lambda e: e.tensor_tensor(w2[:], w2[:], zr[:], ALU.mult))
        A("dve", lambda e: e.scalar_tensor_tensor(out3[:, 2, :], w2[:], -2.0, em[:], ALU.mult, ALU.add))
        A("dve", lambda e: e.tensor_scalar_add(out3[:, 0, :], out3[:, 2, :], 1.0))
    cexp(lamQ[:, 0, :], lamQ[:, 1, :], 64, aQ, tq, tqi)
    cexp(k.lamC[:, 0, :], k.lamC[:, 1, :], 4, aC3, tc_, tci)
    A = lambda eng, fn: S.add(eng, fn, reads=[P], writes=[P])
    S.add("dve", lambda e: e.tensor_copy(k.aC[:], aC3[:, 0:2, :]), reads=[P], writes=["A:aC"])
    mag, theta = tc_[0], tc_[1]
    S.add("dve", lambda e: e.tensor_copy(k.rr[:], mag[:]), reads=[P], writes=["A:tab"])
    th_b = theta[:, :].unsqueeze(2).broadcast_to([128, 32, 32])
    tau_b = k.tau[:, 0, :].unsqueeze(1).broadcast_to([128, 32, 32])
    msk_b = k.tau[:, 1, :].unsqueeze(1).broadcast_to([128, 32, 32])
    mag_b = mag[:, :].unsqueeze(2).broadcast_to([128, 32, 32])
    ang = k.tab[:, 2].rearrange("p a b -> p (a b)")
    S.add("dve", lambda e: e.tensor_tensor(k.tab[:, 2], th_b, tau_b, ALU.mult), reads=[P], writes=["A:tab"])

    def sin_tab(dst, shift):
        A2 = lambda eng, fn: S.add(eng, fn, reads=[P, "A:tab"], writes=[P, "A:tab"])
        A2("dve", lambda e: e.tensor_scalar(tabt[:], ang, 1.0 / (2.0 * PI), shift / (2.0 * PI), ALU.mult, ALU.add))
        A2("dve", lambda e: e.tensor_copy(tabi[:], tabt[:]))
        A2("dve", lambda e: e.tensor_copy(tabt[:], tabi[:]))
        A2("dve", lambda e: e.tensor_scalar(tabt[:], tabt[:], -2.0 * PI, shift, ALU.mult, ALU.add))
        A2("dve", lambda e: e.tensor_tensor(tabt[:], tabt[:], ang, ALU.add))
        A2("dve", lambda e: e.tensor_scalar(tabt[:], tabt[:], -PI, PI, ALU.max, ALU.min))
        A2("act", lambda e: e.activation(dst, tabt[:], AF.Sin))
    sin_tab(k.tab[:, 1].rearrange("p a b -> p (a b)"), 0.0)
    sin_tab(k.tab[:, 0].rearrange("p a b -> p (a b)"), PI / 2.0)
    S.add("dve", lambda e: e.tensor_tensor(k.tab[:, 2], mag_b, msk_b, ALU.mult), reads=[P, "A:tab"], writes=["A:tab"])
    S.add("dve", lambda e: e.memset(k.sprev[:], 0.0), writes=["A:sprev"])
    lr, li = lamQ[:, 0, :], lamQ[:, 1, :]
    ar1, ai = aQ[:, 2, :], aQ[:, 1, :]
    t0, t1, t2, t3, t4 = tq
    A("dve", lambda e: e.tensor_tensor(t0[:], ar1, lr, ALU.mult))
    A("dve", lambda e: e.tensor_tensor(t1[:], ai, li, ALU.mult))
    A("dve", lambda e: e.tensor_tensor(t0[:], t0[:], t1[:], ALU.add))
    A("dve", lambda e: e.tensor_tensor(t1[:], ai, lr, ALU.mult))
    A("dve", lambda e: e.tensor_tensor(t2[:], ar1, li, ALU.mult))
    A("dve", lambda e: e.tensor_tensor(t1[:], t1[:], t2[:], ALU.subtract))
    A("dve", lambda e: e.tensor_tensor(t2[:], lr, lr, ALU.mult))
    A("dve", lambda e: e.tensor_tensor(t3[:], li, li, ALU.mult))
    A("dve", lambda e: e.tensor_tensor(t2[:], t2[:], t3[:], ALU.add))
    A("dve", lambda e: e.reciprocal(t2[:], t2[:]))
    A("dve", lambda e: e.tensor_tensor(t0[:], t0[:], t2[:], ALU.mult))
    A("dve", lambda e: e.tensor_tensor(t1[:], t1[:], t2[:], ALU.mult))
    A("dve", lambda e: e.tensor_tensor(t2[:], t0[:], bQ[:, 0, :], ALU.mult))
    A("dve", lambda e: e.tensor_tensor(t3[:], t1[:], bQ[:, 1, :], ALU.mult))
    A("dve", lambda e: e.tensor_tensor(bbQ[:, 0, :], t2[:], t3[:], ALU.subtract))
    A("dve", lambda e: e.tensor_tensor(t2[:], t0[:], bQ[:, 1, :], ALU.mult))
    A("dve", lambda e: e.tensor_tensor(t3[:], t1[:], bQ[:, 0, :], ALU.mult))
    A("dve", lambda e: e.tensor_tensor(bbQ[:, 1, :], t2[:], t3[:], ALU.add))
    bdb = k.bd[:, :].unsqueeze(1).unsqueeze(3)
    for cc in range(8):
        for ri in range(2):
            src = bbQ[:, ri, cc * 64:(cc + 1) * 64].rearrange("p (m x) -> p m x", x=16).unsqueeze(2).broadcast_to([128, 4, 8, 16])
            dst = k.Wm[:, cc, :, ri, :].rearrange("p m (g x) -> p m g x", x=16)
            S.add("dve", lambda e, src=src, dst=dst: e.tensor_tensor(dst, src, bdb.broadcast_to([128, 4, 8, 16]), ALU.mult),
                  reads=[P], writes=["A:Wm"])
    S.add("dve", lambda e: e.tensor_scalar_mul(k.cC[:, :, :, 1, :], k.cC[:, :, :, 1, :], -1.0), reads=[P], writes=[P])
    for cc in range(8):
        src = k.cC[:, cc].rearrange("p m r x -> p (m r) x").unsqueeze(2).broadcast_to([128, 8, 8, 16])
        dst = k.Cm[:, cc].rearrange("p m r (g x) -> p (m r) g x", x=16)
        S.add("dve", lambda e, src=src, dst=dst: e.tensor_tensor(dst, src, bdb.broadcast_to([128, 8, 8, 16]), ALU.mult),
              reads=[P], writes=["A:Cm"])
    k.barrier()


def s5_group(k, layer, gi, g, tiles, mid=None):
    S = k.S
    j = k.s5j
    BS = 32
    blocks = [(b * BS, False, 0) for b in range(g["pn"] // BS)]
    if g["samp"]:
        blocks += [(g["pn"], True, 0), (g["pn"] + BS, True, 8)]
    BUv = k.BU[:].rearrange("p c m r t -> p (c m) r t")
    Hbv = k.Hb[:].rearrange("p c m r t -> p (c m) r t")
    ar, ai = k.aC[:, 0, :], k.aC[:, 1, :]
    cosT, sinT, RT = k.tab[:, 0], k.tab[:, 1], k.tab[:, 2]
    mb = [k.mixbuf[:, 2 * i:2 * i + 2, :].rearrange("p a (b c) -> p (a b) c", c=32) for i in range(4)]
    if not hasattr(k, "s5blk"):
        k.s5blk = 0
    zbs = []
    for _ in blocks:
        zbs.append(k.s5blk % 2)
        k.s5blk += 1

    def stageA(bi):
        (bl0, samp, s0) = blocks[bi]
        zb = zbs[bi]
        ZR, ZI = k.ZR[zb], k.ZI[zb]
        zrn, zin = ("A:ZR", zb), ("A:ZI", zb)
        BUb = k.BUs[bi % 2]
        bun = ("A:BU", bi % 2)
        BUv = BUb[:].rearrange("p c m r t -> p (c m) r t")
        BUr, BUi = BUv[:, :, 0, :], BUv[:, :, 1, :]
        for cc in range(8):
            pb = k.ps[cc % 4]
            pbn = "ps%d" % (cc % 4)

            def mmb(e, cc=cc, pb=pb, bl0=bl0):
                ins = None
                for m in range(4):
                    for ri in range(2):
                        o = (m * 2 + ri) * BS
                        ins = e.matmul(pb[:, o:o + BS], k.Wm[:, cc, m, ri, :], k.h[:, cc, bl0:bl0 + BS], start=True, stop=True)
                return ins
            S.add("pe", mmb, reads=["A:Wm", "h"], writes=[pbn])
            S.add("act", lambda e, cc=cc, pb=pb: e.activation(BUb[:, cc].rearrange("p m r t -> p (m r t)"), pb[:, 0:8 * BS], AF.Copy),
                  reads=[pbn], writes=[bun])
        if not samp:
            t1, t2, t3, t4 = mb
            S.add("pool", lambda e: e.tensor_tensor(t1, BUr, cosT, ALU.mult), reads=[bun, "A:tab", "mixbuf"], writes=[("mbt", 0)])
            S.add("pool", lambda e: e.tensor_tensor(t2, BUi, sinT, ALU.mult), reads=[bun, "A:tab", "mixbuf"], writes=[("mbt", 1)])
            S.add("pool", lambda e, ZR=ZR: e.tensor_tensor(ZR[:], t1, t2, ALU.add), reads=[("mbt", 0), ("mbt", 1)], writes=[zrn])
            S.add("pool", lambda e: e.tensor_tensor(t1, BUi, cosT, ALU.mult), reads=[bun, "A:tab", "mixbuf"], writes=[("mbt", 0)])
            S.add("pool", lambda e: e.tensor_tensor(t2, BUr, sinT, ALU.mult), reads=[bun, "A:tab", "mixbuf"], writes=[("mbt", 1)])
            S.add("pool", lambda e, ZI=ZI: e.tensor_tensor(ZI[:], t1, t2, ALU.subtract), reads=[("mbt", 0), ("mbt", 1)], writes=[zin])


    def stageB(bi):
        (bl0, samp, s0) = blocks[bi]
        zb = zbs[bi]
        ZR, ZI = k.ZR[zb], k.ZI[zb]
        zrn, zin = ("A:ZR", zb), ("A:ZI", zb)
        BUb = k.BUs[bi % 2]
        bun = ("A:BU", bi % 2)
        BUv = BUb[:].rearrange("p c m r t -> p (c m) r t")
        BUr, BUi = BUv[:, :, 0, :], BUv[:, :, 1, :]
        t1, t2, t3, t4 = mb
        if samp:
            H8 = 8
            for t in range(LS):
                def prev(ri, t=t):
                    if t == 0:
                        return k.h0[:, s0:s0 + H8, ri, :].rearrange("p s x -> p x s")
                    return (ZR if ri == 0 else ZI)[:, :, t - 1:BS:4]
                arb = ar.unsqueeze(2).broadcast_to([128, 32, H8])
                aib = ai.unsqueeze(2).broadcast_to([128, 32, H8])
                T = lambda buf, i: buf[:, i, :].rearrange("p (x s) -> p x s", s=8)
                rd = ["A:h0", "A:aC", zrn, zin, "mixbuf"]
                S.add("dve", lambda e, p=prev(0), arb=arb: e.tensor_tensor(T(k.scd, 0), arb, p, ALU.mult), reads=rd, writes=["A:scd"])
                S.add("dve", lambda e, p=prev(1), aib=aib: e.tensor_tensor(T(k.scd, 1), aib, p, ALU.mult), reads=rd, writes=["A:scd"])
                S.add("dve", lambda e: e.tensor_tensor(T(k.scd, 0), T(k.scd, 0), T(k.scd, 1), ALU.subtract), reads=["A:scd"], writes=["A:scd"])
                S.add("pool", lambda e, p=prev(0), aib=aib: e.tensor_tensor(T(k.scp, 0), aib, p, ALU.mult), reads=rd, writes=["A:scp"])
                S.add("pool", lambda e, p=prev(1), arb=arb: e.tensor_tensor(T(k.scp, 1), arb, p, ALU.mult), reads=rd, writes=["A:scp"])
                S.add("pool", lambda e: e.tensor_tensor(T(k.scp, 0), T(k.scp, 0), T(k.scp, 1), ALU.add), reads=["A:scp"], writes=["A:scp"])
                S.add("dve", lambda e, o=ZR[:, :, t:BS:4], b=BUr[:, :, t:BS:4]: e.tensor_tensor(o, T(k.scd, 0), b, ALU.add),
                      reads=["A:scd", bun], writes=[zrn])
                S.add("pool", lambda e, o=ZI[:, :, t:BS:4], b=BUi[:, :, t:BS:4]: e.tensor_tensor(o, T(k.scp, 0), b, ALU.add),
                      reads=["A:scp", bun], writes=[zin])
            S.add("dve", lambda e, ZR=ZR, s0=s0: e.tensor_copy(k.h0[:, s0:s0 + H8, 0, :].rearrange("p s x -> p x s"), ZR[:, :, 3:BS:4]),
                  reads=[zrn], writes=["A:h0"])
            S.add("dve", lambda e, ZI=ZI, s0=s0: e.tensor_copy(k.h0[:, s0:s0 + H8, 1, :].rearrange("p s x -> p x s"), ZI[:, :, 3:BS:4]),
                  reads=[zin], writes=["A:h0"])
            S.add("act", lambda e, ZR=ZR: e.activation(Hbv[:, :, 0, :], ZR[:], AF.Copy), reads=[zrn], writes=["A:Hb"])
            S.add("act", lambda e, ZI=ZI: e.activation(Hbv[:, :, 1, :], ZI[:], AF.Copy), reads=[zin], writes=["A:Hb"])
        else:
            S.add("dve", lambda e: e.tensor_tensor(k.tA[:, 0:2, :], k.sprev[:], k.rr[:, :].unsqueeze(1).broadcast_to([128, 2, 32]), ALU.mult),
                  reads=["A:sprev", "A:tab"], writes=["A:tA"])
            S.add("dve", lambda e, ZR=ZR: e.tensor_tensor(ZR[:, :, 0], ZR[:, :, 0], k.tA[:, 0, :], ALU.add), reads=["A:tA", zrn], writes=[zrn])
            S.add("dve", lambda e, ZI=ZI: e.tensor_tensor(ZI[:, :, 0], ZI[:, :, 0], k.tA[:, 1, :], ALU.add), reads=["A:tA", zin], writes=[zin])
            Rf = RT.rearrange("p a b -> p (a b)")
            S.add("dve", lambda e, ZR=ZR: e.tensor_tensor_scan(ZR[:].rearrange("p a b -> p (a b)"), Rf, ZR[:].rearrange("p a b -> p (a b)"), 0.0, ALU.mult, ALU.add),
                  reads=[zrn, "A:tab"], writes=[zrn])
            S.add("dve", lambda e, ZI=ZI: e.tensor_tensor_scan(ZI[:].rearrange("p a b -> p (a b)"), Rf, ZI[:].rearrange("p a b -> p (a b)"), 0.0, ALU.mult, ALU.add),
                  reads=[zin, "A:tab"], writes=[zin])
            cl, sl_ = cosT[:, :, BS - 1], sinT[:, :, BS - 1]
            S.add("dve", lambda e, ZR=ZR: e.tensor_tensor(k.tA[:, 0, :], ZR[:, :, BS - 1], cl, ALU.mult), reads=[zrn, "A:tab"], writes=["A:tA"])
            S.add("dve", lambda e, ZI=ZI: e.tensor_tensor(k.tA[:, 1, :], ZI[:, :, BS - 1], sl_, ALU.mult), reads=[zin, "A:tab"], writes=["A:tA"])
            S.add("dve", lambda e, ZR=ZR: e.tensor_tensor(k.tA[:, 2, :], ZR[:, :, BS - 1], sl_, ALU.mult), reads=[zrn, "A:tab"], writes=["A:tA"])
            S.add("dve", lambda e, ZI=ZI: e.tensor_tensor(k.tA[:, 3, :], ZI[:, :, BS - 1], cl, ALU.mult), reads=[zin, "A:tab"], writes=["A:tA"])
            S.add("dve", lambda e: e.tensor_tensor(k.sprev[:, 0, :], k.tA[:, 0, :], k.tA[:, 1, :], ALU.subtract), reads=["A:tA"], writes=["A:sprev"])
            S.add("dve", lambda e: e.tensor_tensor(k.sprev[:, 1, :], k.tA[:, 2, :], k.tA[:, 3, :], ALU.add), reads=["A:tA"], writes=["A:sprev"])
            S.add("dve", lambda e, ZR=ZR: e.tensor_tensor(t3, ZR[:], cosT, ALU.mult), reads=[zrn, "A:tab", "mixbuf"], writes=[("mbt", 2)])
            S.add("dve", lambda e, ZI=ZI: e.tensor_tensor(t4, ZI[:], sinT, ALU.mult), reads=[zin, "A:tab", "mixbuf"], writes=[("mbt", 3)])
            S.add("dve", lambda e: e.tensor_tensor(Hbv[:, :, 0, :], t3, t4, ALU.subtract), reads=[("mbt", 2), ("mbt", 3)], writes=["A:Hb"])
            S.add("dve", lambda e, ZR=ZR: e.tensor_tensor(t3, ZR[:], sinT, ALU.mult), reads=[zrn, "A:tab", "mixbuf"], writes=[("mbt", 2)])
            S.add("dve", lambda e, ZI=ZI: e.tensor_tensor(t4, ZI[:], cosT, ALU.mult), reads=[zin, "A:tab", "mixbuf"], writes=[("mbt", 3)])
            S.add("dve", lambda e: e.tensor_tensor(Hbv[:, :, 1, :], t3, t4, ALU.add), reads=[("mbt", 2), ("mbt", 3)], writes=["A:Hb"])

    def stageC(bi):
        (bl0, samp, s0) = blocks[bi]
        def mmy(e):
            ins = None
            for cc in range(8):
                for m in range(4):
                    for ri in range(2):
                        ins = e.matmul(k.ps[4][:, cc * BS:(cc + 1) * BS], k.Cm[:, cc, m, ri, :], k.Hb[:, cc, m, ri, :],
                                       start=(m == 0 and ri == 0), stop=(m == 3 and ri == 1))
            return ins
        S.add("pe", mmy, reads=["A:Cm", "A:Hb"], writes=["ps4"])
        y0, y1 = k.yt
        d5b = k.d5[:, :].unsqueeze(2).broadcast_to([128, 8, BS])
        S.add("pool", lambda e, bl0=bl0: e.tensor_tensor(y0, k.h[:, :, bl0:bl0 + BS], d5b, ALU.mult), reads=["h", "A:d5"], writes=["rstd"])
        S.add("dve", lambda e: e.tensor_tensor(y0, y0, k.ps[4][:, 0:8 * BS].rearrange("p (c t) -> p c t", t=BS), ALU.add),
              reads=["rstd", "ps4"], writes=["rstd"])
        S.add("pool", lambda e: e.tensor_tensor(y1, y0, y0, ALU.mult), reads=["rstd"], writes=["rstd"])
        S.add("pool", lambda e: e.tensor_scalar(y1, y1, 0.044715, 1.0, ALU.mult, ALU.add), reads=["rstd"], writes=["rstd"])
        S.add("pool", lambda e: e.tensor_tensor(y1, y1, y0, ALU.mult), reads=["rstd", "rstd"], writes=["rstd"])
        S.add("act", lambda e: e.activation(y1, y1, AF.Sigmoid, scale=1.5957691216057308), reads=["rstd"], writes=["rstd"])
        S.add("pool", lambda e, bl0=bl0: e.tensor_tensor(k.yg[:, :, bl0:bl0 + BS], y1, y0, ALU.mult), reads=["rstd", "rstd"], writes=["A:yg"])

    stageA(0)
    for bi in range(len(blocks)):
        if bi + 1 < len(blocks):
            stageA(bi + 1)
        stageB(bi)
        stageC(bi)
    if mid is not None:
        mid()
    for (c0, l0, n) in tiles:
        for m in range(NCH):
            sl = k.wgl_cnt % 2
            k.wgl_cnt += 1
            W = k.wgl[sl]
            wh = ("A:wgl", sl)
            S.add("pool", lambda e, W=W, m=m: e.dma_start(out=W[:].rearrange("p a c j -> p (a c j)"), in_=k.wglu_d[j, m]), writes=[wh], dma=True)
            pa, pb = k.ps[m % 2], k.ps[2 + m % 2]
            pan, pbn = "ps%d" % (m % 2), "ps%d" % (2 + m % 2)
            for (a, pp, ppn) in ((0, pa, pan), (1, pb, pbn)):
                def mmg(e, W=W, a=a, pp=pp, l0=l0, n=n):
                    ins = None
                    for c in range(NCH):
                        ins = e.matmul(pp[:, :n], W[:, a, c, :], k.yg[:, c, l0:l0 + n], start=(c == 0), stop=(c == NCH - 1))
                    return ins
                S.add("pe", mmg, reads=[wh, "A:yg"], writes=[ppn])
            sgv = k.rstd[:]
            S.add("act", lambda e, pb=pb, n=n, sgv=sgv: e.activation(sgv[:, :n], pb[:, :n], AF.Sigmoid), reads=[pbn], writes=["rstd"])
            S.add("dve", lambda e, pa=pa, m=m, n=n, sgv=sgv: e.tensor_tensor(k.mixbuf[:, m, :n], pa[:, :n], sgv[:, :n], ALU.mult),
                  reads=[pan, "rstd"], writes=["mixbuf"])
        postnorm_residual(k, layer, 1, c0, n)


def s5_finish(k, layer):
    S = k.S
    j = k.s5j
    S.add("sp", lambda e: e.dma_start(out=k.os5p_d[j], in_=k.sprev[:]), reads=["A:sprev"], dma=True, final=True)
    S.add("sp", lambda e: e.dma_start(out=k.os5s_d[j], in_=k.osS[:]), reads=["A:h0"], dma=True, final=True)
```
